# Optimizing a Trainium2 kernel written in Bass

```python
import jax, jax.numpy as jnp
from jax import lax
import numpy as np

D_MODEL = 1024
BATCH = 8
SEQ = 4096
DEPTH = 1

CHUNK = 64
SUB = 16
N_SUB = CHUNK // SUB
HG_DK = 128
HG_HEADS = D_MODEL // HG_DK
HG_DV = D_MODEL // HG_HEADS
HG_WIDTH = HG_HEADS * HG_DK
HG_VWIDTH = HG_HEADS * HG_DV
CONV_WIDTH = D_MODEL
CONV_TAPS = 31
FFN_HIDDEN = -(-8 * D_MODEL // (3 * 256)) * 256
DEEPNORM_ALPHA = (2 * DEPTH) ** 0.25
DEEPNORM_BETA = (8 * DEPTH) ** -0.25
LN_EPS = 1e-5
IN_WIDTH = 2 * HG_WIDTH + 2 * HG_VWIDTH + 2 * CONV_WIDTH + 2 * D_MODEL
IN_SPLITS = [HG_WIDTH,
             2 * HG_WIDTH,
             2 * HG_WIDTH + HG_VWIDTH,
             2 * HG_WIDTH + 2 * HG_VWIDTH,
             2 * HG_WIDTH + 2 * HG_VWIDTH + CONV_WIDTH,
             2 * HG_WIDTH + 2 * HG_VWIDTH + 2 * CONV_WIDTH,
             2 * HG_WIDTH + 2 * HG_VWIDTH + 2 * CONV_WIDTH + D_MODEL]

kernel_name = "hgrn2_conformer_conv_gated_hybrid"


def layer_norm(x, g, b):
    xf = x.astype(jnp.float32)
    mu = jnp.mean(xf, axis=-1, keepdims=True)
    var = jnp.mean(jnp.square(xf - mu), axis=-1, keepdims=True)
    y = (xf - mu) * lax.rsqrt(var + LN_EPS) * g.astype(jnp.float32) + b.astype(jnp.float32)
    return y.astype(x.dtype)


def rms_norm(x, g):
    xf = x.astype(jnp.float32)
    y = xf * lax.rsqrt(jnp.mean(jnp.square(xf), axis=-1, keepdims=True) + LN_EPS)
    return y * g.astype(jnp.float32)


def _hgrn2_chunk_step(state, inp):
    q, k, v, g = inp
    bsz, h = q.shape[0], q.shape[1]
    b = jnp.cumsum(g, axis=2)
    o = jnp.einsum('bhck,bhkv->bhcv', q * jnp.exp(b), state)
    qs = q.reshape(bsz, h, N_SUB, SUB, HG_DK)
    ks = k.reshape(bsz, h, N_SUB, SUB, HG_DK)
    vs = v.reshape(bsz, h, N_SUB, SUB, HG_DV)
    bs = b.reshape(bsz, h, N_SUB, SUB, HG_DK)
    tri = jnp.tril(jnp.ones((SUB, SUB), dtype=bool))
    diff = bs[:, :, :, :, None, :] - bs[:, :, :, None, :, :]
    decay = jnp.exp(jnp.where(tri[:, :, None], diff, -jnp.inf))
    a_diag = jnp.einsum('bhntk,bhnsk,bhntsk->bhnts', qs, ks, decay)
    o_diag = jnp.einsum('bhnts,bhnsv->bhntv', a_diag, vs)
    ref = jnp.concatenate([jnp.zeros_like(bs[:, :, :1, -1]), bs[:, :, :-1, -1]], axis=2)
    q_x = qs * jnp.exp(bs - ref[:, :, :, None, :])
    earlier = jnp.arange(CHUNK)[None, :] < (jnp.arange(N_SUB) * SUB)[:, None]
    k_exp = ref[:, :, :, None, :] - b[:, :, None, :, :]
    k_x = k[:, :, None] * jnp.exp(jnp.where(earlier[:, :, None], k_exp, -jnp.inf))
    a_cross = jnp.einsum('bhntk,bhnsk->bhnts', q_x, k_x)
    o_cross = jnp.einsum('bhnts,bhsv->bhntv', a_cross, v)
    o = o + (o_diag + o_cross).reshape(bsz, h, CHUNK, HG_DV)
    b_end = b[:, :, -1]
    k_end = k * jnp.exp(b_end[:, :, None, :] - b)
    state = jnp.exp(b_end)[..., None] * state + jnp.einsum('bhck,bhcv->bhkv', k_end, v)
    return state, o


def hgrn2_recurrence(q, k, v, g):
    bsz, seq = q.shape[0], q.shape[1]
    n = seq // CHUNK

    def to_chunks(t):
        return t.reshape(bsz, n, CHUNK, HG_HEADS, t.shape[-1]).transpose(1, 0, 3, 2, 4)

    state0 = jnp.zeros((bsz, HG_HEADS, HG_DK, HG_DV), jnp.float32)
    _, o = lax.scan(_hgrn2_chunk_step, state0, (to_chunks(q), to_chunks(k), to_chunks(v), to_chunks(g)))
    return o.transpose(1, 0, 3, 2, 4).reshape(bsz, seq, HG_HEADS, HG_DV)


def causal_depthwise_conv(u, w, bias):
    y = lax.conv_general_dilated(u, w[:, None, :].astype(u.dtype), window_strides=(1,),
                                 padding=[(CONV_TAPS - 1, 0)],
                                 dimension_numbers=('NWC', 'WIO', 'NWC'),
                                 feature_group_count=u.shape[-1])
    return y + bias.astype(u.dtype)


def setup_inputs(seed: int = 0) -> dict:
    key = jax.random.key(seed)
    ks = jax.random.split(key, 20)
    f32 = jnp.float32

    def nrm(k, shape, scale):
        return jax.random.normal(k, shape, f32) * scale

    return {
        "x": jax.random.normal(ks[0], (BATCH, SEQ, D_MODEL), f32),
        "w_in": nrm(ks[1], (DEPTH, D_MODEL, IN_WIDTH), D_MODEL ** -0.5),
        "lb_param": nrm(ks[2], (DEPTH + 1, HG_WIDTH), 0.1),
        "hg_norm_g": 1.0 + nrm(ks[3], (DEPTH, HG_DV), 0.01),
        "w_hg_out": nrm(ks[4], (DEPTH, HG_VWIDTH, D_MODEL), HG_VWIDTH ** -0.5),
        "conv_w": nrm(ks[5], (DEPTH, CONV_TAPS, CONV_WIDTH), CONV_TAPS ** -0.5),
        "conv_b": nrm(ks[6], (DEPTH, CONV_WIDTH), 0.01),
        "conv_ln_g": 1.0 + nrm(ks[7], (DEPTH, CONV_WIDTH), 0.01),
        "conv_ln_b": nrm(ks[8], (DEPTH, CONV_WIDTH), 0.01),
        "w_conv_out": nrm(ks[9], (DEPTH, CONV_WIDTH, D_MODEL), CONV_WIDTH ** -0.5),
        "w_out": nrm(ks[10], (DEPTH, D_MODEL, D_MODEL), D_MODEL ** -0.5 * DEEPNORM_BETA),
        "ln1_g": 1.0 + nrm(ks[11], (DEPTH, D_MODEL), 0.01),
        "ln1_b": nrm(ks[12], (DEPTH, D_MODEL), 0.01),
        "w_ffn_in": nrm(ks[13], (DEPTH, D_MODEL, 2 * FFN_HIDDEN), D_MODEL ** -0.5),
        "w_ffn_out": nrm(ks[14], (DEPTH, FFN_HIDDEN, D_MODEL), FFN_HIDDEN ** -0.5 * DEEPNORM_BETA),
        "ln2_g": 1.0 + nrm(ks[15], (DEPTH, D_MODEL), 0.01),
        "ln2_b": nrm(ks[16], (DEPTH, D_MODEL), 0.01),
    }


def reference(x, w_in, lb_param, hg_norm_g, w_hg_out, conv_w, conv_b, conv_ln_g, conv_ln_b,
              w_conv_out, w_out, ln1_g, ln1_b, w_ffn_in, w_ffn_out, ln2_g, ln2_b):
    dtype = x.dtype
    bsz, seq = x.shape[0], x.shape[1]
    lb_all = jnp.cumsum(jax.nn.softmax(lb_param.astype(jnp.float32), axis=0), axis=0)
    h = x
    for l in range(DEPTH):
        proj = h @ w_in[l]
        q_in, f_logit, i_in, o_gate, glu_v, glu_g, gate_a, gate_b = jnp.split(proj, IN_SPLITS, axis=-1)

        lb = lb_all[l]
        f = lb + (1.0 - lb) * jax.nn.sigmoid(f_logit.astype(jnp.float32))
        k = (1.0 - f).reshape(bsz, seq, HG_HEADS, HG_DK)
        g = jnp.log(f).reshape(bsz, seq, HG_HEADS, HG_DK)
        q = (jax.nn.silu(q_in.astype(jnp.float32)) * HG_DK ** -0.5).reshape(bsz, seq, HG_HEADS, HG_DK)
        v = i_in.astype(jnp.float32).reshape(bsz, seq, HG_HEADS, HG_DV)
        o = hgrn2_recurrence(q, k, v, g)
        o = rms_norm(o, hg_norm_g[l]).reshape(bsz, seq, HG_VWIDTH)
        o = (o * jax.nn.silu(o_gate.astype(jnp.float32))).astype(dtype)
        y_a = o @ w_hg_out[l]

        u = glu_v * jax.nn.sigmoid(glu_g)
        u = causal_depthwise_conv(u, conv_w[l], conv_b[l])
        u = jax.nn.silu(layer_norm(u, conv_ln_g[l], conv_ln_b[l]))
        y_b = u @ w_conv_out[l]

        mixed = jax.nn.sigmoid(gate_a) * y_a + jax.nn.sigmoid(gate_b) * y_b
        h = layer_norm(DEEPNORM_ALPHA * h + mixed @ w_out[l], ln1_g[l], ln1_b[l])

        gu = h @ w_ffn_in[l]
        ffn_g, ffn_u = jnp.split(gu, [FFN_HIDDEN], axis=-1)
        ffn = (jax.nn.silu(ffn_g) * ffn_u) @ w_ffn_out[l]
        h = layer_norm(DEEPNORM_ALPHA * h + ffn, ln2_g[l], ln2_b[l])
    return h
```

```python
import numpy as np
from contextlib import ExitStack
import concourse.bass as bass
import concourse.mybir as mybir
from concourse.bass_utils import run_bass_kernel_spmd

F32 = mybir.dt.float32
BF16 = mybir.dt.bfloat16
AF = mybir.ActivationFunctionType
ALU = mybir.AluOpType

D = 1024
NH = 8
FFN = 2816
TAPS = 31
IN_W = 8192
ALPHA = 2.0 ** 0.25
LN_EPS = 1e-5
RMS_EPS = LN_EPS * 128.0
N_CORES = 8
SEQ = 4096


class Prog:
    ENGS = ("pe", "act", "dve", "pool", "sp")
    SAME_SYNC = ("act", "dve", "pool")

    def __init__(self):
        self.ops = []
        self.lastw = {}
        self.readers = {}
        self.dsem_count = {}
        self.epoch = 0

    def add(self, eng, fn, reads=(), writes=(), dsem=None, dur=None, lat=0.0, tbl=None):
        i = len(self.ops)
        deps = set()
        for r in reads:
            w = self.lastw.get(r)
            if w is not None:
                deps.add(w)
        for w_ in writes:
            lw = self.lastw.get(w_)
            if lw is not None:
                deps.add(lw)
            for rd in self.readers.get(w_, ()):
                deps.add(rd)
        for r in reads:
            self.readers.setdefault(r, []).append(i)
        for w_ in writes:
            self.lastw[w_] = i
            self.readers[w_] = []
        op = dict(eng=eng, fn=fn, deps=deps, dsem=dsem, sig=False, dval=None, ep=self.epoch,
                  dur=(0.3 if dur is None else dur), lat=lat, tbl=tbl)
        if dsem is not None:
            self.dsem_count[dsem] = self.dsem_count.get(dsem, 0) + 16
            op["dval"] = self.dsem_count[dsem]
        self.ops.append(op)
        return i

    def list_schedule(self, mode="blevel"):
        ops = self.ops
        n = len(ops)
        body = [i for i in range(n) if ops[i]["fn"] is not None]
        tailops = [i for i in range(n) if ops[i]["fn"] is None]
        succ = [[] for _ in range(n)]
        indeg = [0] * n
        for i in body:
            for d in ops[i]["deps"]:
                succ[d].append(i)
                indeg[i] += 1
        last_d = {}
        for i in body:
            k = ops[i]["dsem"]
            if k is not None:
                if k in last_d and last_d[k] not in ops[i]["deps"]:
                    succ[last_d[k]].append(i)
                    indeg[i] += 1
                last_d[k] = i
        blev = [0.0] * n
        for i in reversed(body):
            m = 0.0
            for j in succ[i]:
                if blev[j] > m:
                    m = blev[j]
            blev[i] = m + ops[i]["dur"] + ops[i]["lat"] + (DMA_BOOST if ops[i]["dsem"] is not None else 0.0)
        finish = [0.0] * n
        ready_t = [0.0] * n
        eng_free = {e: 0.0 for e in self.ENGS}
        act_tbl = [None]
        ready = {e: [] for e in self.ENGS}
        for i in body:
            if indeg[i] == 0:
                ready[ops[i]["eng"]].append(i)
        order = []
        left = len(body)
        while left:
            best = None
            for e in self.ENGS:
                lst = ready[e]
                if not lst:
                    continue
                t_e = max(eng_free[e], min(ready_t[i] for i in lst))
                if best is None or t_e < best[0]:
                    best = (t_e, e)
            t_e, e = best
            lst = ready[e]
            cands = [i for i in lst if ready_t[i] <= t_e + 1e-9]
            if mode == "blevel":
                if e == "act":
                    cur = act_tbl[0]
                    i = max(cands, key=lambda q: (blev[q] - (1.3 if (ops[q]["tbl"] not in (None, cur)) else 0.0), -q))
                else:
                    i = max(cands, key=lambda q: (blev[q], -q))
            else:
                i = min(cands)
            lst.remove(i)
            op = ops[i]
            dur = op["dur"]
            if e == "act" and op["tbl"] is not None and op["tbl"] != act_tbl[0]:
                dur += 1.3
                act_tbl[0] = op["tbl"]
            eng_free[e] = t_e + dur
            finish[i] = t_e + dur + op["lat"]
            order.append(i)
            left -= 1
            for j in succ[i]:
                indeg[j] -= 1
                lat = 0.0 if ops[j]["eng"] == e and op["dsem"] is None else XLAT
                if finish[i] + lat > ready_t[j]:
                    ready_t[j] = finish[i] + lat
                if indeg[j] == 0:
                    ready[ops[j]["eng"]].append(j)
        order += tailops
        remap = {old: new for new, old in enumerate(order)}
        newops = [ops[i] for i in order]
        for op in newops:
            op["deps"] = {remap[d] for d in op["deps"]}
        self.ops = newops
        self.est_makespan = max(finish) if finish else 0.0

    def finalize(self):
        ops = self.ops
        for op in ops:
            keep = set()
            for d in op["deps"]:
                dop = ops[d]
                if dop["dsem"] is not None:
                    keep.add(d)
                elif dop["eng"] != op["eng"] or op["eng"] in self.SAME_SYNC:
                    dop["sig"] = True
                    keep.add(d)
            op["deps"] = keep
        cnt = {}
        for op in ops:
            if op["dsem"] is None and op["sig"]:
                k = (op["eng"], op["ep"])
                cnt[k] = cnt.get(k, 0) + 1
                op["sval"] = cnt[k]
        self.max_sval = max(cnt.values()) if cnt else 0
        waited = {e: {} for e in self.ENGS}
        for op in ops:
            need = {}
            for d in op["deps"]:
                dop = ops[d]
                if dop["dsem"] is not None:
                    key, val = ("d", dop["dsem"]), dop["dval"]
                else:
                    key, val = ("e", (dop["eng"], dop["ep"])), dop["sval"]
                if val > need.get(key, 0):
                    need[key] = val
            w = waited[op["eng"]]
            waits = []
            for key, val in need.items():
                if val > w.get(key, 0):
                    w[key] = val
                    waits.append((key, val))
            op["waits"] = waits

    def emit(self, block, esems, dsems):
        handles = {"pe": block.tensor, "act": block.scalar, "dve": block.vector,
                   "pool": block.gpsimd, "sp": block.sync}
        for ename in self.ENGS:
            myops = [op for op in self.ops if op["eng"] == ename]
            if not myops:
                continue

            def body(eng, myops=myops, ename=ename):
                for op in myops:
                    for (kind, k), val in op["waits"]:
                        eng.wait_ge(dsems[k] if kind == "d" else esems[k], val)
                    if op["fn"] is None:
                        continue
                    inst = op["fn"](eng)
                    if op["dsem"] is not None:
                        inst.then_inc(dsems[op["dsem"]], 16)
                    elif op["sig"]:
                        inst.then_inc(esems[(ename, op["ep"])], 1)

            handles[ename](body)


LIST_SCHED = True
XLAT = 0.28
DMA_BOOST = 0.0
SCHED_MODE = "blevel"


def build_nc(S, TS=512, NSLOT=8, dbg=99):
    NST = S // TS
    NT = TS // 512
    NB = TS // 128
    NCH = TS // 64
    assert S % TS == 0 and TS % 512 == 0

    nc = bass.Bass("TRN2", target_bir_lowering=False)

    def din(name, shape):
        return nc.dram_tensor(name, shape, F32, kind="ExternalInput").ap()

    x_d = din("x", [S, D])
    w_in_d = din("w_in", [D, IN_W])
    w_hg_d = din("w_hg_out", [D, D])
    w_cv_d = din("w_conv_out", [D, D])
    w_out_d = din("w_out", [D, D])
    w_f1_d = din("w_ffn_in", [D, 2 * FFN])
    w_f2_d = din("w_ffn_out", [FFN, D])
    pp_d = din("pp", [128, PP_N])
    lnp_d = din("lnp", [128, 4, D])
    cst_d = din("cst", [128, CST_N(TS)])
    out_d = nc.dram_tensor("out", [S, D], F32, kind="ExternalOutput").ap()

    P = Prog()
    es = ExitStack()
    with es:
        def sb(name, shape, dt):
            return es.enter_context(nc.sbuf_tensor("sb_" + name, shape, dt))

        R = sb("R", [128, NB, D], F32)
        xTs = [sb(f"xT{i}", [128, 8, TS], BF16) for i in range(2)]
        on = sb("on", [128, 8, TS], BF16)
        shared = sb("shared", [128, 24 * TS], BF16)
        slots = [sb(f"slot{i}", [128, 8, 256], BF16) for i in range(NSLOT)]
        lnt = sb("lnt", [128, 4, D], F32)
        cs = sb("cs", [128, CST_N(TS)], F32)
        pp = sb("pp", [128, PP_N], F32)
        Fall = sb("Fall", [128, 8, TS], F32)
        Fb = [[Fall[:, p * 4 + i, :] for i in range(4)] for p in range(2)]
        Qpp = [sb(f"Qpp{p}", [128, TS], BF16) for p in range(2)]
        Kt = [sb(f"Kt{p}", [128, TS], BF16) for p in range(2)]
        ogs = [sb(f"ogs{p}", [128, TS], BF16) for p in range(2)]
        v_tm = [sb(f"v_tm{p}", [128, NB, 128], BF16) for p in range(2)]
        K_tm = [[sb(f"K_tm{p}_{i}", [128, NB, 128], BF16) for i in range(2)] for p in range(2)]
        KVs = [sb(f"KVs{p}", [128, NCH, 128], F32) for p in range(2)]
        Tb = [sb(f"Tb{p}", [128, NCH + 1, 128], BF16) for p in range(2)]
        ebuf = [sb(f"ebuf{p}", [128, NCH], F32) for p in range(2)]
        Am = [sb(f"Am{i}", [128, 4, 128], BF16) for i in range(2)]
        Tprev = sb("Tprev", [128, NH, 128], BF16)
        halo = sb("halo", [128, 8, 32], BF16)
        ubuf = [sb(f"ubuf{p}", [128, 32 + TS], BF16) for p in range(2)]
        Dg = [sb(f"Dg{p}", [128, TAPS, 128], BF16) for p in range(2)]
        scr = [sb(f"scr{i}", [128, 512], F32) for i in range(8)]
        scb = [sb(f"scb{i}", [128, 512], BF16) for i in range(4)]
        identB = sb("identB", [128, 128], BF16)
        ones128 = sb("ones128", [128, 128], BF16)
        ones1024 = sb("ones1024", [128, 128], BF16)
        lbt = sb("lbt", [128, 4, NH], F32)
        cwh = sb("cwh", [128, 8 * TAPS], F32)
        st6 = sb("st6", [128, 12], F32)
        mv = sb("mv", [128, 4], F32)
        cmean = sb("cmean", [128, 512], F32)
        crstd = sb("crstd", [128, 512], F32)
        psb = [es.enter_context(nc.psum_tensor(f"psb{i}", [128, 512], F32)) for i in range(8)]

        cpre = shared[:, 0:16 * TS].bitcast(F32).rearrange("p (j t) -> p j t", j=8)
        un = shared[:, 16 * TS:24 * TS].rearrange("p (j t) -> p j t", j=8)
        mixed = shared[:, 0:8 * TS].rearrange("p (j t) -> p j t", j=8)
        actb = shared[:, 0:22 * TS].rearrange("p (j t) -> p j t", j=22)

        def k_cpre(j, tt):
            b0 = (j * TS + tt * 512) * 4
            return [("sh", b0 // 1024), ("sh", b0 // 1024 + 1)]

        def k_bf(base_seg_elems, j, tt):
            b0 = (base_seg_elems + j * TS + tt * 512) * 2
            return [("sh", b0 // 1024)]

        def k_un(j, tt):
            return k_bf(16 * TS, j, tt)

        def k_mixed(j, tt):
            return k_bf(0, j, tt)

        def k_act(j, tt):
            return k_bf(0, j, tt)

        identF = cs[:, 0:128]
        mask2 = cs[:, 128:256]
        c_eps_rms = cs[:, 256:257]
        c_eps_ln = cs[:, 257:258]
        c_eps_ln4 = cs[:, 258:259]
        scanmask = cs[:, 320:320 + TS]
        lbp = pp[:, 0:16].rearrange("p (a h) -> p a h", a=2)
        gn = pp[:, 16:17]
        cw = pp[:, 32:32 + 8 * TAPS].rearrange("p (j t) -> p j t", j=8)
        cb = pp[:, 288:296]
        cg = pp[:, 296:304]
        cbb = pp[:, 304:312]

        esems = {(e, ep): es.enter_context(nc.semaphore(f"s_{e}_{ep}"))
                 for e in ("pe", "act", "dve", "pool") for ep in range(NST + 1)}
        dnames = ([f"ws{i}" for i in range(NSLOT)] + [f"r{i}" for i in range(NB)]
                  + [f"o{i}" for i in range(NB)] + [f"xs{i}" for i in range(NB)] + ["c0", "c1", "c2"])
        dsems = {d: es.enter_context(nc.semaphore("d_" + d)) for d in dnames}
        print('sbuf bytes remaining', nc.sbuf_bytes_remaining)
        block = es.enter_context(nc.Block())

        def fsz(ap):
            n = 1
            for d in ap.shape[1:]:
                n *= d
            return n

        def mm(out, lhsT, rhs, start, stop, reads, writes):
            P.add("pe", lambda e: e.matmul(out, lhsT=lhsT, rhs=rhs, start=start, stop=stop), reads, writes,
                  dur=max(64, fsz(rhs)) / 2300.0 + 0.01)

        def tr(out, in_, ident, reads, writes):
            P.add("pe", lambda e: e.transpose(out, in_, ident), reads, writes, dur=0.09)

        def act(out, in_, func, reads, writes, scale=None, bias=None):
            kw = {}
            if scale is not None:
                kw["scale"] = scale
            if bias is not None:
                kw["bias"] = bias
            tbl = {AF.Tanh: "A", AF.Silu: "A", AF.Ln: "B", AF.Sigmoid: "C"}.get(func)
            P.add("act", lambda e: e.activation(out=out, in_=in_, func=func, **kw), reads, writes,
                  dur=0.17 + fsz(out) / 1200.0 + (0.19 if (scale is not None and not isinstance(scale, float)) or
                                                     (bias is not None and not isinstance(bias, float)) else 0.0),
                  tbl=tbl)

        def tt_(out, in0, in1, op, reads, writes, eng="dve"):
            P.add(eng, lambda e: e.tensor_tensor(out=out, in0=in0, in1=in1, op=op), reads, writes,
                  dur=0.15 + fsz(out) / 960.0)

        def ts_(out, in0, s1, s2, op0, op1, reads, writes, eng="dve"):
            P.add(eng, lambda e: e.tensor_scalar(out=out, in0=in0, scalar1=s1, scalar2=s2, op0=op0, op1=op1),
                  reads, writes, dur=0.15 + fsz(out) / 960.0)

        def stt(out, in0, scalar, in1, op0, op1, reads, writes):
            P.add("dve", lambda e: e.scalar_tensor_tensor(out=out, in0=in0, scalar=scalar, in1=in1,
                                                          op0=op0, op1=op1), reads, writes,
                  dur=0.15 + fsz(out) / 960.0)

        def cp(out, in_, reads, writes, eng="dve"):
            if eng == "act":
                P.add("act", lambda e: e.activation(out=out, in_=in_, func=AF.Copy), reads, writes,
                      dur=0.17 + fsz(out) / 1200.0)
            else:
                P.add(eng, lambda e: e.tensor_copy(out=out, in_=in_), reads, writes, dur=0.15 + fsz(out) / 960.0)

        def dma(eng, out, in_, reads, writes, dsem):
            nbytes = 128 * fsz(out) * 4
            P.add(eng, lambda e: e.dma_start(out=out, in_=in_), reads, writes, dsem=dsem,
                  dur=0.1, lat=2.0 + nbytes / 150000.0)

        st = {"ps": 0, "scr": 0, "scb": 0, "slot": 0, "alt": 0, "am": 0}

        ps_free = list(range(6))

        def newps():
            assert ps_free, "out of PSUM banks"
            i = ps_free.pop(0)
            return psb[i], ("ps", i)

        def relps(*keys):
            for k in keys:
                assert k[1] not in ps_free
                ps_free.append(k[1])

        def newscr():
            i = st["scr"]; st["scr"] = (i + 1) % 8
            return scr[i], ("scr", i)

        def newscb():
            i = st["scb"]; st["scb"] = (i + 1) % 4
            return scb[i], ("scb", i)

        def wblock(wd, k0, KC, n0):
            i = st["slot"]; st["slot"] = (i + 1) % NSLOT
            key = ("slot", i)
            src = wd[k0:k0 + KC * 128, n0:n0 + 256].rearrange("(kc p) n -> p kc n", p=128)
            dma("pool", slots[i][:, 0:KC, :], src, [], [key], f"ws{i}")
            return slots[i], key

        def alt_eng():
            st["alt"] ^= 1
            return "act" if st["alt"] else "dve"

        dma("sp", cs[:], cst_d, [], ["cs"], "c0")
        dma("sp", pp[:], pp_d, [], ["pp"], "c1")
        dma("sp", lnt[:], lnp_d, [], ["lnt"], "c2")
        cp(identB[:], identF, ["cs"], ["identB"])
        P.add("dve", lambda e: e.memset(scr[0][:], 0.0), [], [("scr", 0)])
        P.add("dve", lambda e: e.memset(scr[1][:], 1.0 / 128.0), [], [("scr", 1)])
        P.add("dve", lambda e: e.memset(scr[2][:], 1.0 / 1024.0), [], [("scr", 2)])
        cp(ones128[:], scr[1][:, 0:128], [("scr", 1)], ["ones128"])
        cp(ones1024[:], scr[2][:, 0:128], [("scr", 2)], ["ones1024"])
        Tpf = Tprev[:].rearrange("p h d -> p (h d)")
        cp(Tpf[:, 0:512], scr[0][:], [("scr", 0)], [("Tprev", h) for h in range(4)])
        cp(Tpf[:, 512:1024], scr[0][:], [("scr", 0)], [("Tprev", h) for h in range(4, 8)])
        cp(halo[:].rearrange("p j t -> p (j t)"), scr[0][:, 0:256], [("scr", 0)], [("halo", j) for j in range(8)])
        for p_ in range(2):
            for i in range(2):
                for tb in range(0, NB, 4):
                    cp(K_tm[p_][i][:, tb:tb + 4, :].rearrange("p a b -> p (a b)"), scr[0][:], [("scr", 0)],
                       [("Ktm", p_, tb // 4, 0), ("Ktm", p_, tb // 4, 1)])
        tt_(lbt[:, 3, :], lbp[:, 0, :], lbp[:, 1, :], ALU.subtract, ["pp"], ["lbt3"])
        act(lbt[:, 0, :], lbt[:, 3, :], AF.Sigmoid, ["lbt3"], ["lbt0"])
        ts_(lbt[:, 1, :], lbt[:, 0, :], -0.5, 0.5, ALU.mult, ALU.add, ["lbt0"], ["lbt1"])
        ts_(lbt[:, 2, :], lbt[:, 0, :], 0.5, -0.5, ALU.mult, ALU.add, ["lbt0"], ["lbt2"])
        ts_(lbt[:, 3, :], lbt[:, 0, :], 0.5, 0.5, ALU.mult, ALU.add, ["lbt0"], ["lbt3b"])
        ts_(cwh[:], pp[:, 32:32 + 8 * TAPS], 0.5, None, ALU.mult, ALU.bypass, ["pp"], ["cwh"])
        LB = ["lbt3b", "lbt1", "lbt2"]

        def transpose_blk(src, src_keys, tb, xb):
            for g in range(2):
                ps, pk = newps()
                for kk in range(4):
                    kc = g * 4 + kk
                    tr(ps[:, kk * 128:(kk + 1) * 128], src[:, kc * 128:(kc + 1) * 128], identF,
                       list(src_keys) + ["cs"], [pk])
                cp(xTs[xb][:, g * 4:(g + 1) * 4, tb * 128:(tb + 1) * 128],
                   ps[:].rearrange("p (a b) -> p a b", a=4), [pk], [("xT", xb, tb, g)], eng=alt_eng())
                relps(pk)

        def xT_keys(tt):
            return [("xT", st["xb"], tb, g) for tb in range(tt * 4, tt * 4 + 4) for g in range(2)]

        def xstage(tb):
            ap = Fall[:, 2 * tb:2 * tb + 2, :].rearrange("p a t -> p (a t)")
            keys = [(f"F{idx % 4}", idx // 4, 0) for idx in (2 * tb, 2 * tb + 1)]
            return ap, keys

        def prefetch_x_load(sti_):
            for tb in range(NB):
                ap, keys = xstage(tb)
                dma("sp", ap, x_d[sti_ * TS + tb * 128:sti_ * TS + (tb + 1) * 128, :], [], keys, f"xs{tb}")

        def prefetch_x_transpose(sti_):
            for tb in range(NB):
                ap, keys = xstage(tb)
                transpose_blk(ap, keys, tb, sti_ % 2)

        def layer_norm_R(tb, gi, eps_ap=None):
            eps_ap = c_eps_ln if eps_ap is None else eps_ap
            rk = ("R", tb)
            P.add("dve", lambda e: e.bn_stats(out=st6[:, 0:6], in_=R[:, tb, 0:512]), [rk], ["st6a"])
            P.add("dve", lambda e: e.bn_stats(out=st6[:, 6:12], in_=R[:, tb, 512:1024]), [rk], ["st6b"])
            P.add("dve", lambda e: e.bn_aggr(out=mv[:, 0:2], in_=st6[:]), ["st6a", "st6b"], ["mv01"])
            act(mv[:, 2:3], mv[:, 1:2], AF.Ln, ["mv01", "cs"], ["mv2"], bias=eps_ap)
            act(mv[:, 3:4], mv[:, 2:3], AF.Exp, ["mv2"], ["mv3"], scale=-0.5)
            ts_(R[:, tb, :], R[:, tb, :], mv[:, 0:1], mv[:, 3:4], ALU.subtract, ALU.mult,
                [rk, "mv01", "mv3"], [rk])
            tt_(R[:, tb, :], R[:, tb, :], lnt[:, gi, :], ALU.mult, [rk, "lnt"], [rk])
            tt_(R[:, tb, :], R[:, tb, :], lnt[:, gi + 1, :], ALU.add, [rk, "lnt"], [rk])

        for sti in range(NST):
            t0 = sti * TS
            P.epoch = sti + 1
            st["xb"] = sti % 2
            xT = xTs[sti % 2]
            if sti == 0:
                prefetch_x_load(0)
                prefetch_x_transpose(0)
            for tb in range(NB):
                dma("sp", R[:, tb, :], x_d[t0 + tb * 128:t0 + (tb + 1) * 128, :], [], [("R", tb)], f"r{tb}")

            wst = {}

            def head_front(h):
                par = h % 2
                if h % 2 == 0:
                    hp = h // 2
                    for nm, sec in (("q", 0), ("f", 1), ("i", 2), ("o", 3)):
                        wst[nm] = wblock(w_in_d, 0, 8, sec * 1024 + hp * 256)
                (wq, kq), (wf, kf), (wi, ki), (wo, ko) = wst["q"], wst["f"], wst["i"], wst["o"]
                hc = (h % 2) * 128
                F0, F1, F2, F3 = Fb[par]
                allF = lambda n: [(n, par, tt) for tt in range(NT)]
                for tt in range(NT):
                    tsl = slice(tt * 512, (tt + 1) * 512)
                    for (wblk, wk, dst, dk_, fn) in ((wq, kq, F0, ("F0", par, tt), AF.Silu),
                                                     (wf, kf, F1, ("F1", par, tt), AF.Tanh),
                                                     (wo, ko, ogs[par], ("ogs", par, tt), AF.Silu)):
                        ps, pk = newps()
                        for kc in range(8):
                            mm(ps[:], wblk[:, kc, hc:hc + 128], xT[:, kc, tsl], kc == 0, kc == 7,
                               [wk] + xT_keys(tt), [pk])
                        act(dst[:, tsl], ps[:], fn, [pk], [dk_], scale=(0.5 if fn == AF.Tanh else None))
                        relps(pk)
                        yield
                for tg in range(NB // 4):
                    ps, pk = newps()
                    for tl in range(4):
                        tb = tg * 4 + tl
                        for kc in range(8):
                            mm(ps[:, tl * 128:(tl + 1) * 128], xT[:, kc, tb * 128:(tb + 1) * 128],
                               wi[:, kc, hc:hc + 128], kc == 0, kc == 7,
                               [ki, ("xT", st["xb"], tb, 0), ("xT", st["xb"], tb, 1)], [pk])
                    cp(v_tm[par][:, tg * 4:(tg + 1) * 4, :], ps[:].rearrange("p (a b) -> p a b", a=4),
                       [pk], [("vtm", par, tg)])
                    relps(pk)
                    yield
                act(F2[:], F1[:], AF.Ln, allF("F1") + LB, allF("F2"),
                    scale=lbt[:, 1, h:h + 1], bias=lbt[:, 3, h:h + 1])
                ts_(F3[:], F1[:], lbt[:, 2, h:h + 1], lbt[:, 1, h:h + 1], ALU.mult, ALU.add,
                    allF("F1") + LB, allF("F3"))
                yield
                P.add("dve", lambda e, F1=F1, F2=F2: e.tensor_tensor_scan(
                    out=F1[:], data0=scanmask, data1=F2[:], initial=0.0, op0=ALU.mult, op1=ALU.add),
                    allF("F2") + ["cs"], allF("F1"), dur=0.1 + 2 * TS / 960.0)
                yield
                act(F2[:], F1[:], AF.Exp, allF("F1"), allF("F2"))
                yield
                cp(ebuf[par][:], F2[:].rearrange("p (c t) -> p c t", t=64)[:, :, 63], allF("F2"), [("ebuf", par)])
                tt_(Qpp[par][:], F0[:], F2[:], ALU.mult, allF("F0") + allF("F2"), [("Qpp", par)])
                yield
                act(F0[:], F1[:], AF.Exp, allF("F1"), allF("F0"), scale=-1.0)
                yield
                tt_(Kt[par][:], F3[:], F0[:], ALU.mult, allF("F3") + allF("F0"), [("Kt", par)])
                yield

            def head_back(h):
                par = h % 2
                Ktp, Qp, vt, Tbp, eb, KV = Kt[par], Qpp[par], v_tm[par], Tb[par], ebuf[par], KVs[par]
                for tg in range(NB // 4):
                    ps, pk = newps()
                    psv = ps[:].bitcast(BF16)
                    for tl in range(4):
                        tb = tg * 4 + tl
                        tr(psv[:, tl * 128:(tl + 1) * 128], Ktp[:, tb * 128:(tb + 1) * 128], identB[:],
                           [("Kt", par), "identB"], [pk])
                    for hf in range(2):
                        cp(K_tm[par][hf][hf * 64:hf * 64 + 64, tg * 4:(tg + 1) * 4, :],
                           psv[hf * 64:hf * 64 + 64, 0:512].rearrange("p (a b) -> p a b", a=4),
                           [pk], [("Ktm", par, tg, hf)], eng=("act" if hf == 0 else "dve"))
                    relps(pk)
                    yield
                for cg_ in range(NCH // 4):
                    ps, pk = newps()
                    for cl in range(4):
                        c = cg_ * 4 + cl
                        tb, hf = c // 2, c % 2
                        mm(ps[:, cl * 128:(cl + 1) * 128], K_tm[par][hf][:, tb, :], vt[:, tb, :], True, True,
                           [("Ktm", par, tb // 4, hf), ("vtm", par, tb // 4)], [pk])
                    tt_(KV[:, cg_ * 4:(cg_ + 1) * 4, :], ps[:].rearrange("p (a b) -> p a b", a=4),
                        eb[:, cg_ * 4:(cg_ + 1) * 4].unsqueeze(2).broadcast_to([128, 4, 128]), ALU.mult,
                        [pk, ("ebuf", par)], [("KVs", par, cg_)])
                    relps(pk)
                    yield
                cp(Tbp[:, 0, :], Tprev[:, h, :], [("Tprev", h)], [("Tb", par, 0)])
                for c in range(NCH):
                    stt(Tbp[:, c + 1, :], Tbp[:, c, :], eb[:, c:c + 1], KV[:, c, :], ALU.mult, ALU.add,
                        [("Tb", par, c), ("ebuf", par), ("KVs", par, c // 4)], [("Tb", par, c + 1)])
                    if c % 2 == 1:
                        yield
                cp(Tprev[:, h, :], Tbp[:, NCH, :], [("Tb", par, NCH)], [("Tprev", h)], eng="act")
                yield
                yield
                for tt in range(NT):
                    tsl = slice(tt * 512, (tt + 1) * 512)
                    pA, pAk = newps()
                    for tbl in range(4):
                        tb = tt * 4 + tbl
                        mm(pA[:, tbl * 128:(tbl + 1) * 128], Ktp[:, tb * 128:(tb + 1) * 128],
                           Qp[:, tb * 128:(tb + 1) * 128], True, True, [("Kt", par), ("Qpp", par)], [pAk])
                    st["am"] ^= 1
                    am = Am[st["am"]]
                    amk = ("Am", st["am"])
                    tt_(am[:], pA[:].rearrange("p (a b) -> p a b", a=4),
                        mask2.unsqueeze(1).broadcast_to([128, 4, 128]), ALU.mult, [pAk, "cs"], [amk])
                    relps(pAk)
                    yield
                    pO, pOk = newps()
                    for tbl in range(4):
                        tb = tt * 4 + tbl
                        mm(pO[:, tbl * 128:(tbl + 1) * 128], vt[:, tb, :], am[:, tbl, :], True, False,
                           [("vtm", par, tb // 4), amk], [pOk])
                        for hf in range(2):
                            c = 2 * tb + hf
                            mm(pO[:, tbl * 128 + hf * 64:tbl * 128 + hf * 64 + 64], Tbp[:, c, :],
                               Qp[:, c * 64:(c + 1) * 64], False, hf == 1, [("Tb", par, c), ("Qpp", par)], [pOk])
                    osq, osk = newscb()
                    act(osq[:], pO[:], AF.Square, [pOk], [osk])
                    yield
                    pM, pMk = newps()
                    mm(pM[:], ones128[:], osq[:], True, True, ["ones128", osk], [pMk])
                    lnv, lnk = newscr()
                    act(lnv[:], pM[:], AF.Ln, [pMk, "cs"], [lnk], bias=c_eps_rms)
                    relps(pMk)
                    rstd, rsk = newscr()
                    act(rstd[:], lnv[:], AF.Exp, [lnk], [rsk], scale=-0.5)
                    t1, t1k = newscr()
                    stt(t1[:], pO[:], gn, rstd[:], ALU.mult, ALU.mult, [pOk, "pp", rsk], [t1k])
                    relps(pOk)
                    tt_(on[:, h, tsl], t1[:], ogs[par][:, tsl], ALU.mult, [t1k, ("ogs", par, tt)], [("on", h, tt)])
                    yield

            def conv_front(j):
                par = j % 2
                if j % 2 == 0:
                    wst["gv"] = wblock(w_in_d, 0, 8, 4096 + (j // 2) * 256)
                    wst["gg"] = wblock(w_in_d, 0, 8, 5120 + (j // 2) * 256)
                (wgv, kgv), (wgg, kgg) = wst["gv"], wst["gg"]
                jc = (j % 2) * 128
                ub = ubuf[par]
                cp(ub[:, 0:32], halo[:, j, :], [("halo", j)], [("ubuf_h", par)], eng="act")
                for tt in range(NT):
                    psv_, pvk = newps()
                    psg, pgk = newps()
                    tsl = slice(tt * 512, (tt + 1) * 512)
                    for kc in range(8):
                        mm(psv_[:], wgv[:, kc, jc:jc + 128], xT[:, kc, tsl], kc == 0, kc == 7,
                           [kgv] + xT_keys(tt), [pvk])
                        if kc % 2 == 1:
                            yield
                    for kc in range(8):
                        mm(psg[:], wgg[:, kc, jc:jc + 128], xT[:, kc, tsl], kc == 0, kc == 7,
                           [kgg] + xT_keys(tt), [pgk])
                        if kc % 2 == 1:
                            yield
                    sg_, sgk = newscr()
                    act(sg_[:], psg[:], AF.Tanh, [pgk], [sgk], scale=0.5)
                    stt(ub[:, 32 + tt * 512:32 + (tt + 1) * 512], sg_[:], 1.0, psv_[:], ALU.add, ALU.mult,
                        [pvk, sgk], [("ubuf", par, tt)])
                    relps(pvk, pgk)
                cp(halo[:, j, :], ub[:, TS:TS + 32], [("ubuf", par, NT - 1)], [("halo", j)], eng="act")
                for t0_, t1_ in ((0, 8), (8, 16), (16, 24), (24, TAPS)):
                    nt_ = t1_ - t0_
                    tt_(Dg[par][:, t0_:t1_, :], identB[:].unsqueeze(1).broadcast_to([128, nt_, 128]),
                        cwh[:, j * TAPS + t0_:j * TAPS + t1_].unsqueeze(2).broadcast_to([128, nt_, 128]), ALU.mult,
                        ["identB", "cwh"], [("Dg", par, t0_ // 8)])
                    yield

            def conv_back(j):
                par = j % 2
                ub = ubuf[par]
                for tt in range(NT):
                    pc, pck = psb[6 + par], ("ps", 6 + par)
                    ur = [("ubuf_h", par)] + [("ubuf", par, t_) for t_ in range(tt + 1)]
                    for tap in range(TAPS):
                        off = 2 + tt * 512 + tap
                        mm(pc[:], Dg[par][:, tap, :], ub[:, off:off + 512], tap == 0, tap == TAPS - 1,
                           [("Dg", par, tap // 8)] + ur, [pck])
                        if tap % 3 == 2:
                            yield
                    act(cpre[:, j, tt * 512:(tt + 1) * 512], pc[:], AF.Identity, [pck, "pp"], k_cpre(j, tt),
                        bias=cb[:, j:j + 1])
                    yield

            def merged(gens):
                gens = list(gens)
                while gens:
                    for g in list(gens):
                        try:
                            next(g)
                        except StopIteration:
                            gens.remove(g)
                            continue
                        yield

            def thread(front, back, n):
                yield from front(0)
                for i in range(n):
                    gs = [back(i)]
                    if i + 1 < n:
                        gs.append(front(i + 1))
                    yield from merged(gs)

            threads = []
            if dbg >= 3:
                threads.append(thread(head_front, head_back, NH))
            if dbg >= 4:
                threads.append(thread(conv_front, conv_back, 8))
            while threads:
                for th in list(threads):
                    try:
                        next(th)
                    except StopIteration:
                        threads.remove(th)

            for tt in range(NT if dbg >= 4 else 0):
                tsl = slice(tt * 512, (tt + 1) * 512)
                pS1, pS1k = newps()
                pS2, pS2k = newps()
                for j in range(8):
                    cbf, cbk = newscb()
                    csq, csk = newscb()
                    cp(cbf[:], cpre[:, j, tsl], k_cpre(j, tt), [cbk])
                    act(csq[:], cpre[:, j, tsl], AF.Square, k_cpre(j, tt), [csk])
                    mm(pS1[:], ones1024[:], cbf[:], j == 0, j == 7, ["ones1024", cbk], [pS1k])
                    mm(pS2[:], ones1024[:], csq[:], j == 0, j == 7, ["ones1024", csk], [pS2k])
                mean, mk_ = cmean, "cmean"
                cp(mean[:], pS1[:], [pS1k], [mk_], eng="act")
                relps(pS1k)
                msq, msk = newscr()
                tt_(msq[:], mean[:], mean[:], ALU.mult, [mk_], [msk])
                var, vk = newscr()
                tt_(var[:], pS2[:], msq[:], ALU.subtract, [pS2k, msk], [vk])
                relps(pS2k)
                lnv, lnk = newscr()
                act(lnv[:], var[:], AF.Ln, [vk, "cs"], [lnk], bias=c_eps_ln)
                rstd, rsk = crstd, "crstd"
                act(rstd[:], lnv[:], AF.Exp, [lnk], [rsk], scale=-0.5)
                for j in range(8):
                    ta, tak = newscr()
                    tt_(ta[:], cpre[:, j, tsl], mean[:], ALU.subtract, k_cpre(j, tt) + [mk_], [tak])
                    t2, t2k = newscr()
                    tt_(t2[:], ta[:], rstd[:], ALU.mult, [tak, rsk], [t2k])
                    act(un[:, j, tsl], t2[:], AF.Silu, [t2k, "pp"], k_un(j, tt),
                        scale=cg[:, j:j + 1], bias=cbb[:, j:j + 1])

            for j in range(8 if dbg >= 5 else 0):
                if j % 2 == 0:
                    wha, kha = wblock(w_hg_d, 0, 8, (j // 2) * 256)
                    wcb, kcb = wblock(w_cv_d, 0, 8, (j // 2) * 256)
                    wga, kga = wblock(w_in_d, 0, 8, 6144 + (j // 2) * 256)
                    wgb, kgb = wblock(w_in_d, 0, 8, 7168 + (j // 2) * 256)
                jc = (j % 2) * 128
                for tt in range(NT):
                    tsl = slice(tt * 512, (tt + 1) * 512)
                    pya, pyak = newps()
                    pyb, pybk = newps()
                    pga, pgak = newps()
                    pgb, pgbk = newps()
                    for kc in range(8):
                        mm(pga[:], wga[:, kc, jc:jc + 128], xT[:, kc, tsl], kc == 0, kc == 7,
                           [kga] + xT_keys(tt), [pgak])
                    for kc in range(8):
                        mm(pgb[:], wgb[:, kc, jc:jc + 128], xT[:, kc, tsl], kc == 0, kc == 7,
                           [kgb] + xT_keys(tt), [pgbk])
                    for kc in range(8):
                        mm(pya[:], wha[:, kc, jc:jc + 128], on[:, kc, tsl], kc == 0, kc == 7,
                           [kha, ("on", kc, tt)], [pyak])
                    for kc in range(8):
                        mm(pyb[:], wcb[:, kc, jc:jc + 128], un[:, kc, tsl], kc == 0, kc == 7,
                           [kcb] + k_un(kc, tt), [pybk])
                    sa, sak = newscr()
                    act(sa[:], pga[:], AF.Tanh, [pgak], [sak], scale=0.5)
                    sb_, sbk = newscr()
                    act(sb_[:], pgb[:], AF.Tanh, [pgbk], [sbk], scale=0.5)
                    m1, m1k = newscr()
                    stt(m1[:], sa[:], 1.0, pya[:], ALU.add, ALU.mult, [pyak, sak], [m1k])
                    m2, m2k = newscr()
                    stt(m2[:], sb_[:], 1.0, pyb[:], ALU.add, ALU.mult, [pybk, sbk], [m2k])
                    relps(pyak, pybk, pgak, pgbk)
                    tt_(mixed[:, j, tsl], m1[:], m2[:], ALU.add, [m1k, m2k], k_mixed(j, tt))

            wob = [wblock(w_out_d, 0, 8, nb * 256) for nb in range(4)] if dbg >= 6 else []
            for tb in range(NB if dbg >= 6 else 0):
                tt = tb // 4
                for half in range(2):
                    ps, pk = newps()
                    for q in range(2):
                        wblk, wk = wob[half * 2 + q]
                        for kc in range(8):
                            mm(ps[:, q * 256:(q + 1) * 256], mixed[:, kc, tb * 128:(tb + 1) * 128], wblk[:, kc, :],
                               kc == 0, kc == 7, [wk] + k_mixed(kc, tt), [pk])
                    stt(R[:, tb, half * 512:(half + 1) * 512], R[:, tb, half * 512:(half + 1) * 512], 2.0 * ALPHA, ps[:],
                        ALU.mult, ALU.add, [("R", tb), pk], [("R", tb)])
                    relps(pk)
            for tb in range(NB if dbg >= 6 else 0):
                layer_norm_R(tb, 0, c_eps_ln4)
            for tb in range(NB if dbg >= 6 else 0):
                transpose_blk(R[:, tb, :], [("R", tb)], tb, sti % 2)

            if sti + 1 < NST:
                prefetch_x_load(sti + 1)
            for jj in range(22 if dbg >= 8 else 0):
                if jj == 12 and sti + 1 < NST:
                    prefetch_x_transpose(sti + 1)
                if jj % 2 == 0:
                    wg_, kg_ = wblock(w_f1_d, 0, 8, (jj // 2) * 256)
                    wu_, ku_ = wblock(w_f1_d, 0, 8, FFN + (jj // 2) * 256)
                jc = (jj % 2) * 128
                for tt in range(NT):
                    tsl = slice(tt * 512, (tt + 1) * 512)
                    pg_, pgk_ = newps()
                    pu_, puk_ = newps()
                    for kc in range(8):
                        mm(pg_[:], wg_[:, kc, jc:jc + 128], xT[:, kc, tsl], kc == 0, kc == 7,
                           [kg_] + xT_keys(tt), [pgk_])
                    for kc in range(8):
                        mm(pu_[:], wu_[:, kc, jc:jc + 128], xT[:, kc, tsl], kc == 0, kc == 7,
                           [ku_] + xT_keys(tt), [puk_])
                    sg_, sgk = newscr()
                    act(sg_[:], pg_[:], AF.Silu, [pgk_], [sgk])
                    tt_(actb[:, jj, tsl], sg_[:], pu_[:], ALU.mult, [sgk, puk_], k_act(jj, tt))
                    relps(pgk_, puk_)

            for nb in range(4 if dbg >= 9 else 0):
                wfb = [wblock(w_f2_d, g * 1024, (8 if g < 2 else 6), nb * 256) for g in range(3)]
                for tb in range(NB):
                    tt = tb // 4
                    ps, pk = newps()
                    for kc in range(22):
                        wblk, wk = wfb[kc // 8]
                        mm(ps[:, 0:256], actb[:, kc, tb * 128:(tb + 1) * 128], wblk[:, kc % 8, :],
                           kc == 0, kc == 21, [wk] + k_act(kc, tt), [pk])
                    stt(R[:, tb, nb * 256:(nb + 1) * 256], R[:, tb, nb * 256:(nb + 1) * 256], ALPHA, ps[:, 0:256],
                        ALU.mult, ALU.add, [("R", tb), pk], [("R", tb)])
                    relps(pk)
                    if nb == 3:
                        layer_norm_R(tb, 2)
                        dma("sp", out_d[t0 + tb * 128:t0 + (tb + 1) * 128, :], R[:, tb, :], [("R", tb)], [], f"o{tb}")
            if dbg < 9:
                for tb in range(NB):
                    dma("sp", out_d[t0 + tb * 128:t0 + (tb + 1) * 128, :], R[:, tb, :], [("R", tb)], [], f"o{tb}")

        P.add("sp", None)
        fin = P.ops[-1]
        if LIST_SCHED:
            P.list_schedule(SCHED_MODE)
            print("list schedule: estimated makespan %.0f us" % P.est_makespan)
        P.finalize()
        fin["waits"] = [(("d", f"o{tb}"), P.dsem_count[f"o{tb}"]) for tb in range(NB)]
        P.emit(block, esems, dsems)
    return nc


PP_N = 320


def CST_N(TS):
    return 320 + TS


def make_consts(TS):
    c = np.zeros((128, CST_N(TS)), np.float32)
    c[:, 0:128] = np.eye(128, dtype=np.float32)
    p = np.arange(128)[:, None]
    t = np.arange(128)[None, :]
    c[:, 128:256] = ((p // 64 == t // 64) & (p % 64 <= t % 64)).astype(np.float32)
    c[:, 256] = RMS_EPS
    c[:, 257] = LN_EPS
    c[:, 258] = 4.0 * LN_EPS
    m = np.ones(TS, np.float32)
    m[::64] = 0.0
    c[:, 320:320 + TS] = m
    return c


def pack_params(lb_param, hg_norm_g, conv_w, conv_b, conv_ln_g, conv_ln_b):
    pp = np.zeros((128, PP_N), np.float32)
    pp[:, 0:16] = lb_param.reshape(2, NH, 128).transpose(2, 0, 1).reshape(128, 16)
    pp[:, 16] = hg_norm_g.reshape(128)
    pp[:, 32:32 + 8 * TAPS] = conv_w.reshape(TAPS, 8, 128).transpose(2, 1, 0).reshape(128, 8 * TAPS)
    pp[:, 288:296] = conv_b.reshape(8, 128).T
    pp[:, 296:304] = conv_ln_g.reshape(8, 128).T
    pp[:, 304:312] = conv_ln_b.reshape(8, 128).T
    return pp


def make_in_maps(x, w_in, lb_param, hg_norm_g, w_hg_out, conv_w, conv_b, conv_ln_g, conv_ln_b,
                 w_conv_out, w_out, ln1_g, ln1_b, w_ffn_in, w_ffn_out, ln2_g, ln2_b, TS=512):
    f = lambda a: np.ascontiguousarray(np.asarray(a, dtype=np.float32))
    B = x.shape[0]
    pp = pack_params(f(lb_param), f(hg_norm_g)[0], f(conv_w)[0], f(conv_b)[0], f(conv_ln_g)[0], f(conv_ln_b)[0])
    lnp = np.ascontiguousarray(np.broadcast_to(
        np.stack([f(ln1_g)[0], f(ln1_b)[0], f(ln2_g)[0], f(ln2_b)[0]])[None], (128, 4, D)))
    shared = {
        "w_in": f(w_in)[0], "w_hg_out": f(w_hg_out)[0], "w_conv_out": f(w_conv_out)[0], "w_out": f(w_out)[0],
        "w_ffn_in": f(w_ffn_in)[0], "w_ffn_out": f(w_ffn_out)[0], "pp": pp, "lnp": lnp, "cst": make_consts(TS),
    }
    xs = f(x)
    return [dict(shared, x=xs[b]) for b in range(B)]


def kernel(**inputs):
    TS = 512
    in_maps = make_in_maps(TS=TS, **inputs)
    nc = build_nc(SEQ, TS=TS)
    res = run_bass_kernel_spmd(nc, in_maps, core_ids=list(range(N_CORES)))
    return np.stack([np.asarray(r["out"], dtype=np.float32) for r in res.results], axis=0)
```

```python
import numpy as np
from contextlib import ExitStack
import concourse.bass as bass
import concourse.mybir as mybir
from concourse.bass_utils import run_bass_kernel_spmd

F32 = mybir.dt.float32
BF16 = mybir.dt.bfloat16
AF = mybir.ActivationFunctionType
ALU = mybir.AluOpType

D = 1024
NH = 8
FFN = 2816
TAPS = 31
IN_W = 8192
ALPHA = 2.0 ** 0.25
LN_EPS = 1e-5
RMS_EPS = LN_EPS * 128.0
N_CORES = 8
SEQ = 4096


class Prog:
    ENGS = ("pe", "act", "dve", "pool", "sp")
    SAME_SYNC = ("act", "dve", "pool")

    def __init__(self):
        self.ops = []
        self.lastw = {}
        self.readers = {}
        self.dsem_count = {}
        self.epoch = 0

    def add(self, eng, fn, reads=(), writes=(), dsem=None, dur=None, lat=0.0, tbl=None):
        i = len(self.ops)
        deps = set()
        for r in reads:
            w = self.lastw.get(r)
            if w is not None:
                deps.add(w)
        for w_ in writes:
            lw = self.lastw.get(w_)
            if lw is not None:
                deps.add(lw)
            for rd in self.readers.get(w_, ()):
                deps.add(rd)
        for r in reads:
            self.readers.setdefault(r, []).append(i)
        for w_ in writes:
            self.lastw[w_] = i
            self.readers[w_] = []
        op = dict(eng=eng, fn=fn, deps=deps, dsem=dsem, sig=False, dval=None, ep=self.epoch,
                  dur=(0.3 if dur is None else dur), lat=lat, tbl=tbl)
        if dsem is not None:
            self.dsem_count[dsem] = self.dsem_count.get(dsem, 0) + 16
            op["dval"] = self.dsem_count[dsem]
        self.ops.append(op)
        return i

    def list_schedule(self, mode="blevel"):
        ops = self.ops
        n = len(ops)
        body = [i for i in range(n) if ops[i]["fn"] is not None]
        tailops = [i for i in range(n) if ops[i]["fn"] is None]
        succ = [[] for _ in range(n)]
        indeg = [0] * n
        for i in body:
            for d in ops[i]["deps"]:
                succ[d].append(i)
                indeg[i] += 1
        last_d = {}
        for i in body:
            k = ops[i]["dsem"]
            if k is not None:
                if k in last_d and last_d[k] not in ops[i]["deps"]:
                    succ[last_d[k]].append(i)
                    indeg[i] += 1
                last_d[k] = i
        blev = [0.0] * n
        for i in reversed(body):
            m = 0.0
            for j in succ[i]:
                if blev[j] > m:
                    m = blev[j]
            blev[i] = m + ops[i]["dur"] + ops[i]["lat"] + (DMA_BOOST if ops[i]["dsem"] is not None else 0.0)
        finish = [0.0] * n
        ready_t = [0.0] * n
        eng_free = {e: 0.0 for e in self.ENGS}
        act_tbl = [None]
        ready = {e: [] for e in self.ENGS}
        for i in body:
            if indeg[i] == 0:
                ready[ops[i]["eng"]].append(i)
        order = []
        left = len(body)
        while left:
            best = None
            for e in self.ENGS:
                lst = ready[e]
                if not lst:
                    continue
                t_e = max(eng_free[e], min(ready_t[i] for i in lst))
                if best is None or t_e < best[0]:
                    best = (t_e, e)
            t_e, e = best
            lst = ready[e]
            cands = [i for i in lst if ready_t[i] <= t_e + 1e-9]
            if mode == "blevel":
                if e == "act":
                    cur = act_tbl[0]
                    i = max(cands, key=lambda q: (blev[q] - (1.3 if (ops[q]["tbl"] not in (None, cur)) else 0.0), -q))
                else:
                    i = max(cands, key=lambda q: (blev[q], -q))
            else:
                i = min(cands)
            lst.remove(i)
            op = ops[i]
            dur = op["dur"]
            if e == "act" and op["tbl"] is not None and op["tbl"] != act_tbl[0]:
                dur += 1.3
                act_tbl[0] = op["tbl"]
            eng_free[e] = t_e + dur
            finish[i] = t_e + dur + op["lat"]
            order.append(i)
            left -= 1
            for j in succ[i]:
                indeg[j] -= 1
                lat = 0.0 if ops[j]["eng"] == e and op["dsem"] is None else XLAT
                if finish[i] + lat > ready_t[j]:
                    ready_t[j] = finish[i] + lat
                if indeg[j] == 0:
                    ready[ops[j]["eng"]].append(j)
        order += tailops
        remap = {old: new for new, old in enumerate(order)}
        newops = [ops[i] for i in order]
        for op in newops:
            op["deps"] = {remap[d] for d in op["deps"]}
        self.ops = newops
        self.est_makespan = max(finish) if finish else 0.0

    def finalize(self):
        ops = self.ops
        for op in ops:
            keep = set()
            for d in op["deps"]:
                dop = ops[d]
                if dop["dsem"] is not None:
                    keep.add(d)
                elif dop["eng"] != op["eng"] or op["eng"] in self.SAME_SYNC:
                    dop["sig"] = True
                    keep.add(d)
            op["deps"] = keep
        cnt = {}
        for op in ops:
            if op["dsem"] is None and op["sig"]:
                k = (op["eng"], op["ep"])
                cnt[k] = cnt.get(k, 0) + 1
                op["sval"] = cnt[k]
        self.max_sval = max(cnt.values()) if cnt else 0
        waited = {e: {} for e in self.ENGS}
        for op in ops:
            need = {}
            for d in op["deps"]:
                dop = ops[d]
                if dop["dsem"] is not None:
                    key, val = ("d", dop["dsem"]), dop["dval"]
                else:
                    key, val = ("e", (dop["eng"], dop["ep"])), dop["sval"]
                if val > need.get(key, 0):
                    need[key] = val
            w = waited[op["eng"]]
            waits = []
            for key, val in need.items():
                if val > w.get(key, 0):
                    w[key] = val
                    waits.append((key, val))
            op["waits"] = waits

    def emit(self, block, esems, dsems):
        handles = {"pe": block.tensor, "act": block.scalar, "dve": block.vector,
                   "pool": block.gpsimd, "sp": block.sync}
        for ename in self.ENGS:
            myops = [op for op in self.ops if op["eng"] == ename]
            if not myops:
                continue

            def body(eng, myops=myops, ename=ename):
                for op in myops:
                    for (kind, k), val in op["waits"]:
                        eng.wait_ge(dsems[k] if kind == "d" else esems[k], val)
                    if op["fn"] is None:
                        continue
                    inst = op["fn"](eng)
                    if op["dsem"] is not None:
                        inst.then_inc(dsems[op["dsem"]], 16)
                    elif op["sig"]:
                        inst.then_inc(esems[(ename, op["ep"])], 1)

            handles[ename](body)


LIST_SCHED = True
XLAT = 0.35
DMA_BOOST = 0.0
SCHED_MODE = "blevel"


def build_nc(S, TS=512, NSLOT=8, dbg=99):
    NST = S // TS
    NT = TS // 512
    NB = TS // 128
    NCH = TS // 64
    assert S % TS == 0 and TS % 512 == 0

    nc = bass.Bass("TRN2", target_bir_lowering=False)

    def din(name, shape):
        return nc.dram_tensor(name, shape, F32, kind="ExternalInput").ap()

    x_d = din("x", [S, D])
    w_in_d = din("w_in", [D, IN_W])
    w_hg_d = din("w_hg_out", [D, D])
    w_cv_d = din("w_conv_out", [D, D])
    w_out_d = din("w_out", [D, D])
    w_f1_d = din("w_ffn_in", [D, 2 * FFN])
    w_f2_d = din("w_ffn_out", [FFN, D])
    pp_d = din("pp", [128, PP_N])
    lnp_d = din("lnp", [128, 4, D])
    cst_d = din("cst", [128, CST_N(TS)])
    out_d = nc.dram_tensor("out", [S, D], F32, kind="ExternalOutput").ap()

    P = Prog()
    es = ExitStack()
    with es:
        def sb(name, shape, dt):
            return es.enter_context(nc.sbuf_tensor("sb_" + name, shape, dt))

        R = sb("R", [128, NB, D], F32)
        xTs = [sb(f"xT{i}", [128, 8, TS], BF16) for i in range(2)]
        on = sb("on", [128, 8, TS], BF16)
        shared = sb("shared", [128, 24 * TS], BF16)
        slots = [sb(f"slot{i}", [128, 8, 256], BF16) for i in range(NSLOT)]
        lnt = sb("lnt", [128, 4, D], F32)
        cs = sb("cs", [128, CST_N(TS)], F32)
        pp = sb("pp", [128, PP_N], F32)
        Fall = sb("Fall", [128, 8, TS], F32)
        Fb = [[Fall[:, p * 4 + i, :] for i in range(4)] for p in range(2)]
        Qpp = [sb(f"Qpp{p}", [128, TS], BF16) for p in range(2)]
        Kt = [sb(f"Kt{p}", [128, TS], BF16) for p in range(2)]
        ogs = [sb(f"ogs{p}", [128, TS], BF16) for p in range(2)]
        v_tm = [sb(f"v_tm{p}", [128, NB, 128], BF16) for p in range(2)]
        K_tm = [[sb(f"K_tm{p}_{i}", [128, NB, 128], BF16) for i in range(2)] for p in range(2)]
        KVs = [sb(f"KVs{p}", [128, NCH, 128], F32) for p in range(2)]
        Tb = [sb(f"Tb{p}", [128, NCH + 1, 128], BF16) for p in range(2)]
        ebuf = [sb(f"ebuf{p}", [128, NCH], F32) for p in range(2)]
        Am = [sb(f"Am{i}", [128, 4, 128], BF16) for i in range(2)]
        Tprev = sb("Tprev", [128, NH, 128], BF16)
        halo = sb("halo", [128, 8, 32], BF16)
        ubuf = [sb(f"ubuf{p}", [128, 32 + TS], BF16) for p in range(2)]
        Dg = [sb(f"Dg{p}", [128, TAPS, 128], BF16) for p in range(2)]
        scr = [sb(f"scr{i}", [128, 512], F32) for i in range(8)]
        scb = [sb(f"scb{i}", [128, 512], BF16) for i in range(4)]
        identB = sb("identB", [128, 128], BF16)
        ones128 = sb("ones128", [128, 128], BF16)
        ones1024 = sb("ones1024", [128, 128], BF16)
        lbt = sb("lbt", [128, 4, NH], F32)
        cwh = sb("cwh", [128, 8 * TAPS], F32)
        st6 = sb("st6", [128, 12], F32)
        mv = sb("mv", [128, 4], F32)
        cmean = sb("cmean", [128, 512], F32)
        crstd = sb("crstd", [128, 512], F32)
        psb = [es.enter_context(nc.psum_tensor(f"psb{i}", [128, 512], F32)) for i in range(8)]

        cpre = shared[:, 0:16 * TS].bitcast(F32).rearrange("p (j t) -> p j t", j=8)
        un = shared[:, 16 * TS:24 * TS].rearrange("p (j t) -> p j t", j=8)
        mixed = shared[:, 0:8 * TS].rearrange("p (j t) -> p j t", j=8)
        actb = shared[:, 0:22 * TS].rearrange("p (j t) -> p j t", j=22)

        def k_cpre(j, tt):
            b0 = (j * TS + tt * 512) * 4
            return [("sh", b0 // 1024), ("sh", b0 // 1024 + 1)]

        def k_bf(base_seg_elems, j, tt):
            b0 = (base_seg_elems + j * TS + tt * 512) * 2
            return [("sh", b0 // 1024)]

        def k_un(j, tt):
            return k_bf(16 * TS, j, tt)

        def k_mixed(j, tt):
            return k_bf(0, j, tt)

        def k_act(j, tt):
            return k_bf(0, j, tt)

        identF = cs[:, 0:128]
        mask2 = cs[:, 128:256]
        c_eps_rms = cs[:, 256:257]
        c_eps_ln = cs[:, 257:258]
        c_eps_ln4 = cs[:, 258:259]
        scanmask = cs[:, 320:320 + TS]
        lbp = pp[:, 0:16].rearrange("p (a h) -> p a h", a=2)
        gn = pp[:, 16:17]
        cw = pp[:, 32:32 + 8 * TAPS].rearrange("p (j t) -> p j t", j=8)
        cb = pp[:, 288:296]
        cg = pp[:, 296:304]
        cbb = pp[:, 304:312]

        esems = {(e, ep): es.enter_context(nc.semaphore(f"s_{e}_{ep}"))
                 for e in ("pe", "act", "dve", "pool") for ep in range(NST + 1)}
        dnames = ([f"ws{i}" for i in range(NSLOT)] + [f"r{i}" for i in range(NB)]
                  + [f"o{i}" for i in range(NB)] + [f"xs{i}" for i in range(NB)] + ["c0", "c1", "c2"])
        dsems = {d: es.enter_context(nc.semaphore("d_" + d)) for d in dnames}
        print('sbuf bytes remaining', nc.sbuf_bytes_remaining)
        block = es.enter_context(nc.Block())

        def fsz(ap):
            n = 1
            for d in ap.shape[1:]:
                n *= d
            return n

        def mm(out, lhsT, rhs, start, stop, reads, writes):
            P.add("pe", lambda e: e.matmul(out, lhsT=lhsT, rhs=rhs, start=start, stop=stop), reads, writes,
                  dur=max(64, fsz(rhs)) / 2100.0 + 0.01)

        def tr(out, in_, ident, reads, writes):
            P.add("pe", lambda e: e.transpose(out, in_, ident), reads, writes, dur=0.09)

        def act(out, in_, func, reads, writes, scale=None, bias=None):
            kw = {}
            if scale is not None:
                kw["scale"] = scale
            if bias is not None:
                kw["bias"] = bias
            tbl = {AF.Tanh: "A", AF.Silu: "A", AF.Ln: "B", AF.Sigmoid: "C"}.get(func)
            P.add("act", lambda e: e.activation(out=out, in_=in_, func=func, **kw), reads, writes,
                  dur=0.17 + fsz(out) / 1200.0 + (0.19 if (scale is not None and not isinstance(scale, float)) or
                                                     (bias is not None and not isinstance(bias, float)) else 0.0),
                  tbl=tbl)

        def tt_(out, in0, in1, op, reads, writes, eng="dve"):
            P.add(eng, lambda e: e.tensor_tensor(out=out, in0=in0, in1=in1, op=op), reads, writes,
                  dur=0.15 + fsz(out) / 960.0)

        def ts_(out, in0, s1, s2, op0, op1, reads, writes, eng="dve"):
            P.add(eng, lambda e: e.tensor_scalar(out=out, in0=in0, scalar1=s1, scalar2=s2, op0=op0, op1=op1),
                  reads, writes, dur=0.15 + fsz(out) / 960.0)

        def stt(out, in0, scalar, in1, op0, op1, reads, writes):
            P.add("dve", lambda e: e.scalar_tensor_tensor(out=out, in0=in0, scalar=scalar, in1=in1,
                                                          op0=op0, op1=op1), reads, writes,
                  dur=0.15 + fsz(out) / 960.0)

        def cp(out, in_, reads, writes, eng="dve"):
            if eng == "act":
                P.add("act", lambda e: e.activation(out=out, in_=in_, func=AF.Copy), reads, writes,
                      dur=0.17 + fsz(out) / 1200.0)
            else:
                P.add(eng, lambda e: e.tensor_copy(out=out, in_=in_), reads, writes, dur=0.15 + fsz(out) / 960.0)

        def dma(eng, out, in_, reads, writes, dsem):
            nbytes = 128 * fsz(out) * 4
            P.add(eng, lambda e: e.dma_start(out=out, in_=in_), reads, writes, dsem=dsem,
                  dur=0.1, lat=2.0 + nbytes / 150000.0)

        st = {"ps": 0, "scr": 0, "scb": 0, "slot": 0, "alt": 0, "am": 0}

        ps_free = list(range(6))

        def newps():
            assert ps_free, "out of PSUM banks"
            i = ps_free.pop(0)
            return psb[i], ("ps", i)

        def relps(*keys):
            for k in keys:
                assert k[1] not in ps_free
                ps_free.append(k[1])

        def newscr():
            i = st["scr"]; st["scr"] = (i + 1) % 8
            return scr[i], ("scr", i)

        def newscb():
            i = st["scb"]; st["scb"] = (i + 1) % 4
            return scb[i], ("scb", i)

        def wblock(wd, k0, KC, n0):
            i = st["slot"]; st["slot"] = (i + 1) % NSLOT
            key = ("slot", i)
            src = wd[k0:k0 + KC * 128, n0:n0 + 256].rearrange("(kc p) n -> p kc n", p=128)
            dma("pool", slots[i][:, 0:KC, :], src, [], [key], f"ws{i}")
            return slots[i], key

        def alt_eng():
            st["alt"] ^= 1
            return "act" if st["alt"] else "dve"

        dma("sp", cs[:], cst_d, [], ["cs"], "c0")
        dma("sp", pp[:], pp_d, [], ["pp"], "c1")
        dma("sp", lnt[:], lnp_d, [], ["lnt"], "c2")
        cp(identB[:], identF, ["cs"], ["identB"])
        P.add("dve", lambda e: e.memset(scr[0][:], 0.0), [], [("scr", 0)])
        P.add("dve", lambda e: e.memset(scr[1][:], 1.0 / 128.0), [], [("scr", 1)])
        P.add("dve", lambda e: e.memset(scr[2][:], 1.0 / 1024.0), [], [("scr", 2)])
        cp(ones128[:], scr[1][:, 0:128], [("scr", 1)], ["ones128"])
        cp(ones1024[:], scr[2][:, 0:128], [("scr", 2)], ["ones1024"])
        Tpf = Tprev[:].rearrange("p h d -> p (h d)")
        cp(Tpf[:, 0:512], scr[0][:], [("scr", 0)], [("Tprev", h) for h in range(4)])
        cp(Tpf[:, 512:1024], scr[0][:], [("scr", 0)], [("Tprev", h) for h in range(4, 8)])
        cp(halo[:].rearrange("p j t -> p (j t)"), scr[0][:, 0:256], [("scr", 0)], [("halo", j) for j in range(8)])
        for p_ in range(2):
            for i in range(2):
                for tb in range(0, NB, 4):
                    cp(K_tm[p_][i][:, tb:tb + 4, :].rearrange("p a b -> p (a b)"), scr[0][:], [("scr", 0)],
                       [("Ktm", p_, tb // 4, 0), ("Ktm", p_, tb // 4, 1)])
        tt_(lbt[:, 3, :], lbp[:, 0, :], lbp[:, 1, :], ALU.subtract, ["pp"], ["lbt3"])
        act(lbt[:, 0, :], lbt[:, 3, :], AF.Sigmoid, ["lbt3"], ["lbt0"])
        ts_(lbt[:, 1, :], lbt[:, 0, :], -0.5, 0.5, ALU.mult, ALU.add, ["lbt0"], ["lbt1"])
        ts_(lbt[:, 2, :], lbt[:, 0, :], 0.5, -0.5, ALU.mult, ALU.add, ["lbt0"], ["lbt2"])
        ts_(lbt[:, 3, :], lbt[:, 0, :], 0.5, 0.5, ALU.mult, ALU.add, ["lbt0"], ["lbt3b"])
        ts_(cwh[:], pp[:, 32:32 + 8 * TAPS], 0.5, None, ALU.mult, ALU.bypass, ["pp"], ["cwh"])
        LB = ["lbt3b", "lbt1", "lbt2"]

        def transpose_blk(src, src_keys, tb, xb):
            for g in range(2):
                ps, pk = newps()
                for kk in range(4):
                    kc = g * 4 + kk
                    tr(ps[:, kk * 128:(kk + 1) * 128], src[:, kc * 128:(kc + 1) * 128], identF,
                       list(src_keys) + ["cs"], [pk])
                cp(xTs[xb][:, g * 4:(g + 1) * 4, tb * 128:(tb + 1) * 128],
                   ps[:].rearrange("p (a b) -> p a b", a=4), [pk], [("xT", xb, tb, g)], eng=alt_eng())
                relps(pk)

        def xT_keys(tt):
            return [("xT", st["xb"], tb, g) for tb in range(tt * 4, tt * 4 + 4) for g in range(2)]

        def xstage(tb):
            ap = Fall[:, 2 * tb:2 * tb + 2, :].rearrange("p a t -> p (a t)")
            keys = [(f"F{idx % 4}", idx // 4, 0) for idx in (2 * tb, 2 * tb + 1)]
            return ap, keys

        def prefetch_x_load(sti_):
            for tb in range(NB):
                ap, keys = xstage(tb)
                dma("sp", ap, x_d[sti_ * TS + tb * 128:sti_ * TS + (tb + 1) * 128, :], [], keys, f"xs{tb}")

        def prefetch_x_transpose(sti_):
            for tb in range(NB):
                ap, keys = xstage(tb)
                transpose_blk(ap, keys, tb, sti_ % 2)

        def layer_norm_R(tb, gi, eps_ap=None):
            eps_ap = c_eps_ln if eps_ap is None else eps_ap
            rk = ("R", tb)
            P.add("dve", lambda e: e.bn_stats(out=st6[:, 0:6], in_=R[:, tb, 0:512]), [rk], ["st6a"])
            P.add("dve", lambda e: e.bn_stats(out=st6[:, 6:12], in_=R[:, tb, 512:1024]), [rk], ["st6b"])
            P.add("dve", lambda e: e.bn_aggr(out=mv[:, 0:2], in_=st6[:]), ["st6a", "st6b"], ["mv01"])
            act(mv[:, 2:3], mv[:, 1:2], AF.Ln, ["mv01", "cs"], ["mv2"], bias=eps_ap)
            act(mv[:, 3:4], mv[:, 2:3], AF.Exp, ["mv2"], ["mv3"], scale=-0.5)
            ts_(R[:, tb, :], R[:, tb, :], mv[:, 0:1], mv[:, 3:4], ALU.subtract, ALU.mult,
                [rk, "mv01", "mv3"], [rk])
            tt_(R[:, tb, :], R[:, tb, :], lnt[:, gi, :], ALU.mult, [rk, "lnt"], [rk])
            tt_(R[:, tb, :], R[:, tb, :], lnt[:, gi + 1, :], ALU.add, [rk, "lnt"], [rk])

        for sti in range(NST):
            t0 = sti * TS
            P.epoch = sti + 1
            st["xb"] = sti % 2
            xT = xTs[sti % 2]
            if sti == 0:
                prefetch_x_load(0)
                prefetch_x_transpose(0)
            for tb in range(NB):
                dma("sp", R[:, tb, :], x_d[t0 + tb * 128:t0 + (tb + 1) * 128, :], [], [("R", tb)], f"r{tb}")

            wst = {}

            def head_front(h):
                par = h % 2
                if h % 2 == 0:
                    hp = h // 2
                    for nm, sec in (("q", 0), ("f", 1), ("i", 2), ("o", 3)):
                        wst[nm] = wblock(w_in_d, 0, 8, sec * 1024 + hp * 256)
                (wq, kq), (wf, kf), (wi, ki), (wo, ko) = wst["q"], wst["f"], wst["i"], wst["o"]
                hc = (h % 2) * 128
                F0, F1, F2, F3 = Fb[par]
                allF = lambda n: [(n, par, tt) for tt in range(NT)]
                for tt in range(NT):
                    tsl = slice(tt * 512, (tt + 1) * 512)
                    for (wblk, wk, dst, dk_, fn) in ((wq, kq, F0, ("F0", par, tt), AF.Silu),
                                                     (wf, kf, F1, ("F1", par, tt), AF.Tanh),
                                                     (wo, ko, ogs[par], ("ogs", par, tt), AF.Silu)):
                        ps, pk = newps()
                        for kc in range(8):
                            mm(ps[:], wblk[:, kc, hc:hc + 128], xT[:, kc, tsl], kc == 0, kc == 7,
                               [wk] + xT_keys(tt), [pk])
                        act(dst[:, tsl], ps[:], fn, [pk], [dk_], scale=(0.5 if fn == AF.Tanh else None))
                        relps(pk)
                        yield
                for tg in range(NB // 4):
                    ps, pk = newps()
                    for tl in range(4):
                        tb = tg * 4 + tl
                        for kc in range(8):
                            mm(ps[:, tl * 128:(tl + 1) * 128], xT[:, kc, tb * 128:(tb + 1) * 128],
                               wi[:, kc, hc:hc + 128], kc == 0, kc == 7,
                               [ki, ("xT", st["xb"], tb, 0), ("xT", st["xb"], tb, 1)], [pk])
                    cp(v_tm[par][:, tg * 4:(tg + 1) * 4, :], ps[:].rearrange("p (a b) -> p a b", a=4),
                       [pk], [("vtm", par, tg)])
                    relps(pk)
                    yield
                act(F2[:], F1[:], AF.Ln, allF("F1") + LB, allF("F2"),
                    scale=lbt[:, 1, h:h + 1], bias=lbt[:, 3, h:h + 1])
                ts_(F3[:], F1[:], lbt[:, 2, h:h + 1], lbt[:, 1, h:h + 1], ALU.mult, ALU.add,
                    allF("F1") + LB, allF("F3"))
                yield
                P.add("dve", lambda e, F1=F1, F2=F2: e.tensor_tensor_scan(
                    out=F1[:], data0=scanmask, data1=F2[:], initial=0.0, op0=ALU.mult, op1=ALU.add),
                    allF("F2") + ["cs"], allF("F1"), dur=0.1 + 2 * TS / 960.0)
                yield
                act(F2[:], F1[:], AF.Exp, allF("F1"), allF("F2"))
                yield
                cp(ebuf[par][:], F2[:].rearrange("p (c t) -> p c t", t=64)[:, :, 63], allF("F2"), [("ebuf", par)])
                tt_(Qpp[par][:], F0[:], F2[:], ALU.mult, allF("F0") + allF("F2"), [("Qpp", par)])
                yield
                act(F0[:], F1[:], AF.Exp, allF("F1"), allF("F0"), scale=-1.0)
                yield
                tt_(Kt[par][:], F3[:], F0[:], ALU.mult, allF("F3") + allF("F0"), [("Kt", par)])
                yield

            def head_back(h):
                par = h % 2
                Ktp, Qp, vt, Tbp, eb, KV = Kt[par], Qpp[par], v_tm[par], Tb[par], ebuf[par], KVs[par]
                for tg in range(NB // 4):
                    ps, pk = newps()
                    psv = ps[:].bitcast(BF16)
                    for tl in range(4):
                        tb = tg * 4 + tl
                        tr(psv[:, tl * 128:(tl + 1) * 128], Ktp[:, tb * 128:(tb + 1) * 128], identB[:],
                           [("Kt", par), "identB"], [pk])
                    for hf in range(2):
                        cp(K_tm[par][hf][hf * 64:hf * 64 + 64, tg * 4:(tg + 1) * 4, :],
                           psv[hf * 64:hf * 64 + 64, 0:512].rearrange("p (a b) -> p a b", a=4),
                           [pk], [("Ktm", par, tg, hf)], eng=("act" if hf == 0 else "dve"))
                    relps(pk)
                    yield
                for cg_ in range(NCH // 4):
                    ps, pk = newps()
                    for cl in range(4):
                        c = cg_ * 4 + cl
                        tb, hf = c // 2, c % 2
                        mm(ps[:, cl * 128:(cl + 1) * 128], K_tm[par][hf][:, tb, :], vt[:, tb, :], True, True,
                           [("Ktm", par, tb // 4, hf), ("vtm", par, tb // 4)], [pk])
                    tt_(KV[:, cg_ * 4:(cg_ + 1) * 4, :], ps[:].rearrange("p (a b) -> p a b", a=4),
                        eb[:, cg_ * 4:(cg_ + 1) * 4].unsqueeze(2).broadcast_to([128, 4, 128]), ALU.mult,
                        [pk, ("ebuf", par)], [("KVs", par, cg_)])
                    relps(pk)
                    yield
                cp(Tbp[:, 0, :], Tprev[:, h, :], [("Tprev", h)], [("Tb", par, 0)])
                for c in range(NCH):
                    stt(Tbp[:, c + 1, :], Tbp[:, c, :], eb[:, c:c + 1], KV[:, c, :], ALU.mult, ALU.add,
                        [("Tb", par, c), ("ebuf", par), ("KVs", par, c // 4)], [("Tb", par, c + 1)])
                    if c % 2 == 1:
                        yield
                cp(Tprev[:, h, :], Tbp[:, NCH, :], [("Tb", par, NCH)], [("Tprev", h)], eng="act")
                yield
                yield
                for tt in range(NT):
                    tsl = slice(tt * 512, (tt + 1) * 512)
                    pA, pAk = newps()
                    for tbl in range(4):
                        tb = tt * 4 + tbl
                        mm(pA[:, tbl * 128:(tbl + 1) * 128], Ktp[:, tb * 128:(tb + 1) * 128],
                           Qp[:, tb * 128:(tb + 1) * 128], True, True, [("Kt", par), ("Qpp", par)], [pAk])
                    st["am"] ^= 1
                    am = Am[st["am"]]
                    amk = ("Am", st["am"])
                    tt_(am[:], pA[:].rearrange("p (a b) -> p a b", a=4),
                        mask2.unsqueeze(1).broadcast_to([128, 4, 128]), ALU.mult, [pAk, "cs"], [amk])
                    relps(pAk)
                    yield
                    pO, pOk = newps()
                    for tbl in range(4):
                        tb = tt * 4 + tbl
                        mm(pO[:, tbl * 128:(tbl + 1) * 128], vt[:, tb, :], am[:, tbl, :], True, False,
                           [("vtm", par, tb // 4), amk], [pOk])
                        for hf in range(2):
                            c = 2 * tb + hf
                            mm(pO[:, tbl * 128 + hf * 64:tbl * 128 + hf * 64 + 64], Tbp[:, c, :],
                               Qp[:, c * 64:(c + 1) * 64], False, hf == 1, [("Tb", par, c), ("Qpp", par)], [pOk])
                    osq, osk = newscb()
                    act(osq[:], pO[:], AF.Square, [pOk], [osk])
                    yield
                    pM, pMk = newps()
                    mm(pM[:], ones128[:], osq[:], True, True, ["ones128", osk], [pMk])
                    lnv, lnk = newscr()
                    act(lnv[:], pM[:], AF.Ln, [pMk, "cs"], [lnk], bias=c_eps_rms)
                    relps(pMk)
                    rstd, rsk = newscr()
                    act(rstd[:], lnv[:], AF.Exp, [lnk], [rsk], scale=-0.5)
                    t1, t1k = newscr()
                    stt(t1[:], pO[:], gn, rstd[:], ALU.mult, ALU.mult, [pOk, "pp", rsk], [t1k])
                    relps(pOk)
                    tt_(on[:, h, tsl], t1[:], ogs[par][:, tsl], ALU.mult, [t1k, ("ogs", par, tt)], [("on", h, tt)])
                    yield

            def conv_front(j):
                par = j % 2
                if j % 2 == 0:
                    wst["gv"] = wblock(w_in_d, 0, 8, 4096 + (j // 2) * 256)
                    wst["gg"] = wblock(w_in_d, 0, 8, 5120 + (j // 2) * 256)
                (wgv, kgv), (wgg, kgg) = wst["gv"], wst["gg"]
                jc = (j % 2) * 128
                ub = ubuf[par]
                cp(ub[:, 0:32], halo[:, j, :], [("halo", j)], [("ubuf_h", par)], eng="act")
                for tt in range(NT):
                    psv_, pvk = newps()
                    psg, pgk = newps()
                    tsl = slice(tt * 512, (tt + 1) * 512)
                    for kc in range(8):
                        mm(psv_[:], wgv[:, kc, jc:jc + 128], xT[:, kc, tsl], kc == 0, kc == 7,
                           [kgv] + xT_keys(tt), [pvk])
                        if kc % 2 == 1:
                            yield
                    for kc in range(8):
                        mm(psg[:], wgg[:, kc, jc:jc + 128], xT[:, kc, tsl], kc == 0, kc == 7,
                           [kgg] + xT_keys(tt), [pgk])
                        if kc % 2 == 1:
                            yield
                    sg_, sgk = newscr()
                    act(sg_[:], psg[:], AF.Tanh, [pgk], [sgk], scale=0.5)
                    stt(ub[:, 32 + tt * 512:32 + (tt + 1) * 512], sg_[:], 1.0, psv_[:], ALU.add, ALU.mult,
                        [pvk, sgk], [("ubuf", par, tt)])
                    relps(pvk, pgk)
                cp(halo[:, j, :], ub[:, TS:TS + 32], [("ubuf", par, NT - 1)], [("halo", j)], eng="act")
                for t0_, t1_ in ((0, 8), (8, 16), (16, 24), (24, TAPS)):
                    nt_ = t1_ - t0_
                    tt_(Dg[par][:, t0_:t1_, :], identB[:].unsqueeze(1).broadcast_to([128, nt_, 128]),
                        cwh[:, j * TAPS + t0_:j * TAPS + t1_].unsqueeze(2).broadcast_to([128, nt_, 128]), ALU.mult,
                        ["identB", "cwh"], [("Dg", par, t0_ // 8)])
                    yield

            def conv_back(j):
                par = j % 2
                ub = ubuf[par]
                for tt in range(NT):
                    pc, pck = psb[6 + par], ("ps", 6 + par)
                    ur = [("ubuf_h", par)] + [("ubuf", par, t_) for t_ in range(tt + 1)]
                    for tap in range(TAPS):
                        off = 2 + tt * 512 + tap
                        mm(pc[:], Dg[par][:, tap, :], ub[:, off:off + 512], tap == 0, tap == TAPS - 1,
                           [("Dg", par, tap // 8)] + ur, [pck])
                        if tap % 3 == 2:
                            yield
                    act(cpre[:, j, tt * 512:(tt + 1) * 512], pc[:], AF.Identity, [pck, "pp"], k_cpre(j, tt),
                        bias=cb[:, j:j + 1])
                    yield

            def merged(gens):
                gens = list(gens)
                while gens:
                    for g in list(gens):
                        try:
                            next(g)
                        except StopIteration:
                            gens.remove(g)
                            continue
                        yield

            def thread(front, back, n):
                yield from front(0)
                for i in range(n):
                    gs = [back(i)]
                    if i + 1 < n:
                        gs.append(front(i + 1))
                    yield from merged(gs)

            threads = []
            if dbg >= 3:
                threads.append(thread(head_front, head_back, NH))
            if dbg >= 4:
                threads.append(thread(conv_front, conv_back, 8))
            while threads:
                for th in list(threads):
                    try:
                        next(th)
                    except StopIteration:
                        threads.remove(th)

            for tt in range(NT if dbg >= 4 else 0):
                tsl = slice(tt * 512, (tt + 1) * 512)
                pS1, pS1k = newps()
                pS2, pS2k = newps()
                for j in range(8):
                    cbf, cbk = newscb()
                    csq, csk = newscb()
                    cp(cbf[:], cpre[:, j, tsl], k_cpre(j, tt), [cbk])
                    act(csq[:], cpre[:, j, tsl], AF.Square, k_cpre(j, tt), [csk])
                    mm(pS1[:], ones1024[:], cbf[:], j == 0, j == 7, ["ones1024", cbk], [pS1k])
                    mm(pS2[:], ones1024[:], csq[:], j == 0, j == 7, ["ones1024", csk], [pS2k])
                mean, mk_ = cmean, "cmean"
                cp(mean[:], pS1[:], [pS1k], [mk_], eng="act")
                relps(pS1k)
                msq, msk = newscr()
                tt_(msq[:], mean[:], mean[:], ALU.mult, [mk_], [msk])
                var, vk = newscr()
                tt_(var[:], pS2[:], msq[:], ALU.subtract, [pS2k, msk], [vk])
                relps(pS2k)
                lnv, lnk = newscr()
                act(lnv[:], var[:], AF.Ln, [vk, "cs"], [lnk], bias=c_eps_ln)
                rstd, rsk = crstd, "crstd"
                act(rstd[:], lnv[:], AF.Exp, [lnk], [rsk], scale=-0.5)
                for j in range(8):
                    ta, tak = newscr()
                    tt_(ta[:], cpre[:, j, tsl], mean[:], ALU.subtract, k_cpre(j, tt) + [mk_], [tak])
                    t2, t2k = newscr()
                    tt_(t2[:], ta[:], rstd[:], ALU.mult, [tak, rsk], [t2k])
                    act(un[:, j, tsl], t2[:], AF.Silu, [t2k, "pp"], k_un(j, tt),
                        scale=cg[:, j:j + 1], bias=cbb[:, j:j + 1])

            for j in range(8 if dbg >= 5 else 0):
                if j % 2 == 0:
                    wha, kha = wblock(w_hg_d, 0, 8, (j // 2) * 256)
                    wcb, kcb = wblock(w_cv_d, 0, 8, (j // 2) * 256)
                    wga, kga = wblock(w_in_d, 0, 8, 6144 + (j // 2) * 256)
                    wgb, kgb = wblock(w_in_d, 0, 8, 7168 + (j // 2) * 256)
                jc = (j % 2) * 128
                for tt in range(NT):
                    tsl = slice(tt * 512, (tt + 1) * 512)
                    pya, pyak = newps()
                    pyb, pybk = newps()
                    pga, pgak = newps()
                    pgb, pgbk = newps()
                    for kc in range(8):
                        mm(pga[:], wga[:, kc, jc:jc + 128], xT[:, kc, tsl], kc == 0, kc == 7,
                           [kga] + xT_keys(tt), [pgak])
                    for kc in range(8):
                        mm(pgb[:], wgb[:, kc, jc:jc + 128], xT[:, kc, tsl], kc == 0, kc == 7,
                           [kgb] + xT_keys(tt), [pgbk])
                    for kc in range(8):
                        mm(pya[:], wha[:, kc, jc:jc + 128], on[:, kc, tsl], kc == 0, kc == 7,
                           [kha, ("on", kc, tt)], [pyak])
                    for kc in range(8):
                        mm(pyb[:], wcb[:, kc, jc:jc + 128], un[:, kc, tsl], kc == 0, kc == 7,
                           [kcb] + k_un(kc, tt), [pybk])
                    sa, sak = newscr()
                    act(sa[:], pga[:], AF.Tanh, [pgak], [sak], scale=0.5)
                    sb_, sbk = newscr()
                    act(sb_[:], pgb[:], AF.Tanh, [pgbk], [sbk], scale=0.5)
                    m1, m1k = newscr()
                    stt(m1[:], sa[:], 1.0, pya[:], ALU.add, ALU.mult, [pyak, sak], [m1k])
                    m2, m2k = newscr()
                    stt(m2[:], sb_[:], 1.0, pyb[:], ALU.add, ALU.mult, [pybk, sbk], [m2k])
                    relps(pyak, pybk, pgak, pgbk)
                    tt_(mixed[:, j, tsl], m1[:], m2[:], ALU.add, [m1k, m2k], k_mixed(j, tt))

            wob = [wblock(w_out_d, 0, 8, nb * 256) for nb in range(4)] if dbg >= 6 else []
            for tb in range(NB if dbg >= 6 else 0):
                tt = tb // 4
                for half in range(2):
                    ps, pk = newps()
                    for q in range(2):
                        wblk, wk = wob[half * 2 + q]
                        for kc in range(8):
                            mm(ps[:, q * 256:(q + 1) * 256], mixed[:, kc, tb * 128:(tb + 1) * 128], wblk[:, kc, :],
                               kc == 0, kc == 7, [wk] + k_mixed(kc, tt), [pk])
                    stt(R[:, tb, half * 512:(half + 1) * 512], R[:, tb, half * 512:(half + 1) * 512], 2.0 * ALPHA, ps[:],
                        ALU.mult, ALU.add, [("R", tb), pk], [("R", tb)])
                    relps(pk)
            for tb in range(NB if dbg >= 6 else 0):
                layer_norm_R(tb, 0, c_eps_ln4)
            for tb in range(NB if dbg >= 6 else 0):
                transpose_blk(R[:, tb, :], [("R", tb)], tb, sti % 2)

            if sti + 1 < NST:
                prefetch_x_load(sti + 1)
            for jj in range(22 if dbg >= 8 else 0):
                if jj == 12 and sti + 1 < NST:
                    prefetch_x_transpose(sti + 1)
                if jj % 2 == 0:
                    wg_, kg_ = wblock(w_f1_d, 0, 8, (jj // 2) * 256)
                    wu_, ku_ = wblock(w_f1_d, 0, 8, FFN + (jj // 2) * 256)
                jc = (jj % 2) * 128
                for tt in range(NT):
                    tsl = slice(tt * 512, (tt + 1) * 512)
                    pg_, pgk_ = newps()
                    pu_, puk_ = newps()
                    for kc in range(8):
                        mm(pg_[:], wg_[:, kc, jc:jc + 128], xT[:, kc, tsl], kc == 0, kc == 7,
                           [kg_] + xT_keys(tt), [pgk_])
                    for kc in range(8):
                        mm(pu_[:], wu_[:, kc, jc:jc + 128], xT[:, kc, tsl], kc == 0, kc == 7,
                           [ku_] + xT_keys(tt), [puk_])
                    sg_, sgk = newscr()
                    act(sg_[:], pg_[:], AF.Silu, [pgk_], [sgk])
                    tt_(actb[:, jj, tsl], sg_[:], pu_[:], ALU.mult, [sgk, puk_], k_act(jj, tt))
                    relps(pgk_, puk_)

            for nb in range(4 if dbg >= 9 else 0):
                wfb = [wblock(w_f2_d, g * 1024, (8 if g < 2 else 6), nb * 256) for g in range(3)]
                for tb in range(NB):
                    tt = tb // 4
                    ps, pk = newps()
                    for kc in range(22):
                        wblk, wk = wfb[kc // 8]
                        mm(ps[:, 0:256], actb[:, kc, tb * 128:(tb + 1) * 128], wblk[:, kc % 8, :],
                           kc == 0, kc == 21, [wk] + k_act(kc, tt), [pk])
                    stt(R[:, tb, nb * 256:(nb + 1) * 256], R[:, tb, nb * 256:(nb + 1) * 256], ALPHA, ps[:, 0:256],
                        ALU.mult, ALU.add, [("R", tb), pk], [("R", tb)])
                    relps(pk)
                    if nb == 3:
                        layer_norm_R(tb, 2)
                        dma("sp", out_d[t0 + tb * 128:t0 + (tb + 1) * 128, :], R[:, tb, :], [("R", tb)], [], f"o{tb}")
            if dbg < 9:
                for tb in range(NB):
                    dma("sp", out_d[t0 + tb * 128:t0 + (tb + 1) * 128, :], R[:, tb, :], [("R", tb)], [], f"o{tb}")

        P.add("sp", None)
        fin = P.ops[-1]
        if LIST_SCHED:
            P.list_schedule(SCHED_MODE)
            print("list schedule: estimated makespan %.0f us" % P.est_makespan)
        P.finalize()
        fin["waits"] = [(("d", f"o{tb}"), P.dsem_count[f"o{tb}"]) for tb in range(NB)]
        P.emit(block, esems, dsems)
    return nc


PP_N = 320


def CST_N(TS):
    return 320 + TS


def make_consts(TS):
    c = np.zeros((128, CST_N(TS)), np.float32)
    c[:, 0:128] = np.eye(128, dtype=np.float32)
    p = np.arange(128)[:, None]
    t = np.arange(128)[None, :]
    c[:, 128:256] = ((p // 64 == t // 64) & (p % 64 <= t % 64)).astype(np.float32)
    c[:, 256] = RMS_EPS
    c[:, 257] = LN_EPS
    c[:, 258] = 4.0 * LN_EPS
    m = np.ones(TS, np.float32)
    m[::64] = 0.0
    c[:, 320:320 + TS] = m
    return c


def pack_params(lb_param, hg_norm_g, conv_w, conv_b, conv_ln_g, conv_ln_b):
    pp = np.zeros((128, PP_N), np.float32)
    pp[:, 0:16] = lb_param.reshape(2, NH, 128).transpose(2, 0, 1).reshape(128, 16)
    pp[:, 16] = hg_norm_g.reshape(128)
    pp[:, 32:32 + 8 * TAPS] = conv_w.reshape(TAPS, 8, 128).transpose(2, 1, 0).reshape(128, 8 * TAPS)
    pp[:, 288:296] = conv_b.reshape(8, 128).T
    pp[:, 296:304] = conv_ln_g.reshape(8, 128).T
    pp[:, 304:312] = conv_ln_b.reshape(8, 128).T
    return pp


def make_in_maps(x, w_in, lb_param, hg_norm_g, w_hg_out, conv_w, conv_b, conv_ln_g, conv_ln_b,
                 w_conv_out, w_out, ln1_g, ln1_b, w_ffn_in, w_ffn_out, ln2_g, ln2_b, TS=512):
    f = lambda a: np.ascontiguousarray(np.asarray(a, dtype=np.float32))
    B = x.shape[0]
    pp = pack_params(f(lb_param), f(hg_norm_g)[0], f(conv_w)[0], f(conv_b)[0], f(conv_ln_g)[0], f(conv_ln_b)[0])
    lnp = np.ascontiguousarray(np.broadcast_to(
        np.stack([f(ln1_g)[0], f(ln1_b)[0], f(ln2_g)[0], f(ln2_b)[0]])[None], (128, 4, D)))
    shared = {
        "w_in": f(w_in)[0], "w_hg_out": f(w_hg_out)[0], "w_conv_out": f(w_conv_out)[0], "w_out": f(w_out)[0],
        "w_ffn_in": f(w_ffn_in)[0], "w_ffn_out": f(w_ffn_out)[0], "pp": pp, "lnp": lnp, "cst": make_consts(TS),
    }
    xs = f(x)
    return [dict(shared, x=xs[b]) for b in range(B)]


def kernel(**inputs):
    TS = 512
    in_maps = make_in_maps(TS=TS, **inputs)
    nc = build_nc(SEQ, TS=TS)
    res = run_bass_kernel_spmd(nc, in_maps, core_ids=list(range(N_CORES)))
    return np.stack([np.asarray(r["out"], dtype=np.float32) for r in res.results], axis=0)
```

```python
import numpy as np
from contextlib import ExitStack
import concourse.bass as bass
import concourse.mybir as mybir
from concourse.bass_utils import run_bass_kernel_spmd

F32 = mybir.dt.float32
BF16 = mybir.dt.bfloat16
AF = mybir.ActivationFunctionType
ALU = mybir.AluOpType

D = 1024
NH = 8
FFN = 2816
TAPS = 31
IN_W = 8192
ALPHA = 2.0 ** 0.25
LN_EPS = 1e-5
RMS_EPS = LN_EPS * 128.0
N_CORES = 8
SEQ = 4096


class Prog:
    ENGS = ("pe", "act", "dve", "pool", "sp")
    SAME_SYNC = ("act", "dve", "pool")

    def __init__(self):
        self.ops = []
        self.lastw = {}
        self.readers = {}
        self.dsem_count = {}
        self.epoch = 0

    def add(self, eng, fn, reads=(), writes=(), dsem=None, dur=None, lat=0.0, tbl=None):
        i = len(self.ops)
        deps = set()
        for r in reads:
            w = self.lastw.get(r)
            if w is not None:
                deps.add(w)
        for w_ in writes:
            lw = self.lastw.get(w_)
            if lw is not None:
                deps.add(lw)
            for rd in self.readers.get(w_, ()):
                deps.add(rd)
        for r in reads:
            self.readers.setdefault(r, []).append(i)
        for w_ in writes:
            self.lastw[w_] = i
            self.readers[w_] = []
        op = dict(eng=eng, fn=fn, deps=deps, dsem=dsem, sig=False, dval=None, ep=self.epoch,
                  dur=(0.3 if dur is None else dur), lat=lat, tbl=tbl)
        if dsem is not None:
            self.dsem_count[dsem] = self.dsem_count.get(dsem, 0) + 16
            op["dval"] = self.dsem_count[dsem]
        self.ops.append(op)
        return i

    def list_schedule(self, mode="blevel"):
        ops = self.ops
        n = len(ops)
        body = [i for i in range(n) if ops[i]["fn"] is not None]
        tailops = [i for i in range(n) if ops[i]["fn"] is None]
        succ = [[] for _ in range(n)]
        indeg = [0] * n
        for i in body:
            for d in ops[i]["deps"]:
                succ[d].append(i)
                indeg[i] += 1
        last_d = {}
        for i in body:
            k = ops[i]["dsem"]
            if k is not None:
                if k in last_d and last_d[k] not in ops[i]["deps"]:
                    succ[last_d[k]].append(i)
                    indeg[i] += 1
                last_d[k] = i
        blev = [0.0] * n
        for i in reversed(body):
            m = 0.0
            for j in succ[i]:
                if blev[j] > m:
                    m = blev[j]
            blev[i] = m + ops[i]["dur"] + ops[i]["lat"] + (DMA_BOOST if ops[i]["dsem"] is not None else 0.0)
        finish = [0.0] * n
        ready_t = [0.0] * n
        eng_free = {e: 0.0 for e in self.ENGS}
        act_tbl = [None]
        ready = {e: [] for e in self.ENGS}
        for i in body:
            if indeg[i] == 0:
                ready[ops[i]["eng"]].append(i)
        order = []
        left = len(body)
        while left:
            best = None
            for e in self.ENGS:
                lst = ready[e]
                if not lst:
                    continue
                t_e = max(eng_free[e], min(ready_t[i] for i in lst))
                if best is None or t_e < best[0]:
                    best = (t_e, e)
            t_e, e = best
            lst = ready[e]
            cands = [i for i in lst if ready_t[i] <= t_e + 1e-9]
            if mode == "blevel":
                if e == "act":
                    cur = act_tbl[0]
                    i = max(cands, key=lambda q: (blev[q] - (1.3 if (ops[q]["tbl"] not in (None, cur)) else 0.0), -q))
                else:
                    i = max(cands, key=lambda q: (blev[q], -q))
            else:
                i = min(cands)
            lst.remove(i)
            op = ops[i]
            dur = op["dur"]
            if e == "act" and op["tbl"] is not None and op["tbl"] != act_tbl[0]:
                dur += 1.3
                act_tbl[0] = op["tbl"]
            eng_free[e] = t_e + dur
            finish[i] = t_e + dur + op["lat"]
            order.append(i)
            left -= 1
            for j in succ[i]:
                indeg[j] -= 1
                lat = 0.0 if ops[j]["eng"] == e and op["dsem"] is None else XLAT
                if finish[i] + lat > ready_t[j]:
                    ready_t[j] = finish[i] + lat
                if indeg[j] == 0:
                    ready[ops[j]["eng"]].append(j)
        order += tailops
        remap = {old: new for new, old in enumerate(order)}
        newops = [ops[i] for i in order]
        for op in newops:
            op["deps"] = {remap[d] for d in op["deps"]}
        self.ops = newops
        self.est_makespan = max(finish) if finish else 0.0

    def finalize(self):
        ops = self.ops
        for op in ops:
            keep = set()
            for d in op["deps"]:
                dop = ops[d]
                if dop["dsem"] is not None:
                    keep.add(d)
                elif dop["eng"] != op["eng"] or op["eng"] in self.SAME_SYNC:
                    dop["sig"] = True
                    keep.add(d)
            op["deps"] = keep
        cnt = {}
        for op in ops:
            if op["dsem"] is None and op["sig"]:
                k = (op["eng"], op["ep"])
                cnt[k] = cnt.get(k, 0) + 1
                op["sval"] = cnt[k]
        self.max_sval = max(cnt.values()) if cnt else 0
        waited = {e: {} for e in self.ENGS}
        for op in ops:
            need = {}
            for d in op["deps"]:
                dop = ops[d]
                if dop["dsem"] is not None:
                    key, val = ("d", dop["dsem"]), dop["dval"]
                else:
                    key, val = ("e", (dop["eng"], dop["ep"])), dop["sval"]
                if val > need.get(key, 0):
                    need[key] = val
            w = waited[op["eng"]]
            waits = []
            for key, val in need.items():
                if val > w.get(key, 0):
                    w[key] = val
                    waits.append((key, val))
            op["waits"] = waits

    def emit(self, block, esems, dsems):
        handles = {"pe": block.tensor, "act": block.scalar, "dve": block.vector,
                   "pool": block.gpsimd, "sp": block.sync}
        for ename in self.ENGS:
            myops = [op for op in self.ops if op["eng"] == ename]
            if not myops:
                continue

            def body(eng, myops=myops, ename=ename):
                for op in myops:
                    for (kind, k), val in op["waits"]:
                        eng.wait_ge(dsems[k] if kind == "d" else esems[k], val)
                    if op["fn"] is None:
                        continue
                    inst = op["fn"](eng)
                    if op["dsem"] is not None:
                        inst.then_inc(dsems[op["dsem"]], 16)
                    elif op["sig"]:
                        inst.then_inc(esems[(ename, op["ep"])], 1)

            handles[ename](body)


LIST_SCHED = True
XLAT = 0.35
DMA_BOOST = 0.0
SCHED_MODE = "blevel"


def build_nc(S, TS=512, NSLOT=8, dbg=99):
    NST = S // TS
    NT = TS // 512
    NB = TS // 128
    NCH = TS // 64
    assert S % TS == 0 and TS % 512 == 0

    nc = bass.Bass("TRN2", target_bir_lowering=False)

    def din(name, shape):
        return nc.dram_tensor(name, shape, F32, kind="ExternalInput").ap()

    x_d = din("x", [S, D])
    w_in_d = din("w_in", [D, IN_W])
    w_hg_d = din("w_hg_out", [D, D])
    w_cv_d = din("w_conv_out", [D, D])
    w_out_d = din("w_out", [D, D])
    w_f1_d = din("w_ffn_in", [D, 2 * FFN])
    w_f2_d = din("w_ffn_out", [FFN, D])
    pp_d = din("pp", [128, PP_N])
    lnp_d = din("lnp", [128, 4, D])
    cst_d = din("cst", [128, CST_N(TS)])
    out_d = nc.dram_tensor("out", [S, D], F32, kind="ExternalOutput").ap()

    P = Prog()
    es = ExitStack()
    with es:
        def sb(name, shape, dt):
            return es.enter_context(nc.sbuf_tensor("sb_" + name, shape, dt))

        R = sb("R", [128, NB, D], F32)
        xTs = [sb(f"xT{i}", [128, 8, TS], BF16) for i in range(2)]
        on = sb("on", [128, 8, TS], BF16)
        shared = sb("shared", [128, 24 * TS], BF16)
        slots = [sb(f"slot{i}", [128, 8, 256], BF16) for i in range(NSLOT)]
        lnt = sb("lnt", [128, 4, D], F32)
        cs = sb("cs", [128, CST_N(TS)], F32)
        pp = sb("pp", [128, PP_N], F32)
        Fall = sb("Fall", [128, 8, TS], F32)
        Fb = [[Fall[:, p * 4 + i, :] for i in range(4)] for p in range(2)]
        Qpp = [sb(f"Qpp{p}", [128, TS], BF16) for p in range(2)]
        Kt = [sb(f"Kt{p}", [128, TS], BF16) for p in range(2)]
        ogs = [sb(f"ogs{p}", [128, TS], BF16) for p in range(2)]
        v_tm = [sb(f"v_tm{p}", [128, NB, 128], BF16) for p in range(2)]
        K_tm = [[sb(f"K_tm{p}_{i}", [128, NB, 128], BF16) for i in range(2)] for p in range(2)]
        KVs = [sb(f"KVs{p}", [128, NCH, 128], F32) for p in range(2)]
        Tb = [sb(f"Tb{p}", [128, NCH + 1, 128], BF16) for p in range(2)]
        ebuf = [sb(f"ebuf{p}", [128, NCH], F32) for p in range(2)]
        Am = [sb(f"Am{i}", [128, 4, 128], BF16) for i in range(2)]
        Tprev = sb("Tprev", [128, NH, 128], BF16)
        halo = sb("halo", [128, 8, 32], BF16)
        ubuf = [sb(f"ubuf{p}", [128, 32 + TS], BF16) for p in range(2)]
        Dg = [sb(f"Dg{p}", [128, TAPS, 128], BF16) for p in range(2)]
        scr = [sb(f"scr{i}", [128, 512], F32) for i in range(8)]
        scb = [sb(f"scb{i}", [128, 512], BF16) for i in range(4)]
        identB = sb("identB", [128, 128], BF16)
        ones128 = sb("ones128", [128, 128], BF16)
        ones1024 = sb("ones1024", [128, 128], BF16)
        lbt = sb("lbt", [128, 4, NH], F32)
        cwh = sb("cwh", [128, 8 * TAPS], F32)
        st6 = sb("st6", [128, 12], F32)
        mv = sb("mv", [128, 4], F32)
        cmean = sb("cmean", [128, 512], F32)
        crstd = sb("crstd", [128, 512], F32)
        psb = [es.enter_context(nc.psum_tensor(f"psb{i}", [128, 512], F32)) for i in range(8)]

        cpre = shared[:, 0:16 * TS].bitcast(F32).rearrange("p (j t) -> p j t", j=8)
        un = shared[:, 16 * TS:24 * TS].rearrange("p (j t) -> p j t", j=8)
        mixed = shared[:, 0:8 * TS].rearrange("p (j t) -> p j t", j=8)
        actb = shared[:, 0:22 * TS].rearrange("p (j t) -> p j t", j=22)

        def k_cpre(j, tt):
            b0 = (j * TS + tt * 512) * 4
            return [("sh", b0 // 1024), ("sh", b0 // 1024 + 1)]

        def k_bf(base_seg_elems, j, tt):
            b0 = (base_seg_elems + j * TS + tt * 512) * 2
            return [("sh", b0 // 1024)]

        def k_un(j, tt):
            return k_bf(16 * TS, j, tt)

        def k_mixed(j, tt):
            return k_bf(0, j, tt)

        def k_act(j, tt):
            return k_bf(0, j, tt)

        identF = cs[:, 0:128]
        mask2 = cs[:, 128:256]
        c_eps_rms = cs[:, 256:257]
        c_eps_ln = cs[:, 257:258]
        c_eps_ln4 = cs[:, 258:259]
        scanmask = cs[:, 320:320 + TS]
        lbp = pp[:, 0:16].rearrange("p (a h) -> p a h", a=2)
        gn = pp[:, 16:17]
        cw = pp[:, 32:32 + 8 * TAPS].rearrange("p (j t) -> p j t", j=8)
        cb = pp[:, 288:296]
        cg = pp[:, 296:304]
        cbb = pp[:, 304:312]

        esems = {(e, ep): es.enter_context(nc.semaphore(f"s_{e}_{ep}"))
                 for e in ("pe", "act", "dve", "pool") for ep in range(NST + 1)}
        dnames = ([f"ws{i}" for i in range(NSLOT)] + [f"r{i}" for i in range(NB)]
                  + [f"o{i}" for i in range(NB)] + [f"xs{i}" for i in range(NB)] + ["c0", "c1", "c2"])
        dsems = {d: es.enter_context(nc.semaphore("d_" + d)) for d in dnames}
        print('sbuf bytes remaining', nc.sbuf_bytes_remaining)
        block = es.enter_context(nc.Block())

        def fsz(ap):
            n = 1
            for d in ap.shape[1:]:
                n *= d
            return n

        def mm(out, lhsT, rhs, start, stop, reads, writes):
            P.add("pe", lambda e: e.matmul(out, lhsT=lhsT, rhs=rhs, start=start, stop=stop), reads, writes,
                  dur=max(64, fsz(rhs)) / 2300.0 + 0.01)

        def tr(out, in_, ident, reads, writes):
            P.add("pe", lambda e: e.transpose(out, in_, ident), reads, writes, dur=0.09)

        def act(out, in_, func, reads, writes, scale=None, bias=None):
            kw = {}
            if scale is not None:
                kw["scale"] = scale
            if bias is not None:
                kw["bias"] = bias
            tbl = {AF.Tanh: "A", AF.Silu: "A", AF.Ln: "B", AF.Sigmoid: "C"}.get(func)
            P.add("act", lambda e: e.activation(out=out, in_=in_, func=func, **kw), reads, writes,
                  dur=0.17 + fsz(out) / 1200.0 + (0.19 if (scale is not None and not isinstance(scale, float)) or
                                                     (bias is not None and not isinstance(bias, float)) else 0.0),
                  tbl=tbl)

        def tt_(out, in0, in1, op, reads, writes, eng="dve"):
            P.add(eng, lambda e: e.tensor_tensor(out=out, in0=in0, in1=in1, op=op), reads, writes,
                  dur=0.22 + fsz(out) / 960.0)

        def ts_(out, in0, s1, s2, op0, op1, reads, writes, eng="dve"):
            P.add(eng, lambda e: e.tensor_scalar(out=out, in0=in0, scalar1=s1, scalar2=s2, op0=op0, op1=op1),
                  reads, writes, dur=0.22 + fsz(out) / 960.0)

        def stt(out, in0, scalar, in1, op0, op1, reads, writes):
            P.add("dve", lambda e: e.scalar_tensor_tensor(out=out, in0=in0, scalar=scalar, in1=in1,
                                                          op0=op0, op1=op1), reads, writes,
                  dur=0.22 + fsz(out) / 960.0)

        def cp(out, in_, reads, writes, eng="dve"):
            if eng == "act":
                P.add("act", lambda e: e.activation(out=out, in_=in_, func=AF.Copy), reads, writes,
                      dur=0.17 + fsz(out) / 1200.0)
            else:
                P.add(eng, lambda e: e.tensor_copy(out=out, in_=in_), reads, writes, dur=0.22 + fsz(out) / 960.0)

        def dma(eng, out, in_, reads, writes, dsem):
            nbytes = 128 * fsz(out) * 4
            P.add(eng, lambda e: e.dma_start(out=out, in_=in_), reads, writes, dsem=dsem,
                  dur=0.1, lat=2.0 + nbytes / 150000.0)

        st = {"ps": 0, "scr": 0, "scb": 0, "slot": 0, "alt": 0, "am": 0}

        ps_free = list(range(6))

        def newps():
            assert ps_free, "out of PSUM banks"
            i = ps_free.pop(0)
            return psb[i], ("ps", i)

        def relps(*keys):
            for k in keys:
                assert k[1] not in ps_free
                ps_free.append(k[1])

        def newscr():
            i = st["scr"]; st["scr"] = (i + 1) % 8
            return scr[i], ("scr", i)

        def newscb():
            i = st["scb"]; st["scb"] = (i + 1) % 4
            return scb[i], ("scb", i)

        def wblock(wd, k0, KC, n0):
            i = st["slot"]; st["slot"] = (i + 1) % NSLOT
            key = ("slot", i)
            src = wd[k0:k0 + KC * 128, n0:n0 + 256].rearrange("(kc p) n -> p kc n", p=128)
            dma("pool", slots[i][:, 0:KC, :], src, [], [key], f"ws{i}")
            return slots[i], key

        def alt_eng():
            st["alt"] ^= 1
            return "act" if st["alt"] else "dve"

        dma("sp", cs[:], cst_d, [], ["cs"], "c0")
        dma("sp", pp[:], pp_d, [], ["pp"], "c1")
        dma("sp", lnt[:], lnp_d, [], ["lnt"], "c2")
        cp(identB[:], identF, ["cs"], ["identB"])
        P.add("dve", lambda e: e.memset(scr[0][:], 0.0), [], [("scr", 0)])
        P.add("dve", lambda e: e.memset(scr[1][:], 1.0 / 128.0), [], [("scr", 1)])
        P.add("dve", lambda e: e.memset(scr[2][:], 1.0 / 1024.0), [], [("scr", 2)])
        cp(ones128[:], scr[1][:, 0:128], [("scr", 1)], ["ones128"])
        cp(ones1024[:], scr[2][:, 0:128], [("scr", 2)], ["ones1024"])
        Tpf = Tprev[:].rearrange("p h d -> p (h d)")
        cp(Tpf[:, 0:512], scr[0][:], [("scr", 0)], [("Tprev", h) for h in range(4)])
        cp(Tpf[:, 512:1024], scr[0][:], [("scr", 0)], [("Tprev", h) for h in range(4, 8)])
        cp(halo[:].rearrange("p j t -> p (j t)"), scr[0][:, 0:256], [("scr", 0)], [("halo", j) for j in range(8)])
        for p_ in range(2):
            for i in range(2):
                for tb in range(0, NB, 4):
                    cp(K_tm[p_][i][:, tb:tb + 4, :].rearrange("p a b -> p (a b)"), scr[0][:], [("scr", 0)],
                       [("Ktm", p_, tb // 4, 0), ("Ktm", p_, tb // 4, 1)])
        tt_(lbt[:, 3, :], lbp[:, 0, :], lbp[:, 1, :], ALU.subtract, ["pp"], ["lbt3"])
        act(lbt[:, 0, :], lbt[:, 3, :], AF.Sigmoid, ["lbt3"], ["lbt0"])
        ts_(lbt[:, 1, :], lbt[:, 0, :], -0.5, 0.5, ALU.mult, ALU.add, ["lbt0"], ["lbt1"])
        ts_(lbt[:, 2, :], lbt[:, 0, :], 0.5, -0.5, ALU.mult, ALU.add, ["lbt0"], ["lbt2"])
        ts_(lbt[:, 3, :], lbt[:, 0, :], 0.5, 0.5, ALU.mult, ALU.add, ["lbt0"], ["lbt3b"])
        ts_(cwh[:], pp[:, 32:32 + 8 * TAPS], 0.5, None, ALU.mult, ALU.bypass, ["pp"], ["cwh"])
        LB = ["lbt3b", "lbt1", "lbt2"]

        def transpose_blk(src, src_keys, tb, xb):
            for g in range(2):
                ps, pk = newps()
                for kk in range(4):
                    kc = g * 4 + kk
                    tr(ps[:, kk * 128:(kk + 1) * 128], src[:, kc * 128:(kc + 1) * 128], identF,
                       list(src_keys) + ["cs"], [pk])
                cp(xTs[xb][:, g * 4:(g + 1) * 4, tb * 128:(tb + 1) * 128],
                   ps[:].rearrange("p (a b) -> p a b", a=4), [pk], [("xT", xb, tb, g)], eng=alt_eng())
                relps(pk)

        def xT_keys(tt):
            return [("xT", st["xb"], tb, g) for tb in range(tt * 4, tt * 4 + 4) for g in range(2)]

        def xstage(tb):
            ap = Fall[:, 2 * tb:2 * tb + 2, :].rearrange("p a t -> p (a t)")
            keys = [(f"F{idx % 4}", idx // 4, 0) for idx in (2 * tb, 2 * tb + 1)]
            return ap, keys

        def prefetch_x_load(sti_):
            for tb in range(NB):
                ap, keys = xstage(tb)
                dma("sp", ap, x_d[sti_ * TS + tb * 128:sti_ * TS + (tb + 1) * 128, :], [], keys, f"xs{tb}")

        def prefetch_x_transpose(sti_):
            for tb in range(NB):
                ap, keys = xstage(tb)
                transpose_blk(ap, keys, tb, sti_ % 2)

        def layer_norm_R(tb, gi, eps_ap=None):
            eps_ap = c_eps_ln if eps_ap is None else eps_ap
            rk = ("R", tb)
            P.add("dve", lambda e: e.bn_stats(out=st6[:, 0:6], in_=R[:, tb, 0:512]), [rk], ["st6a"])
            P.add("dve", lambda e: e.bn_stats(out=st6[:, 6:12], in_=R[:, tb, 512:1024]), [rk], ["st6b"])
            P.add("dve", lambda e: e.bn_aggr(out=mv[:, 0:2], in_=st6[:]), ["st6a", "st6b"], ["mv01"])
            act(mv[:, 2:3], mv[:, 1:2], AF.Ln, ["mv01", "cs"], ["mv2"], bias=eps_ap)
            act(mv[:, 3:4], mv[:, 2:3], AF.Exp, ["mv2"], ["mv3"], scale=-0.5)
            ts_(R[:, tb, :], R[:, tb, :], mv[:, 0:1], mv[:, 3:4], ALU.subtract, ALU.mult,
                [rk, "mv01", "mv3"], [rk])
            tt_(R[:, tb, :], R[:, tb, :], lnt[:, gi, :], ALU.mult, [rk, "lnt"], [rk])
            tt_(R[:, tb, :], R[:, tb, :], lnt[:, gi + 1, :], ALU.add, [rk, "lnt"], [rk])

        for sti in range(NST):
            t0 = sti * TS
            P.epoch = sti + 1
            st["xb"] = sti % 2
            xT = xTs[sti % 2]
            if sti == 0:
                prefetch_x_load(0)
                prefetch_x_transpose(0)
            for tb in range(NB):
                dma("sp", R[:, tb, :], x_d[t0 + tb * 128:t0 + (tb + 1) * 128, :], [], [("R", tb)], f"r{tb}")

            wst = {}

            def head_front(h):
                par = h % 2
                if h % 2 == 0:
                    hp = h // 2
                    for nm, sec in (("q", 0), ("f", 1), ("i", 2), ("o", 3)):
                        wst[nm] = wblock(w_in_d, 0, 8, sec * 1024 + hp * 256)
                (wq, kq), (wf, kf), (wi, ki), (wo, ko) = wst["q"], wst["f"], wst["i"], wst["o"]
                hc = (h % 2) * 128
                F0, F1, F2, F3 = Fb[par]
                allF = lambda n: [(n, par, tt) for tt in range(NT)]
                for tt in range(NT):
                    tsl = slice(tt * 512, (tt + 1) * 512)
                    for (wblk, wk, dst, dk_, fn) in ((wq, kq, F0, ("F0", par, tt), AF.Silu),
                                                     (wf, kf, F1, ("F1", par, tt), AF.Tanh),
                                                     (wo, ko, ogs[par], ("ogs", par, tt), AF.Silu)):
                        ps, pk = newps()
                        for kc in range(8):
                            mm(ps[:], wblk[:, kc, hc:hc + 128], xT[:, kc, tsl], kc == 0, kc == 7,
                               [wk] + xT_keys(tt), [pk])
                        act(dst[:, tsl], ps[:], fn, [pk], [dk_], scale=(0.5 if fn == AF.Tanh else None))
                        relps(pk)
                        yield
                for tg in range(NB // 4):
                    ps, pk = newps()
                    for tl in range(4):
                        tb = tg * 4 + tl
                        for kc in range(8):
                            mm(ps[:, tl * 128:(tl + 1) * 128], xT[:, kc, tb * 128:(tb + 1) * 128],
                               wi[:, kc, hc:hc + 128], kc == 0, kc == 7,
                               [ki, ("xT", st["xb"], tb, 0), ("xT", st["xb"], tb, 1)], [pk])
                    cp(v_tm[par][:, tg * 4:(tg + 1) * 4, :], ps[:].rearrange("p (a b) -> p a b", a=4),
                       [pk], [("vtm", par, tg)])
                    relps(pk)
                    yield
                act(F2[:], F1[:], AF.Ln, allF("F1") + LB, allF("F2"),
                    scale=lbt[:, 1, h:h + 1], bias=lbt[:, 3, h:h + 1])
                ts_(F3[:], F1[:], lbt[:, 2, h:h + 1], lbt[:, 1, h:h + 1], ALU.mult, ALU.add,
                    allF("F1") + LB, allF("F3"))
                yield
                P.add("dve", lambda e, F1=F1, F2=F2: e.tensor_tensor_scan(
                    out=F1[:], data0=scanmask, data1=F2[:], initial=0.0, op0=ALU.mult, op1=ALU.add),
                    allF("F2") + ["cs"], allF("F1"), dur=0.1 + 2 * TS / 960.0)
                yield
                act(F2[:], F1[:], AF.Exp, allF("F1"), allF("F2"))
                yield
                cp(ebuf[par][:], F2[:].rearrange("p (c t) -> p c t", t=64)[:, :, 63], allF("F2"), [("ebuf", par)])
                tt_(Qpp[par][:], F0[:], F2[:], ALU.mult, allF("F0") + allF("F2"), [("Qpp", par)])
                yield
                act(F0[:], F1[:], AF.Exp, allF("F1"), allF("F0"), scale=-1.0)
                yield
                tt_(Kt[par][:], F3[:], F0[:], ALU.mult, allF("F3") + allF("F0"), [("Kt", par)])
                yield

            def head_back(h):
                par = h % 2
                Ktp, Qp, vt, Tbp, eb, KV = Kt[par], Qpp[par], v_tm[par], Tb[par], ebuf[par], KVs[par]
                for tg in range(NB // 4):
                    ps, pk = newps()
                    psv = ps[:].bitcast(BF16)
                    for tl in range(4):
                        tb = tg * 4 + tl
                        tr(psv[:, tl * 128:(tl + 1) * 128], Ktp[:, tb * 128:(tb + 1) * 128], identB[:],
                           [("Kt", par), "identB"], [pk])
                    for hf in range(2):
                        cp(K_tm[par][hf][hf * 64:hf * 64 + 64, tg * 4:(tg + 1) * 4, :],
                           psv[hf * 64:hf * 64 + 64, 0:512].rearrange("p (a b) -> p a b", a=4),
                           [pk], [("Ktm", par, tg, hf)], eng=("act" if hf == 0 else "dve"))
                    relps(pk)
                    yield
                for cg_ in range(NCH // 4):
                    ps, pk = newps()
                    for cl in range(4):
                        c = cg_ * 4 + cl
                        tb, hf = c // 2, c % 2
                        mm(ps[:, cl * 128:(cl + 1) * 128], K_tm[par][hf][:, tb, :], vt[:, tb, :], True, True,
                           [("Ktm", par, tb // 4, hf), ("vtm", par, tb // 4)], [pk])
                    tt_(KV[:, cg_ * 4:(cg_ + 1) * 4, :], ps[:].rearrange("p (a b) -> p a b", a=4),
                        eb[:, cg_ * 4:(cg_ + 1) * 4].unsqueeze(2).broadcast_to([128, 4, 128]), ALU.mult,
                        [pk, ("ebuf", par)], [("KVs", par, cg_)])
                    relps(pk)
                    yield
                cp(Tbp[:, 0, :], Tprev[:, h, :], [("Tprev", h)], [("Tb", par, 0)])
                for c in range(NCH):
                    stt(Tbp[:, c + 1, :], Tbp[:, c, :], eb[:, c:c + 1], KV[:, c, :], ALU.mult, ALU.add,
                        [("Tb", par, c), ("ebuf", par), ("KVs", par, c // 4)], [("Tb", par, c + 1)])
                    if c % 2 == 1:
                        yield
                cp(Tprev[:, h, :], Tbp[:, NCH, :], [("Tb", par, NCH)], [("Tprev", h)], eng="act")
                yield
                yield
                for tt in range(NT):
                    tsl = slice(tt * 512, (tt + 1) * 512)
                    pA, pAk = newps()
                    for tbl in range(4):
                        tb = tt * 4 + tbl
                        mm(pA[:, tbl * 128:(tbl + 1) * 128], Ktp[:, tb * 128:(tb + 1) * 128],
                           Qp[:, tb * 128:(tb + 1) * 128], True, True, [("Kt", par), ("Qpp", par)], [pAk])
                    st["am"] ^= 1
                    am = Am[st["am"]]
                    amk = ("Am", st["am"])
                    tt_(am[:], pA[:].rearrange("p (a b) -> p a b", a=4),
                        mask2.unsqueeze(1).broadcast_to([128, 4, 128]), ALU.mult, [pAk, "cs"], [amk])
                    relps(pAk)
                    yield
                    pO, pOk = newps()
                    for tbl in range(4):
                        tb = tt * 4 + tbl
                        mm(pO[:, tbl * 128:(tbl + 1) * 128], vt[:, tb, :], am[:, tbl, :], True, False,
                           [("vtm", par, tb // 4), amk], [pOk])
                        for hf in range(2):
                            c = 2 * tb + hf
                            mm(pO[:, tbl * 128 + hf * 64:tbl * 128 + hf * 64 + 64], Tbp[:, c, :],
                               Qp[:, c * 64:(c + 1) * 64], False, hf == 1, [("Tb", par, c), ("Qpp", par)], [pOk])
                    osq, osk = newscb()
                    act(osq[:], pO[:], AF.Square, [pOk], [osk])
                    yield
                    pM, pMk = newps()
                    mm(pM[:], ones128[:], osq[:], True, True, ["ones128", osk], [pMk])
                    lnv, lnk = newscr()
                    act(lnv[:], pM[:], AF.Ln, [pMk, "cs"], [lnk], bias=c_eps_rms)
                    relps(pMk)
                    rstd, rsk = newscr()
                    act(rstd[:], lnv[:], AF.Exp, [lnk], [rsk], scale=-0.5)
                    t1, t1k = newscr()
                    stt(t1[:], pO[:], gn, rstd[:], ALU.mult, ALU.mult, [pOk, "pp", rsk], [t1k])
                    relps(pOk)
                    tt_(on[:, h, tsl], t1[:], ogs[par][:, tsl], ALU.mult, [t1k, ("ogs", par, tt)], [("on", h, tt)])
                    yield

            def conv_front(j):
                par = j % 2
                if j % 2 == 0:
                    wst["gv"] = wblock(w_in_d, 0, 8, 4096 + (j // 2) * 256)
                    wst["gg"] = wblock(w_in_d, 0, 8, 5120 + (j // 2) * 256)
                (wgv, kgv), (wgg, kgg) = wst["gv"], wst["gg"]
                jc = (j % 2) * 128
                ub = ubuf[par]
                cp(ub[:, 0:32], halo[:, j, :], [("halo", j)], [("ubuf_h", par)], eng="act")
                for tt in range(NT):
                    psv_, pvk = newps()
                    psg, pgk = newps()
                    tsl = slice(tt * 512, (tt + 1) * 512)
                    for kc in range(8):
                        mm(psv_[:], wgv[:, kc, jc:jc + 128], xT[:, kc, tsl], kc == 0, kc == 7,
                           [kgv] + xT_keys(tt), [pvk])
                        if kc % 2 == 1:
                            yield
                    for kc in range(8):
                        mm(psg[:], wgg[:, kc, jc:jc + 128], xT[:, kc, tsl], kc == 0, kc == 7,
                           [kgg] + xT_keys(tt), [pgk])
                        if kc % 2 == 1:
                            yield
                    sg_, sgk = newscr()
                    act(sg_[:], psg[:], AF.Tanh, [pgk], [sgk], scale=0.5)
                    stt(ub[:, 32 + tt * 512:32 + (tt + 1) * 512], sg_[:], 1.0, psv_[:], ALU.add, ALU.mult,
                        [pvk, sgk], [("ubuf", par, tt)])
                    relps(pvk, pgk)
                cp(halo[:, j, :], ub[:, TS:TS + 32], [("ubuf", par, NT - 1)], [("halo", j)], eng="act")
                for t0_, t1_ in ((0, 8), (8, 16), (16, 24), (24, TAPS)):
                    nt_ = t1_ - t0_
                    tt_(Dg[par][:, t0_:t1_, :], identB[:].unsqueeze(1).broadcast_to([128, nt_, 128]),
                        cwh[:, j * TAPS + t0_:j * TAPS + t1_].unsqueeze(2).broadcast_to([128, nt_, 128]), ALU.mult,
                        ["identB", "cwh"], [("Dg", par, t0_ // 8)])
                    yield

            def conv_back(j):
                par = j % 2
                ub = ubuf[par]
                for tt in range(NT):
                    pc, pck = psb[6 + par], ("ps", 6 + par)
                    ur = [("ubuf_h", par)] + [("ubuf", par, t_) for t_ in range(tt + 1)]
                    for tap in range(TAPS):
                        off = 2 + tt * 512 + tap
                        mm(pc[:], Dg[par][:, tap, :], ub[:, off:off + 512], tap == 0, tap == TAPS - 1,
                           [("Dg", par, tap // 8)] + ur, [pck])
                        if tap % 3 == 2:
                            yield
                    act(cpre[:, j, tt * 512:(tt + 1) * 512], pc[:], AF.Identity, [pck, "pp"], k_cpre(j, tt),
                        bias=cb[:, j:j + 1])
                    yield

            def merged(gens):
                gens = list(gens)
                while gens:
                    for g in list(gens):
                        try:
                            next(g)
                        except StopIteration:
                            gens.remove(g)
                            continue
                        yield

            def thread(front, back, n):
                yield from front(0)
                for i in range(n):
                    gs = [back(i)]
                    if i + 1 < n:
                        gs.append(front(i + 1))
                    yield from merged(gs)

            threads = []
            if dbg >= 3:
                threads.append(thread(head_front, head_back, NH))
            if dbg >= 4:
                threads.append(thread(conv_front, conv_back, 8))
            while threads:
                for th in list(threads):
                    try:
                        next(th)
                    except StopIteration:
                        threads.remove(th)

            for tt in range(NT if dbg >= 4 else 0):
                tsl = slice(tt * 512, (tt + 1) * 512)
                pS1, pS1k = newps()
                pS2, pS2k = newps()
                for j in range(8):
                    cbf, cbk = newscb()
                    csq, csk = newscb()
                    cp(cbf[:], cpre[:, j, tsl], k_cpre(j, tt), [cbk])
                    act(csq[:], cpre[:, j, tsl], AF.Square, k_cpre(j, tt), [csk])
                    mm(pS1[:], ones1024[:], cbf[:], j == 0, j == 7, ["ones1024", cbk], [pS1k])
                    mm(pS2[:], ones1024[:], csq[:], j == 0, j == 7, ["ones1024", csk], [pS2k])
                mean, mk_ = cmean, "cmean"
                cp(mean[:], pS1[:], [pS1k], [mk_], eng="act")
                relps(pS1k)
                msq, msk = newscr()
                tt_(msq[:], mean[:], mean[:], ALU.mult, [mk_], [msk])
                var, vk = newscr()
                tt_(var[:], pS2[:], msq[:], ALU.subtract, [pS2k, msk], [vk])
                relps(pS2k)
                lnv, lnk = newscr()
                act(lnv[:], var[:], AF.Ln, [vk, "cs"], [lnk], bias=c_eps_ln)
                rstd, rsk = crstd, "crstd"
                act(rstd[:], lnv[:], AF.Exp, [lnk], [rsk], scale=-0.5)
                for j in range(8):
                    ta, tak = newscr()
                    tt_(ta[:], cpre[:, j, tsl], mean[:], ALU.subtract, k_cpre(j, tt) + [mk_], [tak])
                    t2, t2k = newscr()
                    tt_(t2[:], ta[:], rstd[:], ALU.mult, [tak, rsk], [t2k])
                    act(un[:, j, tsl], t2[:], AF.Silu, [t2k, "pp"], k_un(j, tt),
                        scale=cg[:, j:j + 1], bias=cbb[:, j:j + 1])

            for j in range(8 if dbg >= 5 else 0):
                if j % 2 == 0:
                    wha, kha = wblock(w_hg_d, 0, 8, (j // 2) * 256)
                    wcb, kcb = wblock(w_cv_d, 0, 8, (j // 2) * 256)
                    wga, kga = wblock(w_in_d, 0, 8, 6144 + (j // 2) * 256)
                    wgb, kgb = wblock(w_in_d, 0, 8, 7168 + (j // 2) * 256)
                jc = (j % 2) * 128
                for tt in range(NT):
                    tsl = slice(tt * 512, (tt + 1) * 512)
                    pya, pyak = newps()
                    pyb, pybk = newps()
                    pga, pgak = newps()
                    pgb, pgbk = newps()
                    for kc in range(8):
                        mm(pga[:], wga[:, kc, jc:jc + 128], xT[:, kc, tsl], kc == 0, kc == 7,
                           [kga] + xT_keys(tt), [pgak])
                    for kc in range(8):
                        mm(pgb[:], wgb[:, kc, jc:jc + 128], xT[:, kc, tsl], kc == 0, kc == 7,
                           [kgb] + xT_keys(tt), [pgbk])
                    for kc in range(8):
                        mm(pya[:], wha[:, kc, jc:jc + 128], on[:, kc, tsl], kc == 0, kc == 7,
                           [kha, ("on", kc, tt)], [pyak])
                    for kc in range(8):
                        mm(pyb[:], wcb[:, kc, jc:jc + 128], un[:, kc, tsl], kc == 0, kc == 7,
                           [kcb] + k_un(kc, tt), [pybk])
                    sa, sak = newscr()
                    act(sa[:], pga[:], AF.Tanh, [pgak], [sak], scale=0.5)
                    sb_, sbk = newscr()
                    act(sb_[:], pgb[:], AF.Tanh, [pgbk], [sbk], scale=0.5)
                    m1, m1k = newscr()
                    stt(m1[:], sa[:], 1.0, pya[:], ALU.add, ALU.mult, [pyak, sak], [m1k])
                    m2, m2k = newscr()
                    stt(m2[:], sb_[:], 1.0, pyb[:], ALU.add, ALU.mult, [pybk, sbk], [m2k])
                    relps(pyak, pybk, pgak, pgbk)
                    tt_(mixed[:, j, tsl], m1[:], m2[:], ALU.add, [m1k, m2k], k_mixed(j, tt))

            wob = [wblock(w_out_d, 0, 8, nb * 256) for nb in range(4)] if dbg >= 6 else []
            for tb in range(NB if dbg >= 6 else 0):
                tt = tb // 4
                for half in range(2):
                    ps, pk = newps()
                    for q in range(2):
                        wblk, wk = wob[half * 2 + q]
                        for kc in range(8):
                            mm(ps[:, q * 256:(q + 1) * 256], mixed[:, kc, tb * 128:(tb + 1) * 128], wblk[:, kc, :],
                               kc == 0, kc == 7, [wk] + k_mixed(kc, tt), [pk])
                    stt(R[:, tb, half * 512:(half + 1) * 512], R[:, tb, half * 512:(half + 1) * 512], 2.0 * ALPHA, ps[:],
                        ALU.mult, ALU.add, [("R", tb), pk], [("R", tb)])
                    relps(pk)
            for tb in range(NB if dbg >= 6 else 0):
                layer_norm_R(tb, 0, c_eps_ln4)
            for tb in range(NB if dbg >= 6 else 0):
                transpose_blk(R[:, tb, :], [("R", tb)], tb, sti % 2)

            if sti + 1 < NST:
                prefetch_x_load(sti + 1)
            for jj in range(22 if dbg >= 8 else 0):
                if jj == 12 and sti + 1 < NST:
                    prefetch_x_transpose(sti + 1)
                if jj % 2 == 0:
                    wg_, kg_ = wblock(w_f1_d, 0, 8, (jj // 2) * 256)
                    wu_, ku_ = wblock(w_f1_d, 0, 8, FFN + (jj // 2) * 256)
                jc = (jj % 2) * 128
                for tt in range(NT):
                    tsl = slice(tt * 512, (tt + 1) * 512)
                    pg_, pgk_ = newps()
                    pu_, puk_ = newps()
                    for kc in range(8):
                        mm(pg_[:], wg_[:, kc, jc:jc + 128], xT[:, kc, tsl], kc == 0, kc == 7,
                           [kg_] + xT_keys(tt), [pgk_])
                    for kc in range(8):
                        mm(pu_[:], wu_[:, kc, jc:jc + 128], xT[:, kc, tsl], kc == 0, kc == 7,
                           [ku_] + xT_keys(tt), [puk_])
                    sg_, sgk = newscr()
                    act(sg_[:], pg_[:], AF.Silu, [pgk_], [sgk])
                    tt_(actb[:, jj, tsl], sg_[:], pu_[:], ALU.mult, [sgk, puk_], k_act(jj, tt))
                    relps(pgk_, puk_)

            for nb in range(4 if dbg >= 9 else 0):
                wfb = [wblock(w_f2_d, g * 1024, (8 if g < 2 else 6), nb * 256) for g in range(3)]
                for tb in range(NB):
                    tt = tb // 4
                    ps, pk = newps()
                    for kc in range(22):
                        wblk, wk = wfb[kc // 8]
                        mm(ps[:, 0:256], actb[:, kc, tb * 128:(tb + 1) * 128], wblk[:, kc % 8, :],
                           kc == 0, kc == 21, [wk] + k_act(kc, tt), [pk])
                    stt(R[:, tb, nb * 256:(nb + 1) * 256], R[:, tb, nb * 256:(nb + 1) * 256], ALPHA, ps[:, 0:256],
                        ALU.mult, ALU.add, [("R", tb), pk], [("R", tb)])
                    relps(pk)
                    if nb == 3:
                        layer_norm_R(tb, 2)
                        dma("sp", out_d[t0 + tb * 128:t0 + (tb + 1) * 128, :], R[:, tb, :], [("R", tb)], [], f"o{tb}")
            if dbg < 9:
                for tb in range(NB):
                    dma("sp", out_d[t0 + tb * 128:t0 + (tb + 1) * 128, :], R[:, tb, :], [("R", tb)], [], f"o{tb}")

        P.add("sp", None)
        fin = P.ops[-1]
        if LIST_SCHED:
            P.list_schedule(SCHED_MODE)
            print("list schedule: estimated makespan %.0f us" % P.est_makespan)
        P.finalize()
        fin["waits"] = [(("d", f"o{tb}"), P.dsem_count[f"o{tb}"]) for tb in range(NB)]
        P.emit(block, esems, dsems)
    return nc


PP_N = 320


def CST_N(TS):
    return 320 + TS


def make_consts(TS):
    c = np.zeros((128, CST_N(TS)), np.float32)
    c[:, 0:128] = np.eye(128, dtype=np.float32)
    p = np.arange(128)[:, None]
    t = np.arange(128)[None, :]
    c[:, 128:256] = ((p // 64 == t // 64) & (p % 64 <= t % 64)).astype(np.float32)
    c[:, 256] = RMS_EPS
    c[:, 257] = LN_EPS
    c[:, 258] = 4.0 * LN_EPS
    m = np.ones(TS, np.float32)
    m[::64] = 0.0
    c[:, 320:320 + TS] = m
    return c


def pack_params(lb_param, hg_norm_g, conv_w, conv_b, conv_ln_g, conv_ln_b):
    pp = np.zeros((128, PP_N), np.float32)
    pp[:, 0:16] = lb_param.reshape(2, NH, 128).transpose(2, 0, 1).reshape(128, 16)
    pp[:, 16] = hg_norm_g.reshape(128)
    pp[:, 32:32 + 8 * TAPS] = conv_w.reshape(TAPS, 8, 128).transpose(2, 1, 0).reshape(128, 8 * TAPS)
    pp[:, 288:296] = conv_b.reshape(8, 128).T
    pp[:, 296:304] = conv_ln_g.reshape(8, 128).T
    pp[:, 304:312] = conv_ln_b.reshape(8, 128).T
    return pp


def make_in_maps(x, w_in, lb_param, hg_norm_g, w_hg_out, conv_w, conv_b, conv_ln_g, conv_ln_b,
                 w_conv_out, w_out, ln1_g, ln1_b, w_ffn_in, w_ffn_out, ln2_g, ln2_b, TS=512):
    f = lambda a: np.ascontiguousarray(np.asarray(a, dtype=np.float32))
    B = x.shape[0]
    pp = pack_params(f(lb_param), f(hg_norm_g)[0], f(conv_w)[0], f(conv_b)[0], f(conv_ln_g)[0], f(conv_ln_b)[0])
    lnp = np.ascontiguousarray(np.broadcast_to(
        np.stack([f(ln1_g)[0], f(ln1_b)[0], f(ln2_g)[0], f(ln2_b)[0]])[None], (128, 4, D)))
    shared = {
        "w_in": f(w_in)[0], "w_hg_out": f(w_hg_out)[0], "w_conv_out": f(w_conv_out)[0], "w_out": f(w_out)[0],
        "w_ffn_in": f(w_ffn_in)[0], "w_ffn_out": f(w_ffn_out)[0], "pp": pp, "lnp": lnp, "cst": make_consts(TS),
    }
    xs = f(x)
    return [dict(shared, x=xs[b]) for b in range(B)]


def kernel(**inputs):
    TS = 512
    in_maps = make_in_maps(TS=TS, **inputs)
    nc = build_nc(SEQ, TS=TS)
    res = run_bass_kernel_spmd(nc, in_maps, core_ids=list(range(N_CORES)))
    return np.stack([np.asarray(r["out"], dtype=np.float32) for r in res.results], axis=0)
```

```python
import numpy as np
from contextlib import ExitStack
import concourse.bass as bass
import concourse.mybir as mybir
from concourse.bass_utils import run_bass_kernel_spmd

F32 = mybir.dt.float32
BF16 = mybir.dt.bfloat16
AF = mybir.ActivationFunctionType
ALU = mybir.AluOpType

D = 1024
NH = 8
FFN = 2816
TAPS = 31
IN_W = 8192
ALPHA = 2.0 ** 0.25
LN_EPS = 1e-5
RMS_EPS = LN_EPS * 128.0
N_CORES = 8
SEQ = 4096


class Prog:
    ENGS = ("pe", "act", "dve", "pool", "sp")
    SAME_SYNC = ("act", "dve", "pool")

    def __init__(self):
        self.ops = []
        self.lastw = {}
        self.readers = {}
        self.dsem_count = {}
        self.epoch = 0

    def add(self, eng, fn, reads=(), writes=(), dsem=None, dur=None, lat=0.0, tbl=None):
        i = len(self.ops)
        deps = set()
        for r in reads:
            w = self.lastw.get(r)
            if w is not None:
                deps.add(w)
        for w_ in writes:
            lw = self.lastw.get(w_)
            if lw is not None:
                deps.add(lw)
            for rd in self.readers.get(w_, ()):
                deps.add(rd)
        for r in reads:
            self.readers.setdefault(r, []).append(i)
        for w_ in writes:
            self.lastw[w_] = i
            self.readers[w_] = []
        op = dict(eng=eng, fn=fn, deps=deps, dsem=dsem, sig=False, dval=None, ep=self.epoch,
                  dur=(0.3 if dur is None else dur), lat=lat, tbl=tbl)
        if dsem is not None:
            self.dsem_count[dsem] = self.dsem_count.get(dsem, 0) + 16
            op["dval"] = self.dsem_count[dsem]
        self.ops.append(op)
        return i

    def list_schedule(self, mode="blevel"):
        ops = self.ops
        n = len(ops)
        body = [i for i in range(n) if ops[i]["fn"] is not None]
        tailops = [i for i in range(n) if ops[i]["fn"] is None]
        succ = [[] for _ in range(n)]
        indeg = [0] * n
        for i in body:
            for d in ops[i]["deps"]:
                succ[d].append(i)
                indeg[i] += 1
        last_d = {}
        for i in body:
            k = ops[i]["dsem"]
            if k is not None:
                if k in last_d and last_d[k] not in ops[i]["deps"]:
                    succ[last_d[k]].append(i)
                    indeg[i] += 1
                last_d[k] = i
        blev = [0.0] * n
        for i in reversed(body):
            m = 0.0
            for j in succ[i]:
                if blev[j] > m:
                    m = blev[j]
            blev[i] = m + ops[i]["dur"] + ops[i]["lat"] + (DMA_BOOST if ops[i]["dsem"] is not None else 0.0)
        finish = [0.0] * n
        ready_t = [0.0] * n
        eng_free = {e: 0.0 for e in self.ENGS}
        act_tbl = [None]
        ready = {e: [] for e in self.ENGS}
        for i in body:
            if indeg[i] == 0:
                ready[ops[i]["eng"]].append(i)
        order = []
        left = len(body)
        while left:
            best = None
            for e in self.ENGS:
                lst = ready[e]
                if not lst:
                    continue
                t_e = max(eng_free[e], min(ready_t[i] for i in lst))
                if best is None or t_e < best[0]:
                    best = (t_e, e)
            t_e, e = best
            lst = ready[e]
            cands = [i for i in lst if ready_t[i] <= t_e + 1e-9]
            if mode == "blevel":
                if e == "act":
                    cur = act_tbl[0]
                    i = max(cands, key=lambda q: (blev[q] - (1.3 if (ops[q]["tbl"] not in (None, cur)) else 0.0), -q))
                else:
                    i = max(cands, key=lambda q: (blev[q], -q))
            else:
                i = min(cands)
            lst.remove(i)
            op = ops[i]
            dur = op["dur"]
            if e == "act" and op["tbl"] is not None and op["tbl"] != act_tbl[0]:
                dur += 1.3
                act_tbl[0] = op["tbl"]
            eng_free[e] = t_e + dur
            finish[i] = t_e + dur + op["lat"]
            order.append(i)
            left -= 1
            for j in succ[i]:
                indeg[j] -= 1
                lat = 0.0 if ops[j]["eng"] == e and op["dsem"] is None else XLAT
                if finish[i] + lat > ready_t[j]:
                    ready_t[j] = finish[i] + lat
                if indeg[j] == 0:
                    ready[ops[j]["eng"]].append(j)
        order += tailops
        remap = {old: new for new, old in enumerate(order)}
        newops = [ops[i] for i in order]
        for op in newops:
            op["deps"] = {remap[d] for d in op["deps"]}
        self.ops = newops
        self.est_makespan = max(finish) if finish else 0.0

    def finalize(self):
        ops = self.ops
        for op in ops:
            keep = set()
            for d in op["deps"]:
                dop = ops[d]
                if dop["dsem"] is not None:
                    keep.add(d)
                elif dop["eng"] != op["eng"] or op["eng"] in self.SAME_SYNC:
                    dop["sig"] = True
                    keep.add(d)
            op["deps"] = keep
        cnt = {}
        for op in ops:
            if op["dsem"] is None and op["sig"]:
                k = (op["eng"], op["ep"])
                cnt[k] = cnt.get(k, 0) + 1
                op["sval"] = cnt[k]
        self.max_sval = max(cnt.values()) if cnt else 0
        waited = {e: {} for e in self.ENGS}
        for op in ops:
            need = {}
            for d in op["deps"]:
                dop = ops[d]
                if dop["dsem"] is not None:
                    key, val = ("d", dop["dsem"]), dop["dval"]
                else:
                    key, val = ("e", (dop["eng"], dop["ep"])), dop["sval"]
                if val > need.get(key, 0):
                    need[key] = val
            w = waited[op["eng"]]
            waits = []
            for key, val in need.items():
                if val > w.get(key, 0):
                    w[key] = val
                    waits.append((key, val))
            op["waits"] = waits

    def emit(self, block, esems, dsems):
        handles = {"pe": block.tensor, "act": block.scalar, "dve": block.vector,
                   "pool": block.gpsimd, "sp": block.sync}
        for ename in self.ENGS:
            myops = [op for op in self.ops if op["eng"] == ename]
            if not myops:
                continue

            def body(eng, myops=myops, ename=ename):
                for op in myops:
                    for (kind, k), val in op["waits"]:
                        eng.wait_ge(dsems[k] if kind == "d" else esems[k], val)
                    if op["fn"] is None:
                        continue
                    inst = op["fn"](eng)
                    if op["dsem"] is not None:
                        inst.then_inc(dsems[op["dsem"]], 16)
                    elif op["sig"]:
                        inst.then_inc(esems[(ename, op["ep"])], 1)

            handles[ename](body)


LIST_SCHED = True
XLAT = 0.35
DMA_BOOST = 0.0
SCHED_MODE = "blevel"


def build_nc(S, TS=512, NSLOT=8, dbg=99):
    NST = S // TS
    NT = TS // 512
    NB = TS // 128
    NCH = TS // 64
    assert S % TS == 0 and TS % 512 == 0

    nc = bass.Bass("TRN2", target_bir_lowering=False)

    def din(name, shape):
        return nc.dram_tensor(name, shape, F32, kind="ExternalInput").ap()

    x_d = din("x", [S, D])
    w_in_d = din("w_in", [D, IN_W])
    w_hg_d = din("w_hg_out", [D, D])
    w_cv_d = din("w_conv_out", [D, D])
    w_out_d = din("w_out", [D, D])
    w_f1_d = din("w_ffn_in", [D, 2 * FFN])
    w_f2_d = din("w_ffn_out", [FFN, D])
    pp_d = din("pp", [128, PP_N])
    lnp_d = din("lnp", [128, 4, D])
    cst_d = din("cst", [128, CST_N(TS)])
    out_d = nc.dram_tensor("out", [S, D], F32, kind="ExternalOutput").ap()

    P = Prog()
    es = ExitStack()
    with es:
        def sb(name, shape, dt):
            return es.enter_context(nc.sbuf_tensor("sb_" + name, shape, dt))

        R = sb("R", [128, NB, D], F32)
        xTs = [sb(f"xT{i}", [128, 8, TS], BF16) for i in range(2)]
        on = sb("on", [128, 8, TS], BF16)
        shared = sb("shared", [128, 24 * TS], BF16)
        slots = [sb(f"slot{i}", [128, 8, 256], BF16) for i in range(NSLOT)]
        lnt = sb("lnt", [128, 4, D], F32)
        cs = sb("cs", [128, CST_N(TS)], F32)
        pp = sb("pp", [128, PP_N], F32)
        Fall = sb("Fall", [128, 8, TS], F32)
        Fb = [[Fall[:, p * 4 + i, :] for i in range(4)] for p in range(2)]
        Qpp = [sb(f"Qpp{p}", [128, TS], BF16) for p in range(2)]
        Kt = [sb(f"Kt{p}", [128, TS], BF16) for p in range(2)]
        ogs = [sb(f"ogs{p}", [128, TS], BF16) for p in range(2)]
        v_tm = [sb(f"v_tm{p}", [128, NB, 128], BF16) for p in range(2)]
        K_tm = [[sb(f"K_tm{p}_{i}", [128, NB, 128], BF16) for i in range(2)] for p in range(2)]
        KVs = [sb(f"KVs{p}", [128, NCH, 128], F32) for p in range(2)]
        Tb = [sb(f"Tb{p}", [128, NCH + 1, 128], BF16) for p in range(2)]
        ebuf = [sb(f"ebuf{p}", [128, NCH], F32) for p in range(2)]
        Am = [sb(f"Am{i}", [128, 4, 128], BF16) for i in range(2)]
        Tprev = sb("Tprev", [128, NH, 128], BF16)
        halo = sb("halo", [128, 8, 32], BF16)
        ubuf = [sb(f"ubuf{p}", [128, 32 + TS], BF16) for p in range(2)]
        Dg = [sb(f"Dg{p}", [128, TAPS, 128], BF16) for p in range(2)]
        scr = [sb(f"scr{i}", [128, 512], F32) for i in range(8)]
        scb = [sb(f"scb{i}", [128, 512], BF16) for i in range(4)]
        identB = sb("identB", [128, 128], BF16)
        ones128 = sb("ones128", [128, 128], BF16)
        ones1024 = sb("ones1024", [128, 128], BF16)
        lbt = sb("lbt", [128, 4, NH], F32)
        cwh = sb("cwh", [128, 8 * TAPS], F32)
        st6 = sb("st6", [128, 12], F32)
        mv = sb("mv", [128, 4], F32)
        cmean = sb("cmean", [128, 512], F32)
        crstd = sb("crstd", [128, 512], F32)
        psb = [es.enter_context(nc.psum_tensor(f"psb{i}", [128, 512], F32)) for i in range(8)]

        cpre = shared[:, 0:16 * TS].bitcast(F32).rearrange("p (j t) -> p j t", j=8)
        un = shared[:, 16 * TS:24 * TS].rearrange("p (j t) -> p j t", j=8)
        mixed = shared[:, 0:8 * TS].rearrange("p (j t) -> p j t", j=8)
        actb = shared[:, 0:22 * TS].rearrange("p (j t) -> p j t", j=22)

        def k_cpre(j, tt):
            b0 = (j * TS + tt * 512) * 4
            return [("sh", b0 // 1024), ("sh", b0 // 1024 + 1)]

        def k_bf(base_seg_elems, j, tt):
            b0 = (base_seg_elems + j * TS + tt * 512) * 2
            return [("sh", b0 // 1024)]

        def k_un(j, tt):
            return k_bf(16 * TS, j, tt)

        def k_mixed(j, tt):
            return k_bf(0, j, tt)

        def k_act(j, tt):
            return k_bf(0, j, tt)

        identF = cs[:, 0:128]
        mask2 = cs[:, 128:256]
        c_eps_rms = cs[:, 256:257]
        c_eps_ln = cs[:, 257:258]
        c_eps_ln4 = cs[:, 258:259]
        scanmask = cs[:, 320:320 + TS]
        lbp = pp[:, 0:16].rearrange("p (a h) -> p a h", a=2)
        gn = pp[:, 16:17]
        cw = pp[:, 32:32 + 8 * TAPS].rearrange("p (j t) -> p j t", j=8)
        cb = pp[:, 288:296]
        cg = pp[:, 296:304]
        cbb = pp[:, 304:312]

        esems = {(e, ep): es.enter_context(nc.semaphore(f"s_{e}_{ep}"))
                 for e in ("pe", "act", "dve", "pool") for ep in range(NST + 1)}
        dnames = ([f"ws{i}" for i in range(NSLOT)] + [f"r{i}" for i in range(NB)]
                  + [f"o{i}" for i in range(NB)] + [f"xs{i}" for i in range(NB)] + ["c0", "c1", "c2"])
        dsems = {d: es.enter_context(nc.semaphore("d_" + d)) for d in dnames}
        print('sbuf bytes remaining', nc.sbuf_bytes_remaining)
        block = es.enter_context(nc.Block())

        def fsz(ap):
            n = 1
            for d in ap.shape[1:]:
                n *= d
            return n

        def mm(out, lhsT, rhs, start, stop, reads, writes):
            P.add("pe", lambda e: e.matmul(out, lhsT=lhsT, rhs=rhs, start=start, stop=stop), reads, writes,
                  dur=max(64, fsz(rhs)) / 2300.0 + 0.01)

        def tr(out, in_, ident, reads, writes):
            P.add("pe", lambda e: e.transpose(out, in_, ident), reads, writes, dur=0.09)

        def act(out, in_, func, reads, writes, scale=None, bias=None):
            kw = {}
            if scale is not None:
                kw["scale"] = scale
            if bias is not None:
                kw["bias"] = bias
            tbl = {AF.Tanh: "A", AF.Silu: "A", AF.Ln: "B", AF.Sigmoid: "C"}.get(func)
            P.add("act", lambda e: e.activation(out=out, in_=in_, func=func, **kw), reads, writes,
                  dur=0.22 + fsz(out) / 1200.0 + (0.19 if (scale is not None and not isinstance(scale, float)) or
                                                     (bias is not None and not isinstance(bias, float)) else 0.0),
                  tbl=tbl)

        def tt_(out, in0, in1, op, reads, writes, eng="dve"):
            P.add(eng, lambda e: e.tensor_tensor(out=out, in0=in0, in1=in1, op=op), reads, writes,
                  dur=0.30 + fsz(out) / 960.0)

        def ts_(out, in0, s1, s2, op0, op1, reads, writes, eng="dve"):
            P.add(eng, lambda e: e.tensor_scalar(out=out, in0=in0, scalar1=s1, scalar2=s2, op0=op0, op1=op1),
                  reads, writes, dur=0.30 + fsz(out) / 960.0)

        def stt(out, in0, scalar, in1, op0, op1, reads, writes):
            P.add("dve", lambda e: e.scalar_tensor_tensor(out=out, in0=in0, scalar=scalar, in1=in1,
                                                          op0=op0, op1=op1), reads, writes,
                  dur=0.30 + fsz(out) / 960.0)

        def cp(out, in_, reads, writes, eng="dve"):
            if eng == "act":
                P.add("act", lambda e: e.activation(out=out, in_=in_, func=AF.Copy), reads, writes,
                      dur=0.22 + fsz(out) / 1200.0)
            else:
                P.add(eng, lambda e: e.tensor_copy(out=out, in_=in_), reads, writes, dur=0.30 + fsz(out) / 960.0)

        def dma(eng, out, in_, reads, writes, dsem):
            nbytes = 128 * fsz(out) * 4
            P.add(eng, lambda e: e.dma_start(out=out, in_=in_), reads, writes, dsem=dsem,
                  dur=0.1, lat=2.0 + nbytes / 150000.0)

        st = {"ps": 0, "scr": 0, "scb": 0, "slot": 0, "alt": 0, "am": 0}

        ps_free = list(range(6))

        def newps():
            assert ps_free, "out of PSUM banks"
            i = ps_free.pop(0)
            return psb[i], ("ps", i)

        def relps(*keys):
            for k in keys:
                assert k[1] not in ps_free
                ps_free.append(k[1])

        def newscr():
            i = st["scr"]; st["scr"] = (i + 1) % 8
            return scr[i], ("scr", i)

        def newscb():
            i = st["scb"]; st["scb"] = (i + 1) % 4
            return scb[i], ("scb", i)

        def wblock(wd, k0, KC, n0):
            i = st["slot"]; st["slot"] = (i + 1) % NSLOT
            key = ("slot", i)
            src = wd[k0:k0 + KC * 128, n0:n0 + 256].rearrange("(kc p) n -> p kc n", p=128)
            dma("pool", slots[i][:, 0:KC, :], src, [], [key], f"ws{i}")
            return slots[i], key

        def alt_eng():
            st["alt"] ^= 1
            return "act" if st["alt"] else "dve"

        dma("sp", cs[:], cst_d, [], ["cs"], "c0")
        dma("sp", pp[:], pp_d, [], ["pp"], "c1")
        dma("sp", lnt[:], lnp_d, [], ["lnt"], "c2")
        cp(identB[:], identF, ["cs"], ["identB"])
        P.add("dve", lambda e: e.memset(scr[0][:], 0.0), [], [("scr", 0)])
        P.add("dve", lambda e: e.memset(scr[1][:], 1.0 / 128.0), [], [("scr", 1)])
        P.add("dve", lambda e: e.memset(scr[2][:], 1.0 / 1024.0), [], [("scr", 2)])
        cp(ones128[:], scr[1][:, 0:128], [("scr", 1)], ["ones128"])
        cp(ones1024[:], scr[2][:, 0:128], [("scr", 2)], ["ones1024"])
        Tpf = Tprev[:].rearrange("p h d -> p (h d)")
        cp(Tpf[:, 0:512], scr[0][:], [("scr", 0)], [("Tprev", h) for h in range(4)])
        cp(Tpf[:, 512:1024], scr[0][:], [("scr", 0)], [("Tprev", h) for h in range(4, 8)])
        cp(halo[:].rearrange("p j t -> p (j t)"), scr[0][:, 0:256], [("scr", 0)], [("halo", j) for j in range(8)])
        for p_ in range(2):
            for i in range(2):
                for tb in range(0, NB, 4):
                    cp(K_tm[p_][i][:, tb:tb + 4, :].rearrange("p a b -> p (a b)"), scr[0][:], [("scr", 0)],
                       [("Ktm", p_, tb // 4, 0), ("Ktm", p_, tb // 4, 1)])
        tt_(lbt[:, 3, :], lbp[:, 0, :], lbp[:, 1, :], ALU.subtract, ["pp"], ["lbt3"])
        act(lbt[:, 0, :], lbt[:, 3, :], AF.Sigmoid, ["lbt3"], ["lbt0"])
        ts_(lbt[:, 1, :], lbt[:, 0, :], -0.5, 0.5, ALU.mult, ALU.add, ["lbt0"], ["lbt1"])
        ts_(lbt[:, 2, :], lbt[:, 0, :], 0.5, -0.5, ALU.mult, ALU.add, ["lbt0"], ["lbt2"])
        ts_(lbt[:, 3, :], lbt[:, 0, :], 0.5, 0.5, ALU.mult, ALU.add, ["lbt0"], ["lbt3b"])
        ts_(cwh[:], pp[:, 32:32 + 8 * TAPS], 0.5, None, ALU.mult, ALU.bypass, ["pp"], ["cwh"])
        LB = ["lbt3b", "lbt1", "lbt2"]

        def transpose_blk(src, src_keys, tb, xb):
            for g in range(2):
                ps, pk = newps()
                for kk in range(4):
                    kc = g * 4 + kk
                    tr(ps[:, kk * 128:(kk + 1) * 128], src[:, kc * 128:(kc + 1) * 128], identF,
                       list(src_keys) + ["cs"], [pk])
                cp(xTs[xb][:, g * 4:(g + 1) * 4, tb * 128:(tb + 1) * 128],
                   ps[:].rearrange("p (a b) -> p a b", a=4), [pk], [("xT", xb, tb, g)], eng=alt_eng())
                relps(pk)

        def xT_keys(tt):
            return [("xT", st["xb"], tb, g) for tb in range(tt * 4, tt * 4 + 4) for g in range(2)]

        def xstage(tb):
            ap = Fall[:, 2 * tb:2 * tb + 2, :].rearrange("p a t -> p (a t)")
            keys = [(f"F{idx % 4}", idx // 4, 0) for idx in (2 * tb, 2 * tb + 1)]
            return ap, keys

        def prefetch_x_load(sti_):
            for tb in range(NB):
                ap, keys = xstage(tb)
                dma("sp", ap, x_d[sti_ * TS + tb * 128:sti_ * TS + (tb + 1) * 128, :], [], keys, f"xs{tb}")

        def prefetch_x_transpose(sti_):
            for tb in range(NB):
                ap, keys = xstage(tb)
                transpose_blk(ap, keys, tb, sti_ % 2)

        def layer_norm_R(tb, gi, eps_ap=None):
            eps_ap = c_eps_ln if eps_ap is None else eps_ap
            rk = ("R", tb)
            P.add("dve", lambda e: e.bn_stats(out=st6[:, 0:6], in_=R[:, tb, 0:512]), [rk], ["st6a"])
            P.add("dve", lambda e: e.bn_stats(out=st6[:, 6:12], in_=R[:, tb, 512:1024]), [rk], ["st6b"])
            P.add("dve", lambda e: e.bn_aggr(out=mv[:, 0:2], in_=st6[:]), ["st6a", "st6b"], ["mv01"])
            act(mv[:, 2:3], mv[:, 1:2], AF.Ln, ["mv01", "cs"], ["mv2"], bias=eps_ap)
            act(mv[:, 3:4], mv[:, 2:3], AF.Exp, ["mv2"], ["mv3"], scale=-0.5)
            ts_(R[:, tb, :], R[:, tb, :], mv[:, 0:1], mv[:, 3:4], ALU.subtract, ALU.mult,
                [rk, "mv01", "mv3"], [rk])
            tt_(R[:, tb, :], R[:, tb, :], lnt[:, gi, :], ALU.mult, [rk, "lnt"], [rk])
            tt_(R[:, tb, :], R[:, tb, :], lnt[:, gi + 1, :], ALU.add, [rk, "lnt"], [rk])

        for sti in range(NST):
            t0 = sti * TS
            P.epoch = sti + 1
            st["xb"] = sti % 2
            xT = xTs[sti % 2]
            if sti == 0:
                prefetch_x_load(0)
                prefetch_x_transpose(0)
            for tb in range(NB):
                dma("sp", R[:, tb, :], x_d[t0 + tb * 128:t0 + (tb + 1) * 128, :], [], [("R", tb)], f"r{tb}")

            wst = {}

            def head_front(h):
                par = h % 2
                if h % 2 == 0:
                    hp = h // 2
                    for nm, sec in (("q", 0), ("f", 1), ("i", 2), ("o", 3)):
                        wst[nm] = wblock(w_in_d, 0, 8, sec * 1024 + hp * 256)
                (wq, kq), (wf, kf), (wi, ki), (wo, ko) = wst["q"], wst["f"], wst["i"], wst["o"]
                hc = (h % 2) * 128
                F0, F1, F2, F3 = Fb[par]
                allF = lambda n: [(n, par, tt) for tt in range(NT)]
                for tt in range(NT):
                    tsl = slice(tt * 512, (tt + 1) * 512)
                    for (wblk, wk, dst, dk_, fn) in ((wq, kq, F0, ("F0", par, tt), AF.Silu),
                                                     (wf, kf, F1, ("F1", par, tt), AF.Tanh),
                                                     (wo, ko, ogs[par], ("ogs", par, tt), AF.Silu)):
                        ps, pk = newps()
                        for kc in range(8):
                            mm(ps[:], wblk[:, kc, hc:hc + 128], xT[:, kc, tsl], kc == 0, kc == 7,
                               [wk] + xT_keys(tt), [pk])
                        act(dst[:, tsl], ps[:], fn, [pk], [dk_], scale=(0.5 if fn == AF.Tanh else None))
                        relps(pk)
                        yield
                for tg in range(NB // 4):
                    ps, pk = newps()
                    for tl in range(4):
                        tb = tg * 4 + tl
                        for kc in range(8):
                            mm(ps[:, tl * 128:(tl + 1) * 128], xT[:, kc, tb * 128:(tb + 1) * 128],
                               wi[:, kc, hc:hc + 128], kc == 0, kc == 7,
                               [ki, ("xT", st["xb"], tb, 0), ("xT", st["xb"], tb, 1)], [pk])
                    cp(v_tm[par][:, tg * 4:(tg + 1) * 4, :], ps[:].rearrange("p (a b) -> p a b", a=4),
                       [pk], [("vtm", par, tg)])
                    relps(pk)
                    yield
                act(F2[:], F1[:], AF.Ln, allF("F1") + LB, allF("F2"),
                    scale=lbt[:, 1, h:h + 1], bias=lbt[:, 3, h:h + 1])
                ts_(F3[:], F1[:], lbt[:, 2, h:h + 1], lbt[:, 1, h:h + 1], ALU.mult, ALU.add,
                    allF("F1") + LB, allF("F3"))
                yield
                P.add("dve", lambda e, F1=F1, F2=F2: e.tensor_tensor_scan(
                    out=F1[:], data0=scanmask, data1=F2[:], initial=0.0, op0=ALU.mult, op1=ALU.add),
                    allF("F2") + ["cs"], allF("F1"), dur=0.1 + 2 * TS / 960.0)
                yield
                act(F2[:], F1[:], AF.Exp, allF("F1"), allF("F2"))
                yield
                cp(ebuf[par][:], F2[:].rearrange("p (c t) -> p c t", t=64)[:, :, 63], allF("F2"), [("ebuf", par)])
                tt_(Qpp[par][:], F0[:], F2[:], ALU.mult, allF("F0") + allF("F2"), [("Qpp", par)])
                yield
                act(F0[:], F1[:], AF.Exp, allF("F1"), allF("F0"), scale=-1.0)
                yield
                tt_(Kt[par][:], F3[:], F0[:], ALU.mult, allF("F3") + allF("F0"), [("Kt", par)])
                yield

            def head_back(h):
                par = h % 2
                Ktp, Qp, vt, Tbp, eb, KV = Kt[par], Qpp[par], v_tm[par], Tb[par], ebuf[par], KVs[par]
                for tg in range(NB // 4):
                    ps, pk = newps()
                    psv = ps[:].bitcast(BF16)
                    for tl in range(4):
                        tb = tg * 4 + tl
                        tr(psv[:, tl * 128:(tl + 1) * 128], Ktp[:, tb * 128:(tb + 1) * 128], identB[:],
                           [("Kt", par), "identB"], [pk])
                    for hf in range(2):
                        cp(K_tm[par][hf][hf * 64:hf * 64 + 64, tg * 4:(tg + 1) * 4, :],
                           psv[hf * 64:hf * 64 + 64, 0:512].rearrange("p (a b) -> p a b", a=4),
                           [pk], [("Ktm", par, tg, hf)], eng=("act" if hf == 0 else "dve"))
                    relps(pk)
                    yield
                for cg_ in range(NCH // 4):
                    ps, pk = newps()
                    for cl in range(4):
                        c = cg_ * 4 + cl
                        tb, hf = c // 2, c % 2
                        mm(ps[:, cl * 128:(cl + 1) * 128], K_tm[par][hf][:, tb, :], vt[:, tb, :], True, True,
                           [("Ktm", par, tb // 4, hf), ("vtm", par, tb // 4)], [pk])
                    tt_(KV[:, cg_ * 4:(cg_ + 1) * 4, :], ps[:].rearrange("p (a b) -> p a b", a=4),
                        eb[:, cg_ * 4:(cg_ + 1) * 4].unsqueeze(2).broadcast_to([128, 4, 128]), ALU.mult,
                        [pk, ("ebuf", par)], [("KVs", par, cg_)])
                    relps(pk)
                    yield
                cp(Tbp[:, 0, :], Tprev[:, h, :], [("Tprev", h)], [("Tb", par, 0)])
                for c in range(NCH):
                    stt(Tbp[:, c + 1, :], Tbp[:, c, :], eb[:, c:c + 1], KV[:, c, :], ALU.mult, ALU.add,
                        [("Tb", par, c), ("ebuf", par), ("KVs", par, c // 4)], [("Tb", par, c + 1)])
                    if c % 2 == 1:
                        yield
                cp(Tprev[:, h, :], Tbp[:, NCH, :], [("Tb", par, NCH)], [("Tprev", h)], eng="act")
                yield
                yield
                for tt in range(NT):
                    tsl = slice(tt * 512, (tt + 1) * 512)
                    pA, pAk = newps()
                    for tbl in range(4):
                        tb = tt * 4 + tbl
                        mm(pA[:, tbl * 128:(tbl + 1) * 128], Ktp[:, tb * 128:(tb + 1) * 128],
                           Qp[:, tb * 128:(tb + 1) * 128], True, True, [("Kt", par), ("Qpp", par)], [pAk])
                    st["am"] ^= 1
                    am = Am[st["am"]]
                    amk = ("Am", st["am"])
                    tt_(am[:], pA[:].rearrange("p (a b) -> p a b", a=4),
                        mask2.unsqueeze(1).broadcast_to([128, 4, 128]), ALU.mult, [pAk, "cs"], [amk])
                    relps(pAk)
                    yield
                    pO, pOk = newps()
                    for tbl in range(4):
                        tb = tt * 4 + tbl
                        mm(pO[:, tbl * 128:(tbl + 1) * 128], vt[:, tb, :], am[:, tbl, :], True, False,
                           [("vtm", par, tb // 4), amk], [pOk])
                        for hf in range(2):
                            c = 2 * tb + hf
                            mm(pO[:, tbl * 128 + hf * 64:tbl * 128 + hf * 64 + 64], Tbp[:, c, :],
                               Qp[:, c * 64:(c + 1) * 64], False, hf == 1, [("Tb", par, c), ("Qpp", par)], [pOk])
                    osq, osk = newscb()
                    act(osq[:], pO[:], AF.Square, [pOk], [osk])
                    yield
                    pM, pMk = newps()
                    mm(pM[:], ones128[:], osq[:], True, True, ["ones128", osk], [pMk])
                    lnv, lnk = newscr()
                    act(lnv[:], pM[:], AF.Ln, [pMk, "cs"], [lnk], bias=c_eps_rms)
                    relps(pMk)
                    rstd, rsk = newscr()
                    act(rstd[:], lnv[:], AF.Exp, [lnk], [rsk], scale=-0.5)
                    t1, t1k = newscr()
                    stt(t1[:], pO[:], gn, rstd[:], ALU.mult, ALU.mult, [pOk, "pp", rsk], [t1k])
                    relps(pOk)
                    tt_(on[:, h, tsl], t1[:], ogs[par][:, tsl], ALU.mult, [t1k, ("ogs", par, tt)], [("on", h, tt)])
                    yield

            def conv_front(j):
                par = j % 2
                if j % 2 == 0:
                    wst["gv"] = wblock(w_in_d, 0, 8, 4096 + (j // 2) * 256)
                    wst["gg"] = wblock(w_in_d, 0, 8, 5120 + (j // 2) * 256)
                (wgv, kgv), (wgg, kgg) = wst["gv"], wst["gg"]
                jc = (j % 2) * 128
                ub = ubuf[par]
                cp(ub[:, 0:32], halo[:, j, :], [("halo", j)], [("ubuf_h", par)], eng="act")
                for tt in range(NT):
                    psv_, pvk = newps()
                    psg, pgk = newps()
                    tsl = slice(tt * 512, (tt + 1) * 512)
                    for kc in range(8):
                        mm(psv_[:], wgv[:, kc, jc:jc + 128], xT[:, kc, tsl], kc == 0, kc == 7,
                           [kgv] + xT_keys(tt), [pvk])
                        if kc % 2 == 1:
                            yield
                    for kc in range(8):
                        mm(psg[:], wgg[:, kc, jc:jc + 128], xT[:, kc, tsl], kc == 0, kc == 7,
                           [kgg] + xT_keys(tt), [pgk])
                        if kc % 2 == 1:
                            yield
                    sg_, sgk = newscr()
                    act(sg_[:], psg[:], AF.Tanh, [pgk], [sgk], scale=0.5)
                    stt(ub[:, 32 + tt * 512:32 + (tt + 1) * 512], sg_[:], 1.0, psv_[:], ALU.add, ALU.mult,
                        [pvk, sgk], [("ubuf", par, tt)])
                    relps(pvk, pgk)
                cp(halo[:, j, :], ub[:, TS:TS + 32], [("ubuf", par, NT - 1)], [("halo", j)], eng="act")
                for t0_, t1_ in ((0, 8), (8, 16), (16, 24), (24, TAPS)):
                    nt_ = t1_ - t0_
                    tt_(Dg[par][:, t0_:t1_, :], identB[:].unsqueeze(1).broadcast_to([128, nt_, 128]),
                        cwh[:, j * TAPS + t0_:j * TAPS + t1_].unsqueeze(2).broadcast_to([128, nt_, 128]), ALU.mult,
                        ["identB", "cwh"], [("Dg", par, t0_ // 8)])
                    yield

            def conv_back(j):
                par = j % 2
                ub = ubuf[par]
                for tt in range(NT):
                    pc, pck = psb[6 + par], ("ps", 6 + par)
                    ur = [("ubuf_h", par)] + [("ubuf", par, t_) for t_ in range(tt + 1)]
                    for tap in range(TAPS):
                        off = 2 + tt * 512 + tap
                        mm(pc[:], Dg[par][:, tap, :], ub[:, off:off + 512], tap == 0, tap == TAPS - 1,
                           [("Dg", par, tap // 8)] + ur, [pck])
                        if tap % 3 == 2:
                            yield
                    act(cpre[:, j, tt * 512:(tt + 1) * 512], pc[:], AF.Identity, [pck, "pp"], k_cpre(j, tt),
                        bias=cb[:, j:j + 1])
                    yield

            def merged(gens):
                gens = list(gens)
                while gens:
                    for g in list(gens):
                        try:
                            next(g)
                        except StopIteration:
                            gens.remove(g)
                            continue
                        yield

            def thread(front, back, n):
                yield from front(0)
                for i in range(n):
                    gs = [back(i)]
                    if i + 1 < n:
                        gs.append(front(i + 1))
                    yield from merged(gs)

            threads = []
            if dbg >= 3:
                threads.append(thread(head_front, head_back, NH))
            if dbg >= 4:
                threads.append(thread(conv_front, conv_back, 8))
            while threads:
                for th in list(threads):
                    try:
                        next(th)
                    except StopIteration:
                        threads.remove(th)

            for tt in range(NT if dbg >= 4 else 0):
                tsl = slice(tt * 512, (tt + 1) * 512)
                pS1, pS1k = newps()
                pS2, pS2k = newps()
                for j in range(8):
                    cbf, cbk = newscb()
                    csq, csk = newscb()
                    cp(cbf[:], cpre[:, j, tsl], k_cpre(j, tt), [cbk])
                    act(csq[:], cpre[:, j, tsl], AF.Square, k_cpre(j, tt), [csk])
                    mm(pS1[:], ones1024[:], cbf[:], j == 0, j == 7, ["ones1024", cbk], [pS1k])
                    mm(pS2[:], ones1024[:], csq[:], j == 0, j == 7, ["ones1024", csk], [pS2k])
                mean, mk_ = cmean, "cmean"
                cp(mean[:], pS1[:], [pS1k], [mk_], eng="act")
                relps(pS1k)
                msq, msk = newscr()
                tt_(msq[:], mean[:], mean[:], ALU.mult, [mk_], [msk])
                var, vk = newscr()
                tt_(var[:], pS2[:], msq[:], ALU.subtract, [pS2k, msk], [vk])
                relps(pS2k)
                lnv, lnk = newscr()
                act(lnv[:], var[:], AF.Ln, [vk, "cs"], [lnk], bias=c_eps_ln)
                rstd, rsk = crstd, "crstd"
                act(rstd[:], lnv[:], AF.Exp, [lnk], [rsk], scale=-0.5)
                for j in range(8):
                    ta, tak = newscr()
                    tt_(ta[:], cpre[:, j, tsl], mean[:], ALU.subtract, k_cpre(j, tt) + [mk_], [tak])
                    t2, t2k = newscr()
                    tt_(t2[:], ta[:], rstd[:], ALU.mult, [tak, rsk], [t2k])
                    act(un[:, j, tsl], t2[:], AF.Silu, [t2k, "pp"], k_un(j, tt),
                        scale=cg[:, j:j + 1], bias=cbb[:, j:j + 1])

            for j in range(8 if dbg >= 5 else 0):
                if j % 2 == 0:
                    wha, kha = wblock(w_hg_d, 0, 8, (j // 2) * 256)
                    wcb, kcb = wblock(w_cv_d, 0, 8, (j // 2) * 256)
                    wga, kga = wblock(w_in_d, 0, 8, 6144 + (j // 2) * 256)
                    wgb, kgb = wblock(w_in_d, 0, 8, 7168 + (j // 2) * 256)
                jc = (j % 2) * 128
                for tt in range(NT):
                    tsl = slice(tt * 512, (tt + 1) * 512)
                    pya, pyak = newps()
                    pyb, pybk = newps()
                    pga, pgak = newps()
                    pgb, pgbk = newps()
                    for kc in range(8):
                        mm(pga[:], wga[:, kc, jc:jc + 128], xT[:, kc, tsl], kc == 0, kc == 7,
                           [kga] + xT_keys(tt), [pgak])
                    for kc in range(8):
                        mm(pgb[:], wgb[:, kc, jc:jc + 128], xT[:, kc, tsl], kc == 0, kc == 7,
                           [kgb] + xT_keys(tt), [pgbk])
                    for kc in range(8):
                        mm(pya[:], wha[:, kc, jc:jc + 128], on[:, kc, tsl], kc == 0, kc == 7,
                           [kha, ("on", kc, tt)], [pyak])
                    for kc in range(8):
                        mm(pyb[:], wcb[:, kc, jc:jc + 128], un[:, kc, tsl], kc == 0, kc == 7,
                           [kcb] + k_un(kc, tt), [pybk])
                    sa, sak = newscr()
                    act(sa[:], pga[:], AF.Tanh, [pgak], [sak], scale=0.5)
                    sb_, sbk = newscr()
                    act(sb_[:], pgb[:], AF.Tanh, [pgbk], [sbk], scale=0.5)
                    m1, m1k = newscr()
                    stt(m1[:], sa[:], 1.0, pya[:], ALU.add, ALU.mult, [pyak, sak], [m1k])
                    m2, m2k = newscr()
                    stt(m2[:], sb_[:], 1.0, pyb[:], ALU.add, ALU.mult, [pybk, sbk], [m2k])
                    relps(pyak, pybk, pgak, pgbk)
                    tt_(mixed[:, j, tsl], m1[:], m2[:], ALU.add, [m1k, m2k], k_mixed(j, tt))

            wob = [wblock(w_out_d, 0, 8, nb * 256) for nb in range(4)] if dbg >= 6 else []
            for tb in range(NB if dbg >= 6 else 0):
                tt = tb // 4
                for half in range(2):
                    ps, pk = newps()
                    for q in range(2):
                        wblk, wk = wob[half * 2 + q]
                        for kc in range(8):
                            mm(ps[:, q * 256:(q + 1) * 256], mixed[:, kc, tb * 128:(tb + 1) * 128], wblk[:, kc, :],
                               kc == 0, kc == 7, [wk] + k_mixed(kc, tt), [pk])
                    stt(R[:, tb, half * 512:(half + 1) * 512], R[:, tb, half * 512:(half + 1) * 512], 2.0 * ALPHA, ps[:],
                        ALU.mult, ALU.add, [("R", tb), pk], [("R", tb)])
                    relps(pk)
            for tb in range(NB if dbg >= 6 else 0):
                layer_norm_R(tb, 0, c_eps_ln4)
            for tb in range(NB if dbg >= 6 else 0):
                transpose_blk(R[:, tb, :], [("R", tb)], tb, sti % 2)

            if sti + 1 < NST:
                prefetch_x_load(sti + 1)
            for jj in range(22 if dbg >= 8 else 0):
                if jj == 12 and sti + 1 < NST:
                    prefetch_x_transpose(sti + 1)
                if jj % 2 == 0:
                    wg_, kg_ = wblock(w_f1_d, 0, 8, (jj // 2) * 256)
                    wu_, ku_ = wblock(w_f1_d, 0, 8, FFN + (jj // 2) * 256)
                jc = (jj % 2) * 128
                for tt in range(NT):
                    tsl = slice(tt * 512, (tt + 1) * 512)
                    pg_, pgk_ = newps()
                    pu_, puk_ = newps()
                    for kc in range(8):
                        mm(pg_[:], wg_[:, kc, jc:jc + 128], xT[:, kc, tsl], kc == 0, kc == 7,
                           [kg_] + xT_keys(tt), [pgk_])
                    for kc in range(8):
                        mm(pu_[:], wu_[:, kc, jc:jc + 128], xT[:, kc, tsl], kc == 0, kc == 7,
                           [ku_] + xT_keys(tt), [puk_])
                    sg_, sgk = newscr()
                    act(sg_[:], pg_[:], AF.Silu, [pgk_], [sgk])
                    tt_(actb[:, jj, tsl], sg_[:], pu_[:], ALU.mult, [sgk, puk_], k_act(jj, tt))
                    relps(pgk_, puk_)

            for nb in range(4 if dbg >= 9 else 0):
                wfb = [wblock(w_f2_d, g * 1024, (8 if g < 2 else 6), nb * 256) for g in range(3)]
                for tb in range(NB):
                    tt = tb // 4
                    ps, pk = newps()
                    for kc in range(22):
                        wblk, wk = wfb[kc // 8]
                        mm(ps[:, 0:256], actb[:, kc, tb * 128:(tb + 1) * 128], wblk[:, kc % 8, :],
                           kc == 0, kc == 21, [wk] + k_act(kc, tt), [pk])
                    stt(R[:, tb, nb * 256:(nb + 1) * 256], R[:, tb, nb * 256:(nb + 1) * 256], ALPHA, ps[:, 0:256],
                        ALU.mult, ALU.add, [("R", tb), pk], [("R", tb)])
                    relps(pk)
                    if nb == 3:
                        layer_norm_R(tb, 2)
                        dma("sp", out_d[t0 + tb * 128:t0 + (tb + 1) * 128, :], R[:, tb, :], [("R", tb)], [], f"o{tb}")
            if dbg < 9:
                for tb in range(NB):
                    dma("sp", out_d[t0 + tb * 128:t0 + (tb + 1) * 128, :], R[:, tb, :], [("R", tb)], [], f"o{tb}")

        P.add("sp", None)
        fin = P.ops[-1]
        if LIST_SCHED:
            P.list_schedule(SCHED_MODE)
            print("list schedule: estimated makespan %.0f us" % P.est_makespan)
        P.finalize()
        fin["waits"] = [(("d", f"o{tb}"), P.dsem_count[f"o{tb}"]) for tb in range(NB)]
        P.emit(block, esems, dsems)
    return nc


PP_N = 320


def CST_N(TS):
    return 320 + TS


def make_consts(TS):
    c = np.zeros((128, CST_N(TS)), np.float32)
    c[:, 0:128] = np.eye(128, dtype=np.float32)
    p = np.arange(128)[:, None]
    t = np.arange(128)[None, :]
    c[:, 128:256] = ((p // 64 == t // 64) & (p % 64 <= t % 64)).astype(np.float32)
    c[:, 256] = RMS_EPS
    c[:, 257] = LN_EPS
    c[:, 258] = 4.0 * LN_EPS
    m = np.ones(TS, np.float32)
    m[::64] = 0.0
    c[:, 320:320 + TS] = m
    return c


def pack_params(lb_param, hg_norm_g, conv_w, conv_b, conv_ln_g, conv_ln_b):
    pp = np.zeros((128, PP_N), np.float32)
    pp[:, 0:16] = lb_param.reshape(2, NH, 128).transpose(2, 0, 1).reshape(128, 16)
    pp[:, 16] = hg_norm_g.reshape(128)
    pp[:, 32:32 + 8 * TAPS] = conv_w.reshape(TAPS, 8, 128).transpose(2, 1, 0).reshape(128, 8 * TAPS)
    pp[:, 288:296] = conv_b.reshape(8, 128).T
    pp[:, 296:304] = conv_ln_g.reshape(8, 128).T
    pp[:, 304:312] = conv_ln_b.reshape(8, 128).T
    return pp


def make_in_maps(x, w_in, lb_param, hg_norm_g, w_hg_out, conv_w, conv_b, conv_ln_g, conv_ln_b,
                 w_conv_out, w_out, ln1_g, ln1_b, w_ffn_in, w_ffn_out, ln2_g, ln2_b, TS=512):
    f = lambda a: np.ascontiguousarray(np.asarray(a, dtype=np.float32))
    B = x.shape[0]
    pp = pack_params(f(lb_param), f(hg_norm_g)[0], f(conv_w)[0], f(conv_b)[0], f(conv_ln_g)[0], f(conv_ln_b)[0])
    lnp = np.ascontiguousarray(np.broadcast_to(
        np.stack([f(ln1_g)[0], f(ln1_b)[0], f(ln2_g)[0], f(ln2_b)[0]])[None], (128, 4, D)))
    shared = {
        "w_in": f(w_in)[0], "w_hg_out": f(w_hg_out)[0], "w_conv_out": f(w_conv_out)[0], "w_out": f(w_out)[0],
        "w_ffn_in": f(w_ffn_in)[0], "w_ffn_out": f(w_ffn_out)[0], "pp": pp, "lnp": lnp, "cst": make_consts(TS),
    }
    xs = f(x)
    return [dict(shared, x=xs[b]) for b in range(B)]


def kernel(**inputs):
    TS = 512
    in_maps = make_in_maps(TS=TS, **inputs)
    nc = build_nc(SEQ, TS=TS)
    res = run_bass_kernel_spmd(nc, in_maps, core_ids=list(range(N_CORES)))
    return np.stack([np.asarray(r["out"], dtype=np.float32) for r in res.results], axis=0)
```

```python
import numpy as np
from contextlib import ExitStack
import concourse.bass as bass
import concourse.mybir as mybir
from concourse.bass_utils import run_bass_kernel_spmd

F32 = mybir.dt.float32
BF16 = mybir.dt.bfloat16
AF = mybir.ActivationFunctionType
ALU = mybir.AluOpType

D = 1024
NH = 8
FFN = 2816
TAPS = 31
IN_W = 8192
ALPHA = 2.0 ** 0.25
LN_EPS = 1e-5
RMS_EPS = LN_EPS * 128.0
N_CORES = 8
SEQ = 4096


class Prog:
    ENGS = ("pe", "act", "dve", "pool", "sp")
    SAME_SYNC = ("act", "dve", "pool")

    def __init__(self):
        self.ops = []
        self.lastw = {}
        self.readers = {}
        self.dsem_count = {}
        self.epoch = 0

    def add(self, eng, fn, reads=(), writes=(), dsem=None, dur=None, lat=0.0, tbl=None):
        i = len(self.ops)
        deps = set()
        for r in reads:
            w = self.lastw.get(r)
            if w is not None:
                deps.add(w)
        for w_ in writes:
            lw = self.lastw.get(w_)
            if lw is not None:
                deps.add(lw)
            for rd in self.readers.get(w_, ()):
                deps.add(rd)
        for r in reads:
            self.readers.setdefault(r, []).append(i)
        for w_ in writes:
            self.lastw[w_] = i
            self.readers[w_] = []
        op = dict(eng=eng, fn=fn, deps=deps, dsem=dsem, sig=False, dval=None, ep=self.epoch,
                  dur=(0.3 if dur is None else dur), lat=lat, tbl=tbl)
        if dsem is not None:
            self.dsem_count[dsem] = self.dsem_count.get(dsem, 0) + 16
            op["dval"] = self.dsem_count[dsem]
        self.ops.append(op)
        return i

    def list_schedule(self, mode="blevel"):
        ops = self.ops
        n = len(ops)
        body = [i for i in range(n) if ops[i]["fn"] is not None]
        tailops = [i for i in range(n) if ops[i]["fn"] is None]
        succ = [[] for _ in range(n)]
        indeg = [0] * n
        for i in body:
            for d in ops[i]["deps"]:
                succ[d].append(i)
                indeg[i] += 1
        last_d = {}
        for i in body:
            k = ops[i]["dsem"]
            if k is not None:
                if k in last_d and last_d[k] not in ops[i]["deps"]:
                    succ[last_d[k]].append(i)
                    indeg[i] += 1
                last_d[k] = i
        blev = [0.0] * n
        for i in reversed(body):
            m = 0.0
            for j in succ[i]:
                if blev[j] > m:
                    m = blev[j]
            blev[i] = m + ops[i]["dur"] + ops[i]["lat"] + (DMA_BOOST if ops[i]["dsem"] is not None else 0.0)
        finish = [0.0] * n
        ready_t = [0.0] * n
        eng_free = {e: 0.0 for e in self.ENGS}
        act_tbl = [None]
        ready = {e: [] for e in self.ENGS}
        for i in body:
            if indeg[i] == 0:
                ready[ops[i]["eng"]].append(i)
        order = []
        left = len(body)
        while left:
            best = None
            for e in self.ENGS:
                lst = ready[e]
                if not lst:
                    continue
                t_e = max(eng_free[e], min(ready_t[i] for i in lst))
                if best is None or t_e < best[0]:
                    best = (t_e, e)
            t_e, e = best
            lst = ready[e]
            cands = [i for i in lst if ready_t[i] <= t_e + 1e-9]
            if mode == "blevel":
                if e == "act":
                    cur = act_tbl[0]
                    i = max(cands, key=lambda q: (blev[q] - (1.3 if (ops[q]["tbl"] not in (None, cur)) else 0.0), -q))
                else:
                    i = max(cands, key=lambda q: (blev[q], -q))
            else:
                i = min(cands)
            lst.remove(i)
            op = ops[i]
            dur = op["dur"]
            if e == "act" and op["tbl"] is not None and op["tbl"] != act_tbl[0]:
                dur += 1.3
                act_tbl[0] = op["tbl"]
            eng_free[e] = t_e + dur
            finish[i] = t_e + dur + op["lat"]
            order.append(i)
            left -= 1
            for j in succ[i]:
                indeg[j] -= 1
                lat = 0.0 if ops[j]["eng"] == e and op["dsem"] is None else XLAT
                if finish[i] + lat > ready_t[j]:
                    ready_t[j] = finish[i] + lat
                if indeg[j] == 0:
                    ready[ops[j]["eng"]].append(j)
        order += tailops
        remap = {old: new for new, old in enumerate(order)}
        newops = [ops[i] for i in order]
        for op in newops:
            op["deps"] = {remap[d] for d in op["deps"]}
        self.ops = newops
        self.est_makespan = max(finish) if finish else 0.0

    def finalize(self):
        ops = self.ops
        for op in ops:
            keep = set()
            for d in op["deps"]:
                dop = ops[d]
                if dop["dsem"] is not None:
                    keep.add(d)
                elif dop["eng"] != op["eng"] or op["eng"] in self.SAME_SYNC:
                    dop["sig"] = True
                    keep.add(d)
            op["deps"] = keep
        cnt = {}
        for op in ops:
            if op["dsem"] is None and op["sig"]:
                k = (op["eng"], op["ep"])
                cnt[k] = cnt.get(k, 0) + 1
                op["sval"] = cnt[k]
        self.max_sval = max(cnt.values()) if cnt else 0
        waited = {e: {} for e in self.ENGS}
        for op in ops:
            need = {}
            for d in op["deps"]:
                dop = ops[d]
                if dop["dsem"] is not None:
                    key, val = ("d", dop["dsem"]), dop["dval"]
                else:
                    key, val = ("e", (dop["eng"], dop["ep"])), dop["sval"]
                if val > need.get(key, 0):
                    need[key] = val
            w = waited[op["eng"]]
            waits = []
            for key, val in need.items():
                if val > w.get(key, 0):
                    w[key] = val
                    waits.append((key, val))
            op["waits"] = waits

    def emit(self, block, esems, dsems):
        handles = {"pe": block.tensor, "act": block.scalar, "dve": block.vector,
                   "pool": block.gpsimd, "sp": block.sync}
        for ename in self.ENGS:
            myops = [op for op in self.ops if op["eng"] == ename]
            if not myops:
                continue

            def body(eng, myops=myops, ename=ename):
                for op in myops:
                    for (kind, k), val in op["waits"]:
                        eng.wait_ge(dsems[k] if kind == "d" else esems[k], val)
                    if op["fn"] is None:
                        continue
                    inst = op["fn"](eng)
                    if op["dsem"] is not None:
                        inst.then_inc(dsems[op["dsem"]], 16)
                    elif op["sig"]:
                        inst.then_inc(esems[(ename, op["ep"])], 1)

            handles[ename](body)


LIST_SCHED = True
XLAT = 0.35
DMA_BOOST = 0.0
SCHED_MODE = "blevel"


def build_nc(S, TS=512, NSLOT=8, dbg=99):
    NST = S // TS
    NT = TS // 512
    NB = TS // 128
    NCH = TS // 64
    assert S % TS == 0 and TS % 512 == 0

    nc = bass.Bass("TRN2", target_bir_lowering=False)

    def din(name, shape):
        return nc.dram_tensor(name, shape, F32, kind="ExternalInput").ap()

    x_d = din("x", [S, D])
    w_in_d = din("w_in", [D, IN_W])
    w_hg_d = din("w_hg_out", [D, D])
    w_cv_d = din("w_conv_out", [D, D])
    w_out_d = din("w_out", [D, D])
    w_f1_d = din("w_ffn_in", [D, 2 * FFN])
    w_f2_d = din("w_ffn_out", [FFN, D])
    pp_d = din("pp", [128, PP_N])
    lnp_d = din("lnp", [128, 4, D])
    cst_d = din("cst", [128, CST_N(TS)])
    out_d = nc.dram_tensor("out", [S, D], F32, kind="ExternalOutput").ap()

    P = Prog()
    es = ExitStack()
    with es:
        def sb(name, shape, dt):
            return es.enter_context(nc.sbuf_tensor("sb_" + name, shape, dt))

        R = sb("R", [128, NB, D], F32)
        xTs = [sb(f"xT{i}", [128, 8, TS], BF16) for i in range(2)]
        on = sb("on", [128, 8, TS], BF16)
        shared = sb("shared", [128, 24 * TS], BF16)
        slots = [sb(f"slot{i}", [128, 8, 256], BF16) for i in range(NSLOT)]
        lnt = sb("lnt", [128, 4, D], F32)
        cs = sb("cs", [128, CST_N(TS)], F32)
        pp = sb("pp", [128, PP_N], F32)
        Fall = sb("Fall", [128, 8, TS], F32)
        Fb = [[Fall[:, p * 4 + i, :] for i in range(4)] for p in range(2)]
        Qpp = [sb(f"Qpp{p}", [128, TS], BF16) for p in range(2)]
        Kt = [sb(f"Kt{p}", [128, TS], BF16) for p in range(2)]
        ogs = [sb(f"ogs{p}", [128, TS], BF16) for p in range(2)]
        v_tm = [sb(f"v_tm{p}", [128, NB, 128], BF16) for p in range(2)]
        K_tm = [[sb(f"K_tm{p}_{i}", [128, NB, 128], BF16) for i in range(2)] for p in range(2)]
        KVs = [sb(f"KVs{p}", [128, NCH, 128], F32) for p in range(2)]
        Tb = [sb(f"Tb{p}", [128, NCH + 1, 128], BF16) for p in range(2)]
        ebuf = [sb(f"ebuf{p}", [128, NCH], F32) for p in range(2)]
        Am = [sb(f"Am{i}", [128, 4, 128], BF16) for i in range(2)]
        Tprev = sb("Tprev", [128, NH, 128], BF16)
        halo = sb("halo", [128, 8, 32], BF16)
        ubuf = [sb(f"ubuf{p}", [128, 32 + TS], BF16) for p in range(2)]
        Dg = [sb(f"Dg{p}", [128, TAPS, 128], BF16) for p in range(2)]
        scr = [sb(f"scr{i}", [128, 512], F32) for i in range(8)]
        scb = [sb(f"scb{i}", [128, 512], BF16) for i in range(4)]
        identB = sb("identB", [128, 128], BF16)
        ones128 = sb("ones128", [128, 128], BF16)
        ones1024 = sb("ones1024", [128, 128], BF16)
        lbt = sb("lbt", [128, 4, NH], F32)
        cwh = sb("cwh", [128, 8 * TAPS], F32)
        st6 = sb("st6", [128, 12], F32)
        mv = sb("mv", [128, 4], F32)
        cmean = sb("cmean", [128, 512], F32)
        crstd = sb("crstd", [128, 512], F32)
        psb = [es.enter_context(nc.psum_tensor(f"psb{i}", [128, 512], F32)) for i in range(8)]

        cpre = shared[:, 0:16 * TS].bitcast(F32).rearrange("p (j t) -> p j t", j=8)
        un = shared[:, 16 * TS:24 * TS].rearrange("p (j t) -> p j t", j=8)
        mixed = shared[:, 0:8 * TS].rearrange("p (j t) -> p j t", j=8)
        actb = shared[:, 0:22 * TS].rearrange("p (j t) -> p j t", j=22)

        def k_cpre(j, tt):
            b0 = (j * TS + tt * 512) * 4
            return [("sh", b0 // 1024), ("sh", b0 // 1024 + 1)]

        def k_bf(base_seg_elems, j, tt):
            b0 = (base_seg_elems + j * TS + tt * 512) * 2
            return [("sh", b0 // 1024)]

        def k_un(j, tt):
            return k_bf(16 * TS, j, tt)

        def k_mixed(j, tt):
            return k_bf(0, j, tt)

        def k_act(j, tt):
            return k_bf(0, j, tt)

        identF = cs[:, 0:128]
        mask2 = cs[:, 128:256]
        c_eps_rms = cs[:, 256:257]
        c_eps_ln = cs[:, 257:258]
        c_eps_ln4 = cs[:, 258:259]
        scanmask = cs[:, 320:320 + TS]
        lbp = pp[:, 0:16].rearrange("p (a h) -> p a h", a=2)
        gn = pp[:, 16:17]
        cw = pp[:, 32:32 + 8 * TAPS].rearrange("p (j t) -> p j t", j=8)
        cb = pp[:, 288:296]
        cg = pp[:, 296:304]
        cbb = pp[:, 304:312]

        esems = {(e, ep): es.enter_context(nc.semaphore(f"s_{e}_{ep}"))
                 for e in ("pe", "act", "dve", "pool") for ep in range(NST + 1)}
        dnames = ([f"ws{i}" for i in range(NSLOT)] + [f"r{i}" for i in range(NB)]
                  + [f"o{i}" for i in range(NB)] + [f"xs{i}" for i in range(NB)] + ["c0", "c1", "c2"])
        dsems = {d: es.enter_context(nc.semaphore("d_" + d)) for d in dnames}
        print('sbuf bytes remaining', nc.sbuf_bytes_remaining)
        block = es.enter_context(nc.Block())

        def fsz(ap):
            n = 1
            for d in ap.shape[1:]:
                n *= d
            return n

        def mm(out, lhsT, rhs, start, stop, reads, writes):
            P.add("pe", lambda e: e.matmul(out, lhsT=lhsT, rhs=rhs, start=start, stop=stop), reads, writes,
                  dur=max(64, fsz(rhs)) / 2300.0 + 0.01)

        def tr(out, in_, ident, reads, writes):
            P.add("pe", lambda e: e.transpose(out, in_, ident), reads, writes, dur=0.09)

        def act(out, in_, func, reads, writes, scale=None, bias=None):
            kw = {}
            if scale is not None:
                kw["scale"] = scale
            if bias is not None:
                kw["bias"] = bias
            tbl = {AF.Tanh: "A", AF.Silu: "A", AF.Ln: "B", AF.Sigmoid: "C"}.get(func)
            P.add("act", lambda e: e.activation(out=out, in_=in_, func=func, **kw), reads, writes,
                  dur=0.30 + fsz(out) / 1200.0 + (0.19 if (scale is not None and not isinstance(scale, float)) or
                                                     (bias is not None and not isinstance(bias, float)) else 0.0),
                  tbl=tbl)

        def tt_(out, in0, in1, op, reads, writes, eng="dve"):
            P.add(eng, lambda e: e.tensor_tensor(out=out, in0=in0, in1=in1, op=op), reads, writes,
                  dur=0.40 + fsz(out) / 960.0)

        def ts_(out, in0, s1, s2, op0, op1, reads, writes, eng="dve"):
            P.add(eng, lambda e: e.tensor_scalar(out=out, in0=in0, scalar1=s1, scalar2=s2, op0=op0, op1=op1),
                  reads, writes, dur=0.40 + fsz(out) / 960.0)

        def stt(out, in0, scalar, in1, op0, op1, reads, writes):
            P.add("dve", lambda e: e.scalar_tensor_tensor(out=out, in0=in0, scalar=scalar, in1=in1,
                                                          op0=op0, op1=op1), reads, writes,
                  dur=0.40 + fsz(out) / 960.0)

        def cp(out, in_, reads, writes, eng="dve"):
            if eng == "act":
                P.add("act", lambda e: e.activation(out=out, in_=in_, func=AF.Copy), reads, writes,
                      dur=0.30 + fsz(out) / 1200.0)
            else:
                P.add(eng, lambda e: e.tensor_copy(out=out, in_=in_), reads, writes, dur=0.40 + fsz(out) / 960.0)

        def dma(eng, out, in_, reads, writes, dsem):
            nbytes = 128 * fsz(out) * 4
            P.add(eng, lambda e: e.dma_start(out=out, in_=in_), reads, writes, dsem=dsem,
                  dur=0.1, lat=2.0 + nbytes / 150000.0)

        st = {"ps": 0, "scr": 0, "scb": 0, "slot": 0, "alt": 0, "am": 0}

        ps_free = list(range(6))

        def newps():
            assert ps_free, "out of PSUM banks"
            i = ps_free.pop(0)
            return psb[i], ("ps", i)

        def relps(*keys):
            for k in keys:
                assert k[1] not in ps_free
                ps_free.append(k[1])

        def newscr():
            i = st["scr"]; st["scr"] = (i + 1) % 8
            return scr[i], ("scr", i)

        def newscb():
            i = st["scb"]; st["scb"] = (i + 1) % 4
            return scb[i], ("scb", i)

        def wblock(wd, k0, KC, n0):
            i = st["slot"]; st["slot"] = (i + 1) % NSLOT
            key = ("slot", i)
            src = wd[k0:k0 + KC * 128, n0:n0 + 256].rearrange("(kc p) n -> p kc n", p=128)
            dma("pool", slots[i][:, 0:KC, :], src, [], [key], f"ws{i}")
            return slots[i], key

        def alt_eng():
            st["alt"] ^= 1
            return "act" if st["alt"] else "dve"

        dma("sp", cs[:], cst_d, [], ["cs"], "c0")
        dma("sp", pp[:], pp_d, [], ["pp"], "c1")
        dma("sp", lnt[:], lnp_d, [], ["lnt"], "c2")
        cp(identB[:], identF, ["cs"], ["identB"])
        P.add("dve", lambda e: e.memset(scr[0][:], 0.0), [], [("scr", 0)])
        P.add("dve", lambda e: e.memset(scr[1][:], 1.0 / 128.0), [], [("scr", 1)])
        P.add("dve", lambda e: e.memset(scr[2][:], 1.0 / 1024.0), [], [("scr", 2)])
        cp(ones128[:], scr[1][:, 0:128], [("scr", 1)], ["ones128"])
        cp(ones1024[:], scr[2][:, 0:128], [("scr", 2)], ["ones1024"])
        Tpf = Tprev[:].rearrange("p h d -> p (h d)")
        cp(Tpf[:, 0:512], scr[0][:], [("scr", 0)], [("Tprev", h) for h in range(4)])
        cp(Tpf[:, 512:1024], scr[0][:], [("scr", 0)], [("Tprev", h) for h in range(4, 8)])
        cp(halo[:].rearrange("p j t -> p (j t)"), scr[0][:, 0:256], [("scr", 0)], [("halo", j) for j in range(8)])
        for p_ in range(2):
            for i in range(2):
                for tb in range(0, NB, 4):
                    cp(K_tm[p_][i][:, tb:tb + 4, :].rearrange("p a b -> p (a b)"), scr[0][:], [("scr", 0)],
                       [("Ktm", p_, tb // 4, 0), ("Ktm", p_, tb // 4, 1)])
        tt_(lbt[:, 3, :], lbp[:, 0, :], lbp[:, 1, :], ALU.subtract, ["pp"], ["lbt3"])
        act(lbt[:, 0, :], lbt[:, 3, :], AF.Sigmoid, ["lbt3"], ["lbt0"])
        ts_(lbt[:, 1, :], lbt[:, 0, :], -0.5, 0.5, ALU.mult, ALU.add, ["lbt0"], ["lbt1"])
        ts_(lbt[:, 2, :], lbt[:, 0, :], 0.5, -0.5, ALU.mult, ALU.add, ["lbt0"], ["lbt2"])
        ts_(lbt[:, 3, :], lbt[:, 0, :], 0.5, 0.5, ALU.mult, ALU.add, ["lbt0"], ["lbt3b"])
        ts_(cwh[:], pp[:, 32:32 + 8 * TAPS], 0.5, None, ALU.mult, ALU.bypass, ["pp"], ["cwh"])
        LB = ["lbt3b", "lbt1", "lbt2"]

        def transpose_blk(src, src_keys, tb, xb):
            for g in range(2):
                ps, pk = newps()
                for kk in range(4):
                    kc = g * 4 + kk
                    tr(ps[:, kk * 128:(kk + 1) * 128], src[:, kc * 128:(kc + 1) * 128], identF,
                       list(src_keys) + ["cs"], [pk])
                cp(xTs[xb][:, g * 4:(g + 1) * 4, tb * 128:(tb + 1) * 128],
                   ps[:].rearrange("p (a b) -> p a b", a=4), [pk], [("xT", xb, tb, g)], eng=alt_eng())
                relps(pk)

        def xT_keys(tt):
            return [("xT", st["xb"], tb, g) for tb in range(tt * 4, tt * 4 + 4) for g in range(2)]

        def xstage(tb):
            ap = Fall[:, 2 * tb:2 * tb + 2, :].rearrange("p a t -> p (a t)")
            keys = [(f"F{idx % 4}", idx // 4, 0) for idx in (2 * tb, 2 * tb + 1)]
            return ap, keys

        def prefetch_x_load(sti_):
            for tb in range(NB):
                ap, keys = xstage(tb)
                dma("sp", ap, x_d[sti_ * TS + tb * 128:sti_ * TS + (tb + 1) * 128, :], [], keys, f"xs{tb}")

        def prefetch_x_transpose(sti_):
            for tb in range(NB):
                ap, keys = xstage(tb)
                transpose_blk(ap, keys, tb, sti_ % 2)

        def layer_norm_R(tb, gi, eps_ap=None):
            eps_ap = c_eps_ln if eps_ap is None else eps_ap
            rk = ("R", tb)
            P.add("dve", lambda e: e.bn_stats(out=st6[:, 0:6], in_=R[:, tb, 0:512]), [rk], ["st6a"])
            P.add("dve", lambda e: e.bn_stats(out=st6[:, 6:12], in_=R[:, tb, 512:1024]), [rk], ["st6b"])
            P.add("dve", lambda e: e.bn_aggr(out=mv[:, 0:2], in_=st6[:]), ["st6a", "st6b"], ["mv01"])
            act(mv[:, 2:3], mv[:, 1:2], AF.Ln, ["mv01", "cs"], ["mv2"], bias=eps_ap)
            act(mv[:, 3:4], mv[:, 2:3], AF.Exp, ["mv2"], ["mv3"], scale=-0.5)
            ts_(R[:, tb, :], R[:, tb, :], mv[:, 0:1], mv[:, 3:4], ALU.subtract, ALU.mult,
                [rk, "mv01", "mv3"], [rk])
            tt_(R[:, tb, :], R[:, tb, :], lnt[:, gi, :], ALU.mult, [rk, "lnt"], [rk])
            tt_(R[:, tb, :], R[:, tb, :], lnt[:, gi + 1, :], ALU.add, [rk, "lnt"], [rk])

        for sti in range(NST):
            t0 = sti * TS
            P.epoch = sti + 1
            st["xb"] = sti % 2
            xT = xTs[sti % 2]
            if sti == 0:
                prefetch_x_load(0)
                prefetch_x_transpose(0)
            for tb in range(NB):
                dma("sp", R[:, tb, :], x_d[t0 + tb * 128:t0 + (tb + 1) * 128, :], [], [("R", tb)], f"r{tb}")

            wst = {}

            def head_front(h):
                par = h % 2
                if h % 2 == 0:
                    hp = h // 2
                    for nm, sec in (("q", 0), ("f", 1), ("i", 2), ("o", 3)):
                        wst[nm] = wblock(w_in_d, 0, 8, sec * 1024 + hp * 256)
                (wq, kq), (wf, kf), (wi, ki), (wo, ko) = wst["q"], wst["f"], wst["i"], wst["o"]
                hc = (h % 2) * 128
                F0, F1, F2, F3 = Fb[par]
                allF = lambda n: [(n, par, tt) for tt in range(NT)]
                for tt in range(NT):
                    tsl = slice(tt * 512, (tt + 1) * 512)
                    for (wblk, wk, dst, dk_, fn) in ((wq, kq, F0, ("F0", par, tt), AF.Silu),
                                                     (wf, kf, F1, ("F1", par, tt), AF.Tanh),
                                                     (wo, ko, ogs[par], ("ogs", par, tt), AF.Silu)):
                        ps, pk = newps()
                        for kc in range(8):
                            mm(ps[:], wblk[:, kc, hc:hc + 128], xT[:, kc, tsl], kc == 0, kc == 7,
                               [wk] + xT_keys(tt), [pk])
                        act(dst[:, tsl], ps[:], fn, [pk], [dk_], scale=(0.5 if fn == AF.Tanh else None))
                        relps(pk)
                        yield
                for tg in range(NB // 4):
                    ps, pk = newps()
                    for tl in range(4):
                        tb = tg * 4 + tl
                        for kc in range(8):
                            mm(ps[:, tl * 128:(tl + 1) * 128], xT[:, kc, tb * 128:(tb + 1) * 128],
                               wi[:, kc, hc:hc + 128], kc == 0, kc == 7,
                               [ki, ("xT", st["xb"], tb, 0), ("xT", st["xb"], tb, 1)], [pk])
                    cp(v_tm[par][:, tg * 4:(tg + 1) * 4, :], ps[:].rearrange("p (a b) -> p a b", a=4),
                       [pk], [("vtm", par, tg)])
                    relps(pk)
                    yield
                act(F2[:], F1[:], AF.Ln, allF("F1") + LB, allF("F2"),
                    scale=lbt[:, 1, h:h + 1], bias=lbt[:, 3, h:h + 1])
                ts_(F3[:], F1[:], lbt[:, 2, h:h + 1], lbt[:, 1, h:h + 1], ALU.mult, ALU.add,
                    allF("F1") + LB, allF("F3"))
                yield
                P.add("dve", lambda e, F1=F1, F2=F2: e.tensor_tensor_scan(
                    out=F1[:], data0=scanmask, data1=F2[:], initial=0.0, op0=ALU.mult, op1=ALU.add),
                    allF("F2") + ["cs"], allF("F1"), dur=0.1 + 2 * TS / 960.0)
                yield
                act(F2[:], F1[:], AF.Exp, allF("F1"), allF("F2"))
                yield
                cp(ebuf[par][:], F2[:].rearrange("p (c t) -> p c t", t=64)[:, :, 63], allF("F2"), [("ebuf", par)])
                tt_(Qpp[par][:], F0[:], F2[:], ALU.mult, allF("F0") + allF("F2"), [("Qpp", par)])
                yield
                act(F0[:], F1[:], AF.Exp, allF("F1"), allF("F0"), scale=-1.0)
                yield
                tt_(Kt[par][:], F3[:], F0[:], ALU.mult, allF("F3") + allF("F0"), [("Kt", par)])
                yield

            def head_back(h):
                par = h % 2
                Ktp, Qp, vt, Tbp, eb, KV = Kt[par], Qpp[par], v_tm[par], Tb[par], ebuf[par], KVs[par]
                for tg in range(NB // 4):
                    ps, pk = newps()
                    psv = ps[:].bitcast(BF16)
                    for tl in range(4):
                        tb = tg * 4 + tl
                        tr(psv[:, tl * 128:(tl + 1) * 128], Ktp[:, tb * 128:(tb + 1) * 128], identB[:],
                           [("Kt", par), "identB"], [pk])
                    for hf in range(2):
                        cp(K_tm[par][hf][hf * 64:hf * 64 + 64, tg * 4:(tg + 1) * 4, :],
                           psv[hf * 64:hf * 64 + 64, 0:512].rearrange("p (a b) -> p a b", a=4),
                           [pk], [("Ktm", par, tg, hf)], eng=("act" if hf == 0 else "dve"))
                    relps(pk)
                    yield
                for cg_ in range(NCH // 4):
                    ps, pk = newps()
                    for cl in range(4):
                        c = cg_ * 4 + cl
                        tb, hf = c // 2, c % 2
                        mm(ps[:, cl * 128:(cl + 1) * 128], K_tm[par][hf][:, tb, :], vt[:, tb, :], True, True,
                           [("Ktm", par, tb // 4, hf), ("vtm", par, tb // 4)], [pk])
                    tt_(KV[:, cg_ * 4:(cg_ + 1) * 4, :], ps[:].rearrange("p (a b) -> p a b", a=4),
                        eb[:, cg_ * 4:(cg_ + 1) * 4].unsqueeze(2).broadcast_to([128, 4, 128]), ALU.mult,
                        [pk, ("ebuf", par)], [("KVs", par, cg_)])
                    relps(pk)
                    yield
                cp(Tbp[:, 0, :], Tprev[:, h, :], [("Tprev", h)], [("Tb", par, 0)])
                for c in range(NCH):
                    stt(Tbp[:, c + 1, :], Tbp[:, c, :], eb[:, c:c + 1], KV[:, c, :], ALU.mult, ALU.add,
                        [("Tb", par, c), ("ebuf", par), ("KVs", par, c // 4)], [("Tb", par, c + 1)])
                    if c % 2 == 1:
                        yield
                cp(Tprev[:, h, :], Tbp[:, NCH, :], [("Tb", par, NCH)], [("Tprev", h)], eng="act")
                yield
                yield
                for tt in range(NT):
                    tsl = slice(tt * 512, (tt + 1) * 512)
                    pA, pAk = newps()
                    for tbl in range(4):
                        tb = tt * 4 + tbl
                        mm(pA[:, tbl * 128:(tbl + 1) * 128], Ktp[:, tb * 128:(tb + 1) * 128],
                           Qp[:, tb * 128:(tb + 1) * 128], True, True, [("Kt", par), ("Qpp", par)], [pAk])
                    st["am"] ^= 1
                    am = Am[st["am"]]
                    amk = ("Am", st["am"])
                    tt_(am[:], pA[:].rearrange("p (a b) -> p a b", a=4),
                        mask2.unsqueeze(1).broadcast_to([128, 4, 128]), ALU.mult, [pAk, "cs"], [amk])
                    relps(pAk)
                    yield
                    pO, pOk = newps()
                    for tbl in range(4):
                        tb = tt * 4 + tbl
                        mm(pO[:, tbl * 128:(tbl + 1) * 128], vt[:, tb, :], am[:, tbl, :], True, False,
                           [("vtm", par, tb // 4), amk], [pOk])
                        for hf in range(2):
                            c = 2 * tb + hf
                            mm(pO[:, tbl * 128 + hf * 64:tbl * 128 + hf * 64 + 64], Tbp[:, c, :],
                               Qp[:, c * 64:(c + 1) * 64], False, hf == 1, [("Tb", par, c), ("Qpp", par)], [pOk])
                    osq, osk = newscb()
                    act(osq[:], pO[:], AF.Square, [pOk], [osk])
                    yield
                    pM, pMk = newps()
                    mm(pM[:], ones128[:], osq[:], True, True, ["ones128", osk], [pMk])
                    lnv, lnk = newscr()
                    act(lnv[:], pM[:], AF.Ln, [pMk, "cs"], [lnk], bias=c_eps_rms)
                    relps(pMk)
                    rstd, rsk = newscr()
                    act(rstd[:], lnv[:], AF.Exp, [lnk], [rsk], scale=-0.5)
                    t1, t1k = newscr()
                    stt(t1[:], pO[:], gn, rstd[:], ALU.mult, ALU.mult, [pOk, "pp", rsk], [t1k])
                    relps(pOk)
                    tt_(on[:, h, tsl], t1[:], ogs[par][:, tsl], ALU.mult, [t1k, ("ogs", par, tt)], [("on", h, tt)])
                    yield

            def conv_front(j):
                par = j % 2
                if j % 2 == 0:
                    wst["gv"] = wblock(w_in_d, 0, 8, 4096 + (j // 2) * 256)
                    wst["gg"] = wblock(w_in_d, 0, 8, 5120 + (j // 2) * 256)
                (wgv, kgv), (wgg, kgg) = wst["gv"], wst["gg"]
                jc = (j % 2) * 128
                ub = ubuf[par]
                cp(ub[:, 0:32], halo[:, j, :], [("halo", j)], [("ubuf_h", par)], eng="act")
                for tt in range(NT):
                    psv_, pvk = newps()
                    psg, pgk = newps()
                    tsl = slice(tt * 512, (tt + 1) * 512)
                    for kc in range(8):
                        mm(psv_[:], wgv[:, kc, jc:jc + 128], xT[:, kc, tsl], kc == 0, kc == 7,
                           [kgv] + xT_keys(tt), [pvk])
                        if kc % 2 == 1:
                            yield
                    for kc in range(8):
                        mm(psg[:], wgg[:, kc, jc:jc + 128], xT[:, kc, tsl], kc == 0, kc == 7,
                           [kgg] + xT_keys(tt), [pgk])
                        if kc % 2 == 1:
                            yield
                    sg_, sgk = newscr()
                    act(sg_[:], psg[:], AF.Tanh, [pgk], [sgk], scale=0.5)
                    stt(ub[:, 32 + tt * 512:32 + (tt + 1) * 512], sg_[:], 1.0, psv_[:], ALU.add, ALU.mult,
                        [pvk, sgk], [("ubuf", par, tt)])
                    relps(pvk, pgk)
                cp(halo[:, j, :], ub[:, TS:TS + 32], [("ubuf", par, NT - 1)], [("halo", j)], eng="act")
                for t0_, t1_ in ((0, 8), (8, 16), (16, 24), (24, TAPS)):
                    nt_ = t1_ - t0_
                    tt_(Dg[par][:, t0_:t1_, :], identB[:].unsqueeze(1).broadcast_to([128, nt_, 128]),
                        cwh[:, j * TAPS + t0_:j * TAPS + t1_].unsqueeze(2).broadcast_to([128, nt_, 128]), ALU.mult,
                        ["identB", "cwh"], [("Dg", par, t0_ // 8)])
                    yield

            def conv_back(j):
                par = j % 2
                ub = ubuf[par]
                for tt in range(NT):
                    pc, pck = psb[6 + par], ("ps", 6 + par)
                    ur = [("ubuf_h", par)] + [("ubuf", par, t_) for t_ in range(tt + 1)]
                    for tap in range(TAPS):
                        off = 2 + tt * 512 + tap
                        mm(pc[:], Dg[par][:, tap, :], ub[:, off:off + 512], tap == 0, tap == TAPS - 1,
                           [("Dg", par, tap // 8)] + ur, [pck])
                        if tap % 3 == 2:
                            yield
                    act(cpre[:, j, tt * 512:(tt + 1) * 512], pc[:], AF.Identity, [pck, "pp"], k_cpre(j, tt),
                        bias=cb[:, j:j + 1])
                    yield

            def merged(gens):
                gens = list(gens)
                while gens:
                    for g in list(gens):
                        try:
                            next(g)
                        except StopIteration:
                            gens.remove(g)
                            continue
                        yield

            def thread(front, back, n):
                yield from front(0)
                for i in range(n):
                    gs = [back(i)]
                    if i + 1 < n:
                        gs.append(front(i + 1))
                    yield from merged(gs)

            threads = []
            if dbg >= 3:
                threads.append(thread(head_front, head_back, NH))
            if dbg >= 4:
                threads.append(thread(conv_front, conv_back, 8))
            while threads:
                for th in list(threads):
                    try:
                        next(th)
                    except StopIteration:
                        threads.remove(th)

            for tt in range(NT if dbg >= 4 else 0):
                tsl = slice(tt * 512, (tt + 1) * 512)
                pS1, pS1k = newps()
                pS2, pS2k = newps()
                for j in range(8):
                    cbf, cbk = newscb()
                    csq, csk = newscb()
                    cp(cbf[:], cpre[:, j, tsl], k_cpre(j, tt), [cbk])
                    act(csq[:], cpre[:, j, tsl], AF.Square, k_cpre(j, tt), [csk])
                    mm(pS1[:], ones1024[:], cbf[:], j == 0, j == 7, ["ones1024", cbk], [pS1k])
                    mm(pS2[:], ones1024[:], csq[:], j == 0, j == 7, ["ones1024", csk], [pS2k])
                mean, mk_ = cmean, "cmean"
                cp(mean[:], pS1[:], [pS1k], [mk_], eng="act")
                relps(pS1k)
                msq, msk = newscr()
                tt_(msq[:], mean[:], mean[:], ALU.mult, [mk_], [msk])
                var, vk = newscr()
                tt_(var[:], pS2[:], msq[:], ALU.subtract, [pS2k, msk], [vk])
                relps(pS2k)
                lnv, lnk = newscr()
                act(lnv[:], var[:], AF.Ln, [vk, "cs"], [lnk], bias=c_eps_ln)
                rstd, rsk = crstd, "crstd"
                act(rstd[:], lnv[:], AF.Exp, [lnk], [rsk], scale=-0.5)
                for j in range(8):
                    ta, tak = newscr()
                    tt_(ta[:], cpre[:, j, tsl], mean[:], ALU.subtract, k_cpre(j, tt) + [mk_], [tak])
                    t2, t2k = newscr()
                    tt_(t2[:], ta[:], rstd[:], ALU.mult, [tak, rsk], [t2k])
                    act(un[:, j, tsl], t2[:], AF.Silu, [t2k, "pp"], k_un(j, tt),
                        scale=cg[:, j:j + 1], bias=cbb[:, j:j + 1])

            for j in range(8 if dbg >= 5 else 0):
                if j % 2 == 0:
                    wha, kha = wblock(w_hg_d, 0, 8, (j // 2) * 256)
                    wcb, kcb = wblock(w_cv_d, 0, 8, (j // 2) * 256)
                    wga, kga = wblock(w_in_d, 0, 8, 6144 + (j // 2) * 256)
                    wgb, kgb = wblock(w_in_d, 0, 8, 7168 + (j // 2) * 256)
                jc = (j % 2) * 128
                for tt in range(NT):
                    tsl = slice(tt * 512, (tt + 1) * 512)
                    pya, pyak = newps()
                    pyb, pybk = newps()
                    pga, pgak = newps()
                    pgb, pgbk = newps()
                    for kc in range(8):
                        mm(pga[:], wga[:, kc, jc:jc + 128], xT[:, kc, tsl], kc == 0, kc == 7,
                           [kga] + xT_keys(tt), [pgak])
                    for kc in range(8):
                        mm(pgb[:], wgb[:, kc, jc:jc + 128], xT[:, kc, tsl], kc == 0, kc == 7,
                           [kgb] + xT_keys(tt), [pgbk])
                    for kc in range(8):
                        mm(pya[:], wha[:, kc, jc:jc + 128], on[:, kc, tsl], kc == 0, kc == 7,
                           [kha, ("on", kc, tt)], [pyak])
                    for kc in range(8):
                        mm(pyb[:], wcb[:, kc, jc:jc + 128], un[:, kc, tsl], kc == 0, kc == 7,
                           [kcb] + k_un(kc, tt), [pybk])
                    sa, sak = newscr()
                    act(sa[:], pga[:], AF.Tanh, [pgak], [sak], scale=0.5)
                    sb_, sbk = newscr()
                    act(sb_[:], pgb[:], AF.Tanh, [pgbk], [sbk], scale=0.5)
                    m1, m1k = newscr()
                    stt(m1[:], sa[:], 1.0, pya[:], ALU.add, ALU.mult, [pyak, sak], [m1k])
                    m2, m2k = newscr()
                    stt(m2[:], sb_[:], 1.0, pyb[:], ALU.add, ALU.mult, [pybk, sbk], [m2k])
                    relps(pyak, pybk, pgak, pgbk)
                    tt_(mixed[:, j, tsl], m1[:], m2[:], ALU.add, [m1k, m2k], k_mixed(j, tt))

            wob = [wblock(w_out_d, 0, 8, nb * 256) for nb in range(4)] if dbg >= 6 else []
            for tb in range(NB if dbg >= 6 else 0):
                tt = tb // 4
                for half in range(2):
                    ps, pk = newps()
                    for q in range(2):
                        wblk, wk = wob[half * 2 + q]
                        for kc in range(8):
                            mm(ps[:, q * 256:(q + 1) * 256], mixed[:, kc, tb * 128:(tb + 1) * 128], wblk[:, kc, :],
                               kc == 0, kc == 7, [wk] + k_mixed(kc, tt), [pk])
                    stt(R[:, tb, half * 512:(half + 1) * 512], R[:, tb, half * 512:(half + 1) * 512], 2.0 * ALPHA, ps[:],
                        ALU.mult, ALU.add, [("R", tb), pk], [("R", tb)])
                    relps(pk)
            for tb in range(NB if dbg >= 6 else 0):
                layer_norm_R(tb, 0, c_eps_ln4)
            for tb in range(NB if dbg >= 6 else 0):
                transpose_blk(R[:, tb, :], [("R", tb)], tb, sti % 2)

            if sti + 1 < NST:
                prefetch_x_load(sti + 1)
            for jj in range(22 if dbg >= 8 else 0):
                if jj == 12 and sti + 1 < NST:
                    prefetch_x_transpose(sti + 1)
                if jj % 2 == 0:
                    wg_, kg_ = wblock(w_f1_d, 0, 8, (jj // 2) * 256)
                    wu_, ku_ = wblock(w_f1_d, 0, 8, FFN + (jj // 2) * 256)
                jc = (jj % 2) * 128
                for tt in range(NT):
                    tsl = slice(tt * 512, (tt + 1) * 512)
                    pg_, pgk_ = newps()
                    pu_, puk_ = newps()
                    for kc in range(8):
                        mm(pg_[:], wg_[:, kc, jc:jc + 128], xT[:, kc, tsl], kc == 0, kc == 7,
                           [kg_] + xT_keys(tt), [pgk_])
                    for kc in range(8):
                        mm(pu_[:], wu_[:, kc, jc:jc + 128], xT[:, kc, tsl], kc == 0, kc == 7,
                           [ku_] + xT_keys(tt), [puk_])
                    sg_, sgk = newscr()
                    act(sg_[:], pg_[:], AF.Silu, [pgk_], [sgk])
                    tt_(actb[:, jj, tsl], sg_[:], pu_[:], ALU.mult, [sgk, puk_], k_act(jj, tt))
                    relps(pgk_, puk_)

            for nb in range(4 if dbg >= 9 else 0):
                wfb = [wblock(w_f2_d, g * 1024, (8 if g < 2 else 6), nb * 256) for g in range(3)]
                for tb in range(NB):
                    tt = tb // 4
                    ps, pk = newps()
                    for kc in range(22):
                        wblk, wk = wfb[kc // 8]
                        mm(ps[:, 0:256], actb[:, kc, tb * 128:(tb + 1) * 128], wblk[:, kc % 8, :],
                           kc == 0, kc == 21, [wk] + k_act(kc, tt), [pk])
                    stt(R[:, tb, nb * 256:(nb + 1) * 256], R[:, tb, nb * 256:(nb + 1) * 256], ALPHA, ps[:, 0:256],
                        ALU.mult, ALU.add, [("R", tb), pk], [("R", tb)])
                    relps(pk)
                    if nb == 3:
                        layer_norm_R(tb, 2)
                        dma("sp", out_d[t0 + tb * 128:t0 + (tb + 1) * 128, :], R[:, tb, :], [("R", tb)], [], f"o{tb}")
            if dbg < 9:
                for tb in range(NB):
                    dma("sp", out_d[t0 + tb * 128:t0 + (tb + 1) * 128, :], R[:, tb, :], [("R", tb)], [], f"o{tb}")

        P.add("sp", None)
        fin = P.ops[-1]
        if LIST_SCHED:
            P.list_schedule(SCHED_MODE)
            print("list schedule: estimated makespan %.0f us" % P.est_makespan)
        P.finalize()
        fin["waits"] = [(("d", f"o{tb}"), P.dsem_count[f"o{tb}"]) for tb in range(NB)]
        P.emit(block, esems, dsems)
    return nc


PP_N = 320


def CST_N(TS):
    return 320 + TS


def make_consts(TS):
    c = np.zeros((128, CST_N(TS)), np.float32)
    c[:, 0:128] = np.eye(128, dtype=np.float32)
    p = np.arange(128)[:, None]
    t = np.arange(128)[None, :]
    c[:, 128:256] = ((p // 64 == t // 64) & (p % 64 <= t % 64)).astype(np.float32)
    c[:, 256] = RMS_EPS
    c[:, 257] = LN_EPS
    c[:, 258] = 4.0 * LN_EPS
    m = np.ones(TS, np.float32)
    m[::64] = 0.0
    c[:, 320:320 + TS] = m
    return c


def pack_params(lb_param, hg_norm_g, conv_w, conv_b, conv_ln_g, conv_ln_b):
    pp = np.zeros((128, PP_N), np.float32)
    pp[:, 0:16] = lb_param.reshape(2, NH, 128).transpose(2, 0, 1).reshape(128, 16)
    pp[:, 16] = hg_norm_g.reshape(128)
    pp[:, 32:32 + 8 * TAPS] = conv_w.reshape(TAPS, 8, 128).transpose(2, 1, 0).reshape(128, 8 * TAPS)
    pp[:, 288:296] = conv_b.reshape(8, 128).T
    pp[:, 296:304] = conv_ln_g.reshape(8, 128).T
    pp[:, 304:312] = conv_ln_b.reshape(8, 128).T
    return pp


def make_in_maps(x, w_in, lb_param, hg_norm_g, w_hg_out, conv_w, conv_b, conv_ln_g, conv_ln_b,
                 w_conv_out, w_out, ln1_g, ln1_b, w_ffn_in, w_ffn_out, ln2_g, ln2_b, TS=512):
    f = lambda a: np.ascontiguousarray(np.asarray(a, dtype=np.float32))
    B = x.shape[0]
    pp = pack_params(f(lb_param), f(hg_norm_g)[0], f(conv_w)[0], f(conv_b)[0], f(conv_ln_g)[0], f(conv_ln_b)[0])
    lnp = np.ascontiguousarray(np.broadcast_to(
        np.stack([f(ln1_g)[0], f(ln1_b)[0], f(ln2_g)[0], f(ln2_b)[0]])[None], (128, 4, D)))
    shared = {
        "w_in": f(w_in)[0], "w_hg_out": f(w_hg_out)[0], "w_conv_out": f(w_conv_out)[0], "w_out": f(w_out)[0],
        "w_ffn_in": f(w_ffn_in)[0], "w_ffn_out": f(w_ffn_out)[0], "pp": pp, "lnp": lnp, "cst": make_consts(TS),
    }
    xs = f(x)
    return [dict(shared, x=xs[b]) for b in range(B)]


def kernel(**inputs):
    TS = 512
    in_maps = make_in_maps(TS=TS, **inputs)
    nc = build_nc(SEQ, TS=TS)
    res = run_bass_kernel_spmd(nc, in_maps, core_ids=list(range(N_CORES)))
    return np.stack([np.asarray(r["out"], dtype=np.float32) for r in res.results], axis=0)
```

```python
import numpy as np
from contextlib import ExitStack
import concourse.bass as bass
import concourse.mybir as mybir
from concourse.bass_utils import run_bass_kernel_spmd

F32 = mybir.dt.float32
BF16 = mybir.dt.bfloat16
AF = mybir.ActivationFunctionType
ALU = mybir.AluOpType

D = 1024
NH = 8
FFN = 2816
TAPS = 31
IN_W = 8192
ALPHA = 2.0 ** 0.25
LN_EPS = 1e-5
RMS_EPS = LN_EPS * 128.0
N_CORES = 8
SEQ = 4096


class Prog:
    ENGS = ("pe", "act", "dve", "pool", "sp")
    SAME_SYNC = ("act", "dve", "pool")

    def __init__(self):
        self.ops = []
        self.lastw = {}
        self.readers = {}
        self.dsem_count = {}
        self.epoch = 0

    def add(self, eng, fn, reads=(), writes=(), dsem=None, dur=None, lat=0.0, tbl=None):
        i = len(self.ops)
        deps = set()
        for r in reads:
            w = self.lastw.get(r)
            if w is not None:
                deps.add(w)
        for w_ in writes:
            lw = self.lastw.get(w_)
            if lw is not None:
                deps.add(lw)
            for rd in self.readers.get(w_, ()):
                deps.add(rd)
        for r in reads:
            self.readers.setdefault(r, []).append(i)
        for w_ in writes:
            self.lastw[w_] = i
            self.readers[w_] = []
        op = dict(eng=eng, fn=fn, deps=deps, dsem=dsem, sig=False, dval=None, ep=self.epoch,
                  dur=(0.3 if dur is None else dur), lat=lat, tbl=tbl)
        if dsem is not None:
            self.dsem_count[dsem] = self.dsem_count.get(dsem, 0) + 16
            op["dval"] = self.dsem_count[dsem]
        self.ops.append(op)
        return i

    def list_schedule(self, mode="blevel"):
        ops = self.ops
        n = len(ops)
        body = [i for i in range(n) if ops[i]["fn"] is not None]
        tailops = [i for i in range(n) if ops[i]["fn"] is None]
        succ = [[] for _ in range(n)]
        indeg = [0] * n
        for i in body:
            for d in ops[i]["deps"]:
                succ[d].append(i)
                indeg[i] += 1
        last_d = {}
        for i in body:
            k = ops[i]["dsem"]
            if k is not None:
                if k in last_d and last_d[k] not in ops[i]["deps"]:
                    succ[last_d[k]].append(i)
                    indeg[i] += 1
                last_d[k] = i
        blev = [0.0] * n
        for i in reversed(body):
            m = 0.0
            for j in succ[i]:
                if blev[j] > m:
                    m = blev[j]
            blev[i] = m + ops[i]["dur"] + ops[i]["lat"] + (DMA_BOOST if ops[i]["dsem"] is not None else 0.0)
        finish = [0.0] * n
        ready_t = [0.0] * n
        eng_free = {e: 0.0 for e in self.ENGS}
        act_tbl = [None]
        ready = {e: [] for e in self.ENGS}
        for i in body:
            if indeg[i] == 0:
                ready[ops[i]["eng"]].append(i)
        order = []
        left = len(body)
        while left:
            best = None
            for e in self.ENGS:
                lst = ready[e]
                if not lst:
                    continue
                t_e = max(eng_free[e], min(ready_t[i] for i in lst))
                if best is None or t_e < best[0]:
                    best = (t_e, e)
            t_e, e = best
            lst = ready[e]
            cands = [i for i in lst if ready_t[i] <= t_e + 1e-9]
            if mode == "blevel":
                if e == "act":
                    cur = act_tbl[0]
                    i = max(cands, key=lambda q: (blev[q] - (1.3 if (ops[q]["tbl"] not in (None, cur)) else 0.0), -q))
                else:
                    i = max(cands, key=lambda q: (blev[q], -q))
            else:
                i = min(cands)
            lst.remove(i)
            op = ops[i]
            dur = op["dur"]
            if e == "act" and op["tbl"] is not None and op["tbl"] != act_tbl[0]:
                dur += 1.3
                act_tbl[0] = op["tbl"]
            eng_free[e] = t_e + dur
            finish[i] = t_e + dur + op["lat"]
            order.append(i)
            left -= 1
            for j in succ[i]:
                indeg[j] -= 1
                lat = 0.0 if ops[j]["eng"] == e and op["dsem"] is None else XLAT
                if finish[i] + lat > ready_t[j]:
                    ready_t[j] = finish[i] + lat
                if indeg[j] == 0:
                    ready[ops[j]["eng"]].append(j)
        order += tailops
        remap = {old: new for new, old in enumerate(order)}
        newops = [ops[i] for i in order]
        for op in newops:
            op["deps"] = {remap[d] for d in op["deps"]}
        self.ops = newops
        self.est_makespan = max(finish) if finish else 0.0

    def finalize(self):
        ops = self.ops
        for op in ops:
            keep = set()
            for d in op["deps"]:
                dop = ops[d]
                if dop["dsem"] is not None:
                    keep.add(d)
                elif dop["eng"] != op["eng"] or op["eng"] in self.SAME_SYNC:
                    dop["sig"] = True
                    keep.add(d)
            op["deps"] = keep
        cnt = {}
        for op in ops:
            if op["dsem"] is None and op["sig"]:
                k = (op["eng"], op["ep"])
                cnt[k] = cnt.get(k, 0) + 1
                op["sval"] = cnt[k]
        self.max_sval = max(cnt.values()) if cnt else 0
        waited = {e: {} for e in self.ENGS}
        for op in ops:
            need = {}
            for d in op["deps"]:
                dop = ops[d]
                if dop["dsem"] is not None:
                    key, val = ("d", dop["dsem"]), dop["dval"]
                else:
                    key, val = ("e", (dop["eng"], dop["ep"])), dop["sval"]
                if val > need.get(key, 0):
                    need[key] = val
            w = waited[op["eng"]]
            waits = []
            for key, val in need.items():
                if val > w.get(key, 0):
                    w[key] = val
                    waits.append((key, val))
            op["waits"] = waits

    def emit(self, block, esems, dsems):
        handles = {"pe": block.tensor, "act": block.scalar, "dve": block.vector,
                   "pool": block.gpsimd, "sp": block.sync}
        for ename in self.ENGS:
            myops = [op for op in self.ops if op["eng"] == ename]
            if not myops:
                continue

            def body(eng, myops=myops, ename=ename):
                for op in myops:
                    for (kind, k), val in op["waits"]:
                        eng.wait_ge(dsems[k] if kind == "d" else esems[k], val)
                    if op["fn"] is None:
                        continue
                    inst = op["fn"](eng)
                    if op["dsem"] is not None:
                        inst.then_inc(dsems[op["dsem"]], 16)
                    elif op["sig"]:
                        inst.then_inc(esems[(ename, op["ep"])], 1)

            handles[ename](body)


LIST_SCHED = True
XLAT = 0.45
DMA_BOOST = 0.0
SCHED_MODE = "blevel"


def build_nc(S, TS=512, NSLOT=8, dbg=99):
    NST = S // TS
    NT = TS // 512
    NB = TS // 128
    NCH = TS // 64
    assert S % TS == 0 and TS % 512 == 0

    nc = bass.Bass("TRN2", target_bir_lowering=False)

    def din(name, shape):
        return nc.dram_tensor(name, shape, F32, kind="ExternalInput").ap()

    x_d = din("x", [S, D])
    w_in_d = din("w_in", [D, IN_W])
    w_hg_d = din("w_hg_out", [D, D])
    w_cv_d = din("w_conv_out", [D, D])
    w_out_d = din("w_out", [D, D])
    w_f1_d = din("w_ffn_in", [D, 2 * FFN])
    w_f2_d = din("w_ffn_out", [FFN, D])
    pp_d = din("pp", [128, PP_N])
    lnp_d = din("lnp", [128, 4, D])
    cst_d = din("cst", [128, CST_N(TS)])
    out_d = nc.dram_tensor("out", [S, D], F32, kind="ExternalOutput").ap()

    P = Prog()
    es = ExitStack()
    with es:
        def sb(name, shape, dt):
            return es.enter_context(nc.sbuf_tensor("sb_" + name, shape, dt))

        R = sb("R", [128, NB, D], F32)
        xTs = [sb(f"xT{i}", [128, 8, TS], BF16) for i in range(2)]
        on = sb("on", [128, 8, TS], BF16)
        shared = sb("shared", [128, 24 * TS], BF16)
        slots = [sb(f"slot{i}", [128, 8, 256], BF16) for i in range(NSLOT)]
        lnt = sb("lnt", [128, 4, D], F32)
        cs = sb("cs", [128, CST_N(TS)], F32)
        pp = sb("pp", [128, PP_N], F32)
        Fall = sb("Fall", [128, 8, TS], F32)
        Fb = [[Fall[:, p * 4 + i, :] for i in range(4)] for p in range(2)]
        Qpp = [sb(f"Qpp{p}", [128, TS], BF16) for p in range(2)]
        Kt = [sb(f"Kt{p}", [128, TS], BF16) for p in range(2)]
        ogs = [sb(f"ogs{p}", [128, TS], BF16) for p in range(2)]
        v_tm = [sb(f"v_tm{p}", [128, NB, 128], BF16) for p in range(2)]
        K_tm = [[sb(f"K_tm{p}_{i}", [128, NB, 128], BF16) for i in range(2)] for p in range(2)]
        KVs = [sb(f"KVs{p}", [128, NCH, 128], F32) for p in range(2)]
        Tb = [sb(f"Tb{p}", [128, NCH + 1, 128], BF16) for p in range(2)]
        ebuf = [sb(f"ebuf{p}", [128, NCH], F32) for p in range(2)]
        Am = [sb(f"Am{i}", [128, 4, 128], BF16) for i in range(2)]
        Tprev = sb("Tprev", [128, NH, 128], BF16)
        halo = sb("halo", [128, 8, 32], BF16)
        ubuf = [sb(f"ubuf{p}", [128, 32 + TS], BF16) for p in range(2)]
        Dg = [sb(f"Dg{p}", [128, TAPS, 128], BF16) for p in range(2)]
        scr = [sb(f"scr{i}", [128, 512], F32) for i in range(8)]
        scb = [sb(f"scb{i}", [128, 512], BF16) for i in range(4)]
        identB = sb("identB", [128, 128], BF16)
        ones128 = sb("ones128", [128, 128], BF16)
        ones1024 = sb("ones1024", [128, 128], BF16)
        lbt = sb("lbt", [128, 4, NH], F32)
        cwh = sb("cwh", [128, 8 * TAPS], F32)
        st6 = sb("st6", [128, 12], F32)
        mv = sb("mv", [128, 4], F32)
        cmean = sb("cmean", [128, 512], F32)
        crstd = sb("crstd", [128, 512], F32)
        psb = [es.enter_context(nc.psum_tensor(f"psb{i}", [128, 512], F32)) for i in range(8)]

        cpre = shared[:, 0:16 * TS].bitcast(F32).rearrange("p (j t) -> p j t", j=8)
        un = shared[:, 16 * TS:24 * TS].rearrange("p (j t) -> p j t", j=8)
        mixed = shared[:, 0:8 * TS].rearrange("p (j t) -> p j t", j=8)
        actb = shared[:, 0:22 * TS].rearrange("p (j t) -> p j t", j=22)

        def k_cpre(j, tt):
            b0 = (j * TS + tt * 512) * 4
            return [("sh", b0 // 1024), ("sh", b0 // 1024 + 1)]

        def k_bf(base_seg_elems, j, tt):
            b0 = (base_seg_elems + j * TS + tt * 512) * 2
            return [("sh", b0 // 1024)]

        def k_un(j, tt):
            return k_bf(16 * TS, j, tt)

        def k_mixed(j, tt):
            return k_bf(0, j, tt)

        def k_act(j, tt):
            return k_bf(0, j, tt)

        identF = cs[:, 0:128]
        mask2 = cs[:, 128:256]
        c_eps_rms = cs[:, 256:257]
        c_eps_ln = cs[:, 257:258]
        c_eps_ln4 = cs[:, 258:259]
        scanmask = cs[:, 320:320 + TS]
        lbp = pp[:, 0:16].rearrange("p (a h) -> p a h", a=2)
        gn = pp[:, 16:17]
        cw = pp[:, 32:32 + 8 * TAPS].rearrange("p (j t) -> p j t", j=8)
        cb = pp[:, 288:296]
        cg = pp[:, 296:304]
        cbb = pp[:, 304:312]

        esems = {(e, ep): es.enter_context(nc.semaphore(f"s_{e}_{ep}"))
                 for e in ("pe", "act", "dve", "pool") for ep in range(NST + 1)}
        dnames = ([f"ws{i}" for i in range(NSLOT)] + [f"r{i}" for i in range(NB)]
                  + [f"o{i}" for i in range(NB)] + [f"xs{i}" for i in range(NB)] + ["c0", "c1", "c2"])
        dsems = {d: es.enter_context(nc.semaphore("d_" + d)) for d in dnames}
        print('sbuf bytes remaining', nc.sbuf_bytes_remaining)
        block = es.enter_context(nc.Block())

        def fsz(ap):
            n = 1
            for d in ap.shape[1:]:
                n *= d
            return n

        def mm(out, lhsT, rhs, start, stop, reads, writes):
            P.add("pe", lambda e: e.matmul(out, lhsT=lhsT, rhs=rhs, start=start, stop=stop), reads, writes,
                  dur=max(64, fsz(rhs)) / 2300.0 + 0.01)

        def tr(out, in_, ident, reads, writes):
            P.add("pe", lambda e: e.transpose(out, in_, ident), reads, writes, dur=0.09)

        def act(out, in_, func, reads, writes, scale=None, bias=None):
            kw = {}
            if scale is not None:
                kw["scale"] = scale
            if bias is not None:
                kw["bias"] = bias
            tbl = {AF.Tanh: "A", AF.Silu: "A", AF.Ln: "B", AF.Sigmoid: "C"}.get(func)
            P.add("act", lambda e: e.activation(out=out, in_=in_, func=func, **kw), reads, writes,
                  dur=0.22 + fsz(out) / 1200.0 + (0.19 if (scale is not None and not isinstance(scale, float)) or
                                                     (bias is not None and not isinstance(bias, float)) else 0.0),
                  tbl=tbl)

        def tt_(out, in0, in1, op, reads, writes, eng="dve"):
            P.add(eng, lambda e: e.tensor_tensor(out=out, in0=in0, in1=in1, op=op), reads, writes,
                  dur=0.30 + fsz(out) / 960.0)

        def ts_(out, in0, s1, s2, op0, op1, reads, writes, eng="dve"):
            P.add(eng, lambda e: e.tensor_scalar(out=out, in0=in0, scalar1=s1, scalar2=s2, op0=op0, op1=op1),
                  reads, writes, dur=0.30 + fsz(out) / 960.0)

        def stt(out, in0, scalar, in1, op0, op1, reads, writes):
            P.add("dve", lambda e: e.scalar_tensor_tensor(out=out, in0=in0, scalar=scalar, in1=in1,
                                                          op0=op0, op1=op1), reads, writes,
                  dur=0.30 + fsz(out) / 960.0)

        def cp(out, in_, reads, writes, eng="dve"):
            if eng == "act":
                P.add("act", lambda e: e.activation(out=out, in_=in_, func=AF.Copy), reads, writes,
                      dur=0.22 + fsz(out) / 1200.0)
            else:
                P.add(eng, lambda e: e.tensor_copy(out=out, in_=in_), reads, writes, dur=0.30 + fsz(out) / 960.0)

        def dma(eng, out, in_, reads, writes, dsem):
            nbytes = 128 * fsz(out) * 4
            P.add(eng, lambda e: e.dma_start(out=out, in_=in_), reads, writes, dsem=dsem,
                  dur=0.1, lat=2.0 + nbytes / 150000.0)

        st = {"ps": 0, "scr": 0, "scb": 0, "slot": 0, "alt": 0, "am": 0}

        ps_free = list(range(6))

        def newps():
            assert ps_free, "out of PSUM banks"
            i = ps_free.pop(0)
            return psb[i], ("ps", i)

        def relps(*keys):
            for k in keys:
                assert k[1] not in ps_free
                ps_free.append(k[1])

        def newscr():
            i = st["scr"]; st["scr"] = (i + 1) % 8
            return scr[i], ("scr", i)

        def newscb():
            i = st["scb"]; st["scb"] = (i + 1) % 4
            return scb[i], ("scb", i)

        def wblock(wd, k0, KC, n0):
            i = st["slot"]; st["slot"] = (i + 1) % NSLOT
            key = ("slot", i)
            src = wd[k0:k0 + KC * 128, n0:n0 + 256].rearrange("(kc p) n -> p kc n", p=128)
            dma("pool", slots[i][:, 0:KC, :], src, [], [key], f"ws{i}")
            return slots[i], key

        def alt_eng():
            st["alt"] ^= 1
            return "act" if st["alt"] else "dve"

        dma("sp", cs[:], cst_d, [], ["cs"], "c0")
        dma("sp", pp[:], pp_d, [], ["pp"], "c1")
        dma("sp", lnt[:], lnp_d, [], ["lnt"], "c2")
        cp(identB[:], identF, ["cs"], ["identB"])
        P.add("dve", lambda e: e.memset(scr[0][:], 0.0), [], [("scr", 0)])
        P.add("dve", lambda e: e.memset(scr[1][:], 1.0 / 128.0), [], [("scr", 1)])
        P.add("dve", lambda e: e.memset(scr[2][:], 1.0 / 1024.0), [], [("scr", 2)])
        cp(ones128[:], scr[1][:, 0:128], [("scr", 1)], ["ones128"])
        cp(ones1024[:], scr[2][:, 0:128], [("scr", 2)], ["ones1024"])
        Tpf = Tprev[:].rearrange("p h d -> p (h d)")
        cp(Tpf[:, 0:512], scr[0][:], [("scr", 0)], [("Tprev", h) for h in range(4)])
        cp(Tpf[:, 512:1024], scr[0][:], [("scr", 0)], [("Tprev", h) for h in range(4, 8)])
        cp(halo[:].rearrange("p j t -> p (j t)"), scr[0][:, 0:256], [("scr", 0)], [("halo", j) for j in range(8)])
        for p_ in range(2):
            for i in range(2):
                for tb in range(0, NB, 4):
                    cp(K_tm[p_][i][:, tb:tb + 4, :].rearrange("p a b -> p (a b)"), scr[0][:], [("scr", 0)],
                       [("Ktm", p_, tb // 4, 0), ("Ktm", p_, tb // 4, 1)])
        tt_(lbt[:, 3, :], lbp[:, 0, :], lbp[:, 1, :], ALU.subtract, ["pp"], ["lbt3"])
        act(lbt[:, 0, :], lbt[:, 3, :], AF.Sigmoid, ["lbt3"], ["lbt0"])
        ts_(lbt[:, 1, :], lbt[:, 0, :], -0.5, 0.5, ALU.mult, ALU.add, ["lbt0"], ["lbt1"])
        ts_(lbt[:, 2, :], lbt[:, 0, :], 0.5, -0.5, ALU.mult, ALU.add, ["lbt0"], ["lbt2"])
        ts_(lbt[:, 3, :], lbt[:, 0, :], 0.5, 0.5, ALU.mult, ALU.add, ["lbt0"], ["lbt3b"])
        ts_(cwh[:], pp[:, 32:32 + 8 * TAPS], 0.5, None, ALU.mult, ALU.bypass, ["pp"], ["cwh"])
        LB = ["lbt3b", "lbt1", "lbt2"]

        def transpose_blk(src, src_keys, tb, xb):
            for g in range(2):
                ps, pk = newps()
                for kk in range(4):
                    kc = g * 4 + kk
                    tr(ps[:, kk * 128:(kk + 1) * 128], src[:, kc * 128:(kc + 1) * 128], identF,
                       list(src_keys) + ["cs"], [pk])
                cp(xTs[xb][:, g * 4:(g + 1) * 4, tb * 128:(tb + 1) * 128],
                   ps[:].rearrange("p (a b) -> p a b", a=4), [pk], [("xT", xb, tb, g)], eng=alt_eng())
                relps(pk)

        def xT_keys(tt):
            return [("xT", st["xb"], tb, g) for tb in range(tt * 4, tt * 4 + 4) for g in range(2)]

        def xstage(tb):
            ap = Fall[:, 2 * tb:2 * tb + 2, :].rearrange("p a t -> p (a t)")
            keys = [(f"F{idx % 4}", idx // 4, 0) for idx in (2 * tb, 2 * tb + 1)]
            return ap, keys

        def prefetch_x_load(sti_):
            for tb in range(NB):
                ap, keys = xstage(tb)
                dma("sp", ap, x_d[sti_ * TS + tb * 128:sti_ * TS + (tb + 1) * 128, :], [], keys, f"xs{tb}")

        def prefetch_x_transpose(sti_):
            for tb in range(NB):
                ap, keys = xstage(tb)
                transpose_blk(ap, keys, tb, sti_ % 2)

        def layer_norm_R(tb, gi, eps_ap=None):
            eps_ap = c_eps_ln if eps_ap is None else eps_ap
            rk = ("R", tb)
            P.add("dve", lambda e: e.bn_stats(out=st6[:, 0:6], in_=R[:, tb, 0:512]), [rk], ["st6a"])
            P.add("dve", lambda e: e.bn_stats(out=st6[:, 6:12], in_=R[:, tb, 512:1024]), [rk], ["st6b"])
            P.add("dve", lambda e: e.bn_aggr(out=mv[:, 0:2], in_=st6[:]), ["st6a", "st6b"], ["mv01"])
            act(mv[:, 2:3], mv[:, 1:2], AF.Ln, ["mv01", "cs"], ["mv2"], bias=eps_ap)
            act(mv[:, 3:4], mv[:, 2:3], AF.Exp, ["mv2"], ["mv3"], scale=-0.5)
            ts_(R[:, tb, :], R[:, tb, :], mv[:, 0:1], mv[:, 3:4], ALU.subtract, ALU.mult,
                [rk, "mv01", "mv3"], [rk])
            tt_(R[:, tb, :], R[:, tb, :], lnt[:, gi, :], ALU.mult, [rk, "lnt"], [rk])
            tt_(R[:, tb, :], R[:, tb, :], lnt[:, gi + 1, :], ALU.add, [rk, "lnt"], [rk])

        for sti in range(NST):
            t0 = sti * TS
            P.epoch = sti + 1
            st["xb"] = sti % 2
            xT = xTs[sti % 2]
            if sti == 0:
                prefetch_x_load(0)
                prefetch_x_transpose(0)
            for tb in range(NB):
                dma("sp", R[:, tb, :], x_d[t0 + tb * 128:t0 + (tb + 1) * 128, :], [], [("R", tb)], f"r{tb}")

            wst = {}

            def head_front(h):
                par = h % 2
                if h % 2 == 0:
                    hp = h // 2
                    for nm, sec in (("q", 0), ("f", 1), ("i", 2), ("o", 3)):
                        wst[nm] = wblock(w_in_d, 0, 8, sec * 1024 + hp * 256)
                (wq, kq), (wf, kf), (wi, ki), (wo, ko) = wst["q"], wst["f"], wst["i"], wst["o"]
                hc = (h % 2) * 128
                F0, F1, F2, F3 = Fb[par]
                allF = lambda n: [(n, par, tt) for tt in range(NT)]
                for tt in range(NT):
                    tsl = slice(tt * 512, (tt + 1) * 512)
                    for (wblk, wk, dst, dk_, fn) in ((wq, kq, F0, ("F0", par, tt), AF.Silu),
                                                     (wf, kf, F1, ("F1", par, tt), AF.Tanh),
                                                     (wo, ko, ogs[par], ("ogs", par, tt), AF.Silu)):
                        ps, pk = newps()
                        for kc in range(8):
                            mm(ps[:], wblk[:, kc, hc:hc + 128], xT[:, kc, tsl], kc == 0, kc == 7,
                               [wk] + xT_keys(tt), [pk])
                        act(dst[:, tsl], ps[:], fn, [pk], [dk_], scale=(0.5 if fn == AF.Tanh else None))
                        relps(pk)
                        yield
                for tg in range(NB // 4):
                    ps, pk = newps()
                    for tl in range(4):
                        tb = tg * 4 + tl
                        for kc in range(8):
                            mm(ps[:, tl * 128:(tl + 1) * 128], xT[:, kc, tb * 128:(tb + 1) * 128],
                               wi[:, kc, hc:hc + 128], kc == 0, kc == 7,
                               [ki, ("xT", st["xb"], tb, 0), ("xT", st["xb"], tb, 1)], [pk])
                    cp(v_tm[par][:, tg * 4:(tg + 1) * 4, :], ps[:].rearrange("p (a b) -> p a b", a=4),
                       [pk], [("vtm", par, tg)])
                    relps(pk)
                    yield
                act(F2[:], F1[:], AF.Ln, allF("F1") + LB, allF("F2"),
                    scale=lbt[:, 1, h:h + 1], bias=lbt[:, 3, h:h + 1])
                ts_(F3[:], F1[:], lbt[:, 2, h:h + 1], lbt[:, 1, h:h + 1], ALU.mult, ALU.add,
                    allF("F1") + LB, allF("F3"))
                yield
                P.add("dve", lambda e, F1=F1, F2=F2: e.tensor_tensor_scan(
                    out=F1[:], data0=scanmask, data1=F2[:], initial=0.0, op0=ALU.mult, op1=ALU.add),
                    allF("F2") + ["cs"], allF("F1"), dur=0.1 + 2 * TS / 960.0)
                yield
                act(F2[:], F1[:], AF.Exp, allF("F1"), allF("F2"))
                yield
                cp(ebuf[par][:], F2[:].rearrange("p (c t) -> p c t", t=64)[:, :, 63], allF("F2"), [("ebuf", par)])
                tt_(Qpp[par][:], F0[:], F2[:], ALU.mult, allF("F0") + allF("F2"), [("Qpp", par)])
                yield
                act(F0[:], F1[:], AF.Exp, allF("F1"), allF("F0"), scale=-1.0)
                yield
                tt_(Kt[par][:], F3[:], F0[:], ALU.mult, allF("F3") + allF("F0"), [("Kt", par)])
                yield

            def head_back(h):
                par = h % 2
                Ktp, Qp, vt, Tbp, eb, KV = Kt[par], Qpp[par], v_tm[par], Tb[par], ebuf[par], KVs[par]
                for tg in range(NB // 4):
                    ps, pk = newps()
                    psv = ps[:].bitcast(BF16)
                    for tl in range(4):
                        tb = tg * 4 + tl
                        tr(psv[:, tl * 128:(tl + 1) * 128], Ktp[:, tb * 128:(tb + 1) * 128], identB[:],
                           [("Kt", par), "identB"], [pk])
                    for hf in range(2):
                        cp(K_tm[par][hf][hf * 64:hf * 64 + 64, tg * 4:(tg + 1) * 4, :],
                           psv[hf * 64:hf * 64 + 64, 0:512].rearrange("p (a b) -> p a b", a=4),
                           [pk], [("Ktm", par, tg, hf)], eng=("act" if hf == 0 else "dve"))
                    relps(pk)
                    yield
                for cg_ in range(NCH // 4):
                    ps, pk = newps()
                    for cl in range(4):
                        c = cg_ * 4 + cl
                        tb, hf = c // 2, c % 2
                        mm(ps[:, cl * 128:(cl + 1) * 128], K_tm[par][hf][:, tb, :], vt[:, tb, :], True, True,
                           [("Ktm", par, tb // 4, hf), ("vtm", par, tb // 4)], [pk])
                    tt_(KV[:, cg_ * 4:(cg_ + 1) * 4, :], ps[:].rearrange("p (a b) -> p a b", a=4),
                        eb[:, cg_ * 4:(cg_ + 1) * 4].unsqueeze(2).broadcast_to([128, 4, 128]), ALU.mult,
                        [pk, ("ebuf", par)], [("KVs", par, cg_)])
                    relps(pk)
                    yield
                cp(Tbp[:, 0, :], Tprev[:, h, :], [("Tprev", h)], [("Tb", par, 0)])
                for c in range(NCH):
                    stt(Tbp[:, c + 1, :], Tbp[:, c, :], eb[:, c:c + 1], KV[:, c, :], ALU.mult, ALU.add,
                        [("Tb", par, c), ("ebuf", par), ("KVs", par, c // 4)], [("Tb", par, c + 1)])
                    if c % 2 == 1:
                        yield
                cp(Tprev[:, h, :], Tbp[:, NCH, :], [("Tb", par, NCH)], [("Tprev", h)], eng="act")
                yield
                yield
                for tt in range(NT):
                    tsl = slice(tt * 512, (tt + 1) * 512)
                    pA, pAk = newps()
                    for tbl in range(4):
                        tb = tt * 4 + tbl
                        mm(pA[:, tbl * 128:(tbl + 1) * 128], Ktp[:, tb * 128:(tb + 1) * 128],
                           Qp[:, tb * 128:(tb + 1) * 128], True, True, [("Kt", par), ("Qpp", par)], [pAk])
                    st["am"] ^= 1
                    am = Am[st["am"]]
                    amk = ("Am", st["am"])
                    tt_(am[:], pA[:].rearrange("p (a b) -> p a b", a=4),
                        mask2.unsqueeze(1).broadcast_to([128, 4, 128]), ALU.mult, [pAk, "cs"], [amk])
                    relps(pAk)
                    yield
                    pO, pOk = newps()
                    for tbl in range(4):
                        tb = tt * 4 + tbl
                        mm(pO[:, tbl * 128:(tbl + 1) * 128], vt[:, tb, :], am[:, tbl, :], True, False,
                           [("vtm", par, tb // 4), amk], [pOk])
                        for hf in range(2):
                            c = 2 * tb + hf
                            mm(pO[:, tbl * 128 + hf * 64:tbl * 128 + hf * 64 + 64], Tbp[:, c, :],
                               Qp[:, c * 64:(c + 1) * 64], False, hf == 1, [("Tb", par, c), ("Qpp", par)], [pOk])
                    osq, osk = newscb()
                    act(osq[:], pO[:], AF.Square, [pOk], [osk])
                    yield
                    pM, pMk = newps()
                    mm(pM[:], ones128[:], osq[:], True, True, ["ones128", osk], [pMk])
                    lnv, lnk = newscr()
                    act(lnv[:], pM[:], AF.Ln, [pMk, "cs"], [lnk], bias=c_eps_rms)
                    relps(pMk)
                    rstd, rsk = newscr()
                    act(rstd[:], lnv[:], AF.Exp, [lnk], [rsk], scale=-0.5)
                    t1, t1k = newscr()
                    stt(t1[:], pO[:], gn, rstd[:], ALU.mult, ALU.mult, [pOk, "pp", rsk], [t1k])
                    relps(pOk)
                    tt_(on[:, h, tsl], t1[:], ogs[par][:, tsl], ALU.mult, [t1k, ("ogs", par, tt)], [("on", h, tt)])
                    yield

            def conv_front(j):
                par = j % 2
                if j % 2 == 0:
                    wst["gv"] = wblock(w_in_d, 0, 8, 4096 + (j // 2) * 256)
                    wst["gg"] = wblock(w_in_d, 0, 8, 5120 + (j // 2) * 256)
                (wgv, kgv), (wgg, kgg) = wst["gv"], wst["gg"]
                jc = (j % 2) * 128
                ub = ubuf[par]
                cp(ub[:, 0:32], halo[:, j, :], [("halo", j)], [("ubuf_h", par)], eng="act")
                for tt in range(NT):
                    psv_, pvk = newps()
                    psg, pgk = newps()
                    tsl = slice(tt * 512, (tt + 1) * 512)
                    for kc in range(8):
                        mm(psv_[:], wgv[:, kc, jc:jc + 128], xT[:, kc, tsl], kc == 0, kc == 7,
                           [kgv] + xT_keys(tt), [pvk])
                        if kc % 2 == 1:
                            yield
                    for kc in range(8):
                        mm(psg[:], wgg[:, kc, jc:jc + 128], xT[:, kc, tsl], kc == 0, kc == 7,
                           [kgg] + xT_keys(tt), [pgk])
                        if kc % 2 == 1:
                            yield
                    sg_, sgk = newscr()
                    act(sg_[:], psg[:], AF.Tanh, [pgk], [sgk], scale=0.5)
                    stt(ub[:, 32 + tt * 512:32 + (tt + 1) * 512], sg_[:], 1.0, psv_[:], ALU.add, ALU.mult,
                        [pvk, sgk], [("ubuf", par, tt)])
                    relps(pvk, pgk)
                cp(halo[:, j, :], ub[:, TS:TS + 32], [("ubuf", par, NT - 1)], [("halo", j)], eng="act")
                for t0_, t1_ in ((0, 8), (8, 16), (16, 24), (24, TAPS)):
                    nt_ = t1_ - t0_
                    tt_(Dg[par][:, t0_:t1_, :], identB[:].unsqueeze(1).broadcast_to([128, nt_, 128]),
                        cwh[:, j * TAPS + t0_:j * TAPS + t1_].unsqueeze(2).broadcast_to([128, nt_, 128]), ALU.mult,
                        ["identB", "cwh"], [("Dg", par, t0_ // 8)])
                    yield

            def conv_back(j):
                par = j % 2
                ub = ubuf[par]
                for tt in range(NT):
                    pc, pck = psb[6 + par], ("ps", 6 + par)
                    ur = [("ubuf_h", par)] + [("ubuf", par, t_) for t_ in range(tt + 1)]
                    for tap in range(TAPS):
                        off = 2 + tt * 512 + tap
                        mm(pc[:], Dg[par][:, tap, :], ub[:, off:off + 512], tap == 0, tap == TAPS - 1,
                           [("Dg", par, tap // 8)] + ur, [pck])
                        if tap % 3 == 2:
                            yield
                    act(cpre[:, j, tt * 512:(tt + 1) * 512], pc[:], AF.Identity, [pck, "pp"], k_cpre(j, tt),
                        bias=cb[:, j:j + 1])
                    yield

            def merged(gens):
                gens = list(gens)
                while gens:
                    for g in list(gens):
                        try:
                            next(g)
                        except StopIteration:
                            gens.remove(g)
                            continue
                        yield

            def thread(front, back, n):
                yield from front(0)
                for i in range(n):
                    gs = [back(i)]
                    if i + 1 < n:
                        gs.append(front(i + 1))
                    yield from merged(gs)

            threads = []
            if dbg >= 3:
                threads.append(thread(head_front, head_back, NH))
            if dbg >= 4:
                threads.append(thread(conv_front, conv_back, 8))
            while threads:
                for th in list(threads):
                    try:
                        next(th)
                    except StopIteration:
                        threads.remove(th)

            for tt in range(NT if dbg >= 4 else 0):
                tsl = slice(tt * 512, (tt + 1) * 512)
                pS1, pS1k = newps()
                pS2, pS2k = newps()
                for j in range(8):
                    cbf, cbk = newscb()
                    csq, csk = newscb()
                    cp(cbf[:], cpre[:, j, tsl], k_cpre(j, tt), [cbk])
                    act(csq[:], cpre[:, j, tsl], AF.Square, k_cpre(j, tt), [csk])
                    mm(pS1[:], ones1024[:], cbf[:], j == 0, j == 7, ["ones1024", cbk], [pS1k])
                    mm(pS2[:], ones1024[:], csq[:], j == 0, j == 7, ["ones1024", csk], [pS2k])
                mean, mk_ = cmean, "cmean"
                cp(mean[:], pS1[:], [pS1k], [mk_], eng="act")
                relps(pS1k)
                msq, msk = newscr()
                tt_(msq[:], mean[:], mean[:], ALU.mult, [mk_], [msk])
                var, vk = newscr()
                tt_(var[:], pS2[:], msq[:], ALU.subtract, [pS2k, msk], [vk])
                relps(pS2k)
                lnv, lnk = newscr()
                act(lnv[:], var[:], AF.Ln, [vk, "cs"], [lnk], bias=c_eps_ln)
                rstd, rsk = crstd, "crstd"
                act(rstd[:], lnv[:], AF.Exp, [lnk], [rsk], scale=-0.5)
                for j in range(8):
                    ta, tak = newscr()
                    tt_(ta[:], cpre[:, j, tsl], mean[:], ALU.subtract, k_cpre(j, tt) + [mk_], [tak])
                    t2, t2k = newscr()
                    tt_(t2[:], ta[:], rstd[:], ALU.mult, [tak, rsk], [t2k])
                    act(un[:, j, tsl], t2[:], AF.Silu, [t2k, "pp"], k_un(j, tt),
                        scale=cg[:, j:j + 1], bias=cbb[:, j:j + 1])

            for j in range(8 if dbg >= 5 else 0):
                if j % 2 == 0:
                    wha, kha = wblock(w_hg_d, 0, 8, (j // 2) * 256)
                    wcb, kcb = wblock(w_cv_d, 0, 8, (j // 2) * 256)
                    wga, kga = wblock(w_in_d, 0, 8, 6144 + (j // 2) * 256)
                    wgb, kgb = wblock(w_in_d, 0, 8, 7168 + (j // 2) * 256)
                jc = (j % 2) * 128
                for tt in range(NT):
                    tsl = slice(tt * 512, (tt + 1) * 512)
                    pya, pyak = newps()
                    pyb, pybk = newps()
                    pga, pgak = newps()
                    pgb, pgbk = newps()
                    for kc in range(8):
                        mm(pga[:], wga[:, kc, jc:jc + 128], xT[:, kc, tsl], kc == 0, kc == 7,
                           [kga] + xT_keys(tt), [pgak])
                    for kc in range(8):
                        mm(pgb[:], wgb[:, kc, jc:jc + 128], xT[:, kc, tsl], kc == 0, kc == 7,
                           [kgb] + xT_keys(tt), [pgbk])
                    for kc in range(8):
                        mm(pya[:], wha[:, kc, jc:jc + 128], on[:, kc, tsl], kc == 0, kc == 7,
                           [kha, ("on", kc, tt)], [pyak])
                    for kc in range(8):
                        mm(pyb[:], wcb[:, kc, jc:jc + 128], un[:, kc, tsl], kc == 0, kc == 7,
                           [kcb] + k_un(kc, tt), [pybk])
                    sa, sak = newscr()
                    act(sa[:], pga[:], AF.Tanh, [pgak], [sak], scale=0.5)
                    sb_, sbk = newscr()
                    act(sb_[:], pgb[:], AF.Tanh, [pgbk], [sbk], scale=0.5)
                    m1, m1k = newscr()
                    stt(m1[:], sa[:], 1.0, pya[:], ALU.add, ALU.mult, [pyak, sak], [m1k])
                    m2, m2k = newscr()
                    stt(m2[:], sb_[:], 1.0, pyb[:], ALU.add, ALU.mult, [pybk, sbk], [m2k])
                    relps(pyak, pybk, pgak, pgbk)
                    tt_(mixed[:, j, tsl], m1[:], m2[:], ALU.add, [m1k, m2k], k_mixed(j, tt))

            wob = [wblock(w_out_d, 0, 8, nb * 256) for nb in range(4)] if dbg >= 6 else []
            for tb in range(NB if dbg >= 6 else 0):
                tt = tb // 4
                for half in range(2):
                    ps, pk = newps()
                    for q in range(2):
                        wblk, wk = wob[half * 2 + q]
                        for kc in range(8):
                            mm(ps[:, q * 256:(q + 1) * 256], mixed[:, kc, tb * 128:(tb + 1) * 128], wblk[:, kc, :],
                               kc == 0, kc == 7, [wk] + k_mixed(kc, tt), [pk])
                    stt(R[:, tb, half * 512:(half + 1) * 512], R[:, tb, half * 512:(half + 1) * 512], 2.0 * ALPHA, ps[:],
                        ALU.mult, ALU.add, [("R", tb), pk], [("R", tb)])
                    relps(pk)
            for tb in range(NB if dbg >= 6 else 0):
                layer_norm_R(tb, 0, c_eps_ln4)
            for tb in range(NB if dbg >= 6 else 0):
                transpose_blk(R[:, tb, :], [("R", tb)], tb, sti % 2)

            if sti + 1 < NST:
                prefetch_x_load(sti + 1)
            for jj in range(22 if dbg >= 8 else 0):
                if jj == 12 and sti + 1 < NST:
                    prefetch_x_transpose(sti + 1)
                if jj % 2 == 0:
                    wg_, kg_ = wblock(w_f1_d, 0, 8, (jj // 2) * 256)
                    wu_, ku_ = wblock(w_f1_d, 0, 8, FFN + (jj // 2) * 256)
                jc = (jj % 2) * 128
                for tt in range(NT):
                    tsl = slice(tt * 512, (tt + 1) * 512)
                    pg_, pgk_ = newps()
                    pu_, puk_ = newps()
                    for kc in range(8):
                        mm(pg_[:], wg_[:, kc, jc:jc + 128], xT[:, kc, tsl], kc == 0, kc == 7,
                           [kg_] + xT_keys(tt), [pgk_])
                    for kc in range(8):
                        mm(pu_[:], wu_[:, kc, jc:jc + 128], xT[:, kc, tsl], kc == 0, kc == 7,
                           [ku_] + xT_keys(tt), [puk_])
                    sg_, sgk = newscr()
                    act(sg_[:], pg_[:], AF.Silu, [pgk_], [sgk])
                    tt_(actb[:, jj, tsl], sg_[:], pu_[:], ALU.mult, [sgk, puk_], k_act(jj, tt))
                    relps(pgk_, puk_)

            for nb in range(4 if dbg >= 9 else 0):
                wfb = [wblock(w_f2_d, g * 1024, (8 if g < 2 else 6), nb * 256) for g in range(3)]
                for tb in range(NB):
                    tt = tb // 4
                    ps, pk = newps()
                    for kc in range(22):
                        wblk, wk = wfb[kc // 8]
                        mm(ps[:, 0:256], actb[:, kc, tb * 128:(tb + 1) * 128], wblk[:, kc % 8, :],
                           kc == 0, kc == 21, [wk] + k_act(kc, tt), [pk])
                    stt(R[:, tb, nb * 256:(nb + 1) * 256], R[:, tb, nb * 256:(nb + 1) * 256], ALPHA, ps[:, 0:256],
                        ALU.mult, ALU.add, [("R", tb), pk], [("R", tb)])
                    relps(pk)
                    if nb == 3:
                        layer_norm_R(tb, 2)
                        dma("sp", out_d[t0 + tb * 128:t0 + (tb + 1) * 128, :], R[:, tb, :], [("R", tb)], [], f"o{tb}")
            if dbg < 9:
                for tb in range(NB):
                    dma("sp", out_d[t0 + tb * 128:t0 + (tb + 1) * 128, :], R[:, tb, :], [("R", tb)], [], f"o{tb}")

        P.add("sp", None)
        fin = P.ops[-1]
        if LIST_SCHED:
            P.list_schedule(SCHED_MODE)
            print("list schedule: estimated makespan %.0f us" % P.est_makespan)
        P.finalize()
        fin["waits"] = [(("d", f"o{tb}"), P.dsem_count[f"o{tb}"]) for tb in range(NB)]
        P.emit(block, esems, dsems)
    return nc


PP_N = 320


def CST_N(TS):
    return 320 + TS


def make_consts(TS):
    c = np.zeros((128, CST_N(TS)), np.float32)
    c[:, 0:128] = np.eye(128, dtype=np.float32)
    p = np.arange(128)[:, None]
    t = np.arange(128)[None, :]
    c[:, 128:256] = ((p // 64 == t // 64) & (p % 64 <= t % 64)).astype(np.float32)
    c[:, 256] = RMS_EPS
    c[:, 257] = LN_EPS
    c[:, 258] = 4.0 * LN_EPS
    m = np.ones(TS, np.float32)
    m[::64] = 0.0
    c[:, 320:320 + TS] = m
    return c


def pack_params(lb_param, hg_norm_g, conv_w, conv_b, conv_ln_g, conv_ln_b):
    pp = np.zeros((128, PP_N), np.float32)
    pp[:, 0:16] = lb_param.reshape(2, NH, 128).transpose(2, 0, 1).reshape(128, 16)
    pp[:, 16] = hg_norm_g.reshape(128)
    pp[:, 32:32 + 8 * TAPS] = conv_w.reshape(TAPS, 8, 128).transpose(2, 1, 0).reshape(128, 8 * TAPS)
    pp[:, 288:296] = conv_b.reshape(8, 128).T
    pp[:, 296:304] = conv_ln_g.reshape(8, 128).T
    pp[:, 304:312] = conv_ln_b.reshape(8, 128).T
    return pp


def make_in_maps(x, w_in, lb_param, hg_norm_g, w_hg_out, conv_w, conv_b, conv_ln_g, conv_ln_b,
                 w_conv_out, w_out, ln1_g, ln1_b, w_ffn_in, w_ffn_out, ln2_g, ln2_b, TS=512):
    f = lambda a: np.ascontiguousarray(np.asarray(a, dtype=np.float32))
    B = x.shape[0]
    pp = pack_params(f(lb_param), f(hg_norm_g)[0], f(conv_w)[0], f(conv_b)[0], f(conv_ln_g)[0], f(conv_ln_b)[0])
    lnp = np.ascontiguousarray(np.broadcast_to(
        np.stack([f(ln1_g)[0], f(ln1_b)[0], f(ln2_g)[0], f(ln2_b)[0]])[None], (128, 4, D)))
    shared = {
        "w_in": f(w_in)[0], "w_hg_out": f(w_hg_out)[0], "w_conv_out": f(w_conv_out)[0], "w_out": f(w_out)[0],
        "w_ffn_in": f(w_ffn_in)[0], "w_ffn_out": f(w_ffn_out)[0], "pp": pp, "lnp": lnp, "cst": make_consts(TS),
    }
    xs = f(x)
    return [dict(shared, x=xs[b]) for b in range(B)]


def kernel(**inputs):
    TS = 512
    in_maps = make_in_maps(TS=TS, **inputs)
    nc = build_nc(SEQ, TS=TS)
    res = run_bass_kernel_spmd(nc, in_maps, core_ids=list(range(N_CORES)))
    return np.stack([np.asarray(r["out"], dtype=np.float32) for r in res.results], axis=0)
```
